# Optimizing a Trainium2 kernel written in Bass

```python
import jax, jax.numpy as jnp
from jax import lax
import numpy as np

D_MODEL = 1024
BATCH = 8
SEQ = 8192
DEPTH = 2

GRID_W = 64
CTX_LEN = 256
N_MIXERS = 2
EXPAND = 2
D_INNER = EXPAND * D_MODEL
CONV_WIDTH = 31
HEAD_DIM = 128
N_Q_HEADS = D_INNER // HEAD_DIM
N_KV_HEADS = N_Q_HEADS // 4
GROUP = N_Q_HEADS // N_KV_HEADS
KV_DIM = N_KV_HEADS * HEAD_DIM
ROPE_AXIS_DIM = HEAD_DIM // 2
ROPE_THETA = 10000.0
Q_BLOCK = 128
RMS_EPS = 1e-6
LN_EPS = 1e-5

kernel_name = "hybrid_conformer_gqa_prefix_dit"


def _rmsnorm(x, g):
    xf = x.astype(jnp.float32)
    y = xf * lax.rsqrt(jnp.mean(xf * xf, axis=-1, keepdims=True) + RMS_EPS)
    return (y * g.astype(jnp.float32)).astype(x.dtype)


def _layernorm(x, g, b):
    xf = x.astype(jnp.float32)
    mu = jnp.mean(xf, axis=-1, keepdims=True)
    xc = xf - mu
    var = jnp.mean(xc * xc, axis=-1, keepdims=True)
    y = xc * lax.rsqrt(var + LN_EPS) * g.astype(jnp.float32) + b.astype(jnp.float32)
    return y.astype(x.dtype)


def _adaln(cond, w, b):
    mod = (jax.nn.silu(cond) @ w + b)[..., None, :]
    return jnp.split(mod, 3, axis=-1)


def _modulate(x, g, shift, scale):
    return _rmsnorm(x, g) * (1.0 + scale) + shift


def _conformer_conv_mixer(h, w_in, b_in, dw_w, dw_b, ln_g, ln_b, w_out, b_out):
    u = h @ w_in + b_in
    a, g_lin, z = jnp.split(u, 3, axis=-1)
    v = a * jax.nn.sigmoid(g_lin)
    v = lax.conv_general_dilated(
        v, dw_w[:, None, :].astype(v.dtype), window_strides=(1,),
        padding=[(CONV_WIDTH // 2, CONV_WIDTH // 2)],
        dimension_numbers=('NWC', 'WIO', 'NWC'),
        feature_group_count=D_INNER) + dw_b
    v = jax.nn.silu(_layernorm(v, ln_g, ln_b))
    return (v * jax.nn.silu(z)) @ w_out + b_out


def _axial_rope_tables(length):
    rows = length // GRID_W
    row = jnp.broadcast_to(jnp.arange(rows)[:, None], (rows, GRID_W)).reshape(-1)
    col = jnp.broadcast_to(jnp.arange(GRID_W)[None, :], (rows, GRID_W)).reshape(-1)
    pos = jnp.stack([row, col], axis=-1).astype(jnp.float32)
    inv_freq = ROPE_THETA ** (-jnp.arange(0, ROPE_AXIS_DIM, 2, dtype=jnp.float32) / ROPE_AXIS_DIM)
    ang = pos[:, :, None] * inv_freq
    return jnp.cos(ang), jnp.sin(ang)


def _apply_axial_rope(x, cos, sin):
    B, L, H, _ = x.shape
    xr = x.astype(jnp.float32).reshape(B, L, H, 2, 2, ROPE_AXIS_DIM // 2)
    x1, x2 = xr[..., 0, :], xr[..., 1, :]
    c, s = cos[:, None], sin[:, None]
    out = jnp.stack([x1 * c - x2 * s, x1 * s + x2 * c], axis=-2)
    return out.reshape(B, L, H, HEAD_DIM).astype(x.dtype)


def _split_heads(t, n_heads):
    return t.reshape(t.shape[0], t.shape[1], n_heads, HEAD_DIM)


def _gqa_attend(q, k, v):
    B, Lq = q.shape[0], q.shape[1]
    nb = Lq // Q_BLOCK
    qb = (q * (HEAD_DIM ** -0.5)).reshape(B, nb, Q_BLOCK, N_KV_HEADS, GROUP, HEAD_DIM)
    qb = qb.transpose(1, 0, 2, 3, 4, 5)

    def one_block(q_blk):
        s = jnp.einsum('bqhgd,bkhd->bhgqk', q_blk, k).astype(jnp.float32)
        p = jax.nn.softmax(s, axis=-1).astype(v.dtype)
        return jnp.einsum('bhgqk,bkhd->bqhgd', p, v)

    o = lax.map(one_block, qb)
    return o.transpose(1, 0, 2, 3, 4, 5).reshape(B, Lq, N_Q_HEADS * HEAD_DIM)


def _attn_mixer(h_lat, h_ctx, w_in, q_norm_g, k_norm_g, w_out, ctx_out):
    q_end, k_end, v_end = D_INNER, D_INNER + KV_DIM, D_INNER + 2 * KV_DIM
    kv_ctx = h_ctx @ w_in[:, q_end:v_end]
    k_ctx = _rmsnorm(_split_heads(kv_ctx[..., :KV_DIM], N_KV_HEADS), k_norm_g)
    v_ctx = _split_heads(kv_ctx[..., KV_DIM:], N_KV_HEADS)
    u = h_lat @ w_in
    q, k, v, z = jnp.split(u, [q_end, k_end, v_end], axis=-1)
    cos, sin = _axial_rope_tables(h_lat.shape[1])
    q = _apply_axial_rope(_rmsnorm(_split_heads(q, N_Q_HEADS), q_norm_g), cos, sin)
    k = _apply_axial_rope(_rmsnorm(_split_heads(k, N_KV_HEADS), k_norm_g), cos, sin)
    v = _split_heads(v, N_KV_HEADS)
    k_all = jnp.concatenate([k_ctx, k], axis=1)
    v_all = jnp.concatenate([v_ctx, v], axis=1)
    y_lat = (_gqa_attend(q, k_all, v_all) * jax.nn.silu(z)) @ w_out
    if not ctx_out:
        return y_lat, None
    q_ctx = _rmsnorm(_split_heads(h_ctx @ w_in[:, :q_end], N_Q_HEADS), q_norm_g)
    z_ctx = h_ctx @ w_in[:, v_end:]
    y_ctx = (_gqa_attend(q_ctx, k_ctx, v_ctx) * jax.nn.silu(z_ctx)) @ w_out
    return y_lat, y_ctx


def setup_inputs(seed: int = 0) -> dict:
    key = jax.random.key(seed)
    ks = jax.random.split(key, 32)
    f32 = jnp.float32
    D, E = D_MODEL, D_INNER

    def nrm(k, shape, scale):
        return jax.random.normal(k, shape, f32) * scale

    return {
        "x": nrm(ks[0], (BATCH, SEQ, D), 1.0),
        "c": nrm(ks[1], (BATCH, D), 1.0),
        "ctx": nrm(ks[2], (BATCH, CTX_LEN, D), 1.0),
        "c_ctx": nrm(ks[3], (D,), 1.0),
        "l0_norm_g": 1.0 + nrm(ks[4], (D,), 0.02),
        "l0_ada_w": nrm(ks[5], (D, 3 * D), D ** -0.5),
        "l0_ada_b": nrm(ks[6], (3 * D,), 0.01),
        "l0_w_in": nrm(ks[7], (D, 3 * E), D ** -0.5),
        "l0_b_in": nrm(ks[8], (3 * E,), 0.01),
        "l0_dw_w": nrm(ks[9], (CONV_WIDTH, E), CONV_WIDTH ** -0.5),
        "l0_dw_b": nrm(ks[10], (E,), 0.01),
        "l0_ln_g": 1.0 + nrm(ks[11], (E,), 0.02),
        "l0_ln_b": nrm(ks[12], (E,), 0.01),
        "l0_w_out": nrm(ks[13], (E, D), E ** -0.5),
        "l0_b_out": nrm(ks[14], (D,), 0.01),
        "l1_norm_g": 1.0 + nrm(ks[15], (D,), 0.02),
        "l1_ada_w": nrm(ks[16], (D, 3 * D), D ** -0.5),
        "l1_ada_b": nrm(ks[17], (3 * D,), 0.01),
        "l1_w_in": nrm(ks[18], (D, 2 * E + 2 * KV_DIM), D ** -0.5),
        "l1_q_norm_g": 1.0 + nrm(ks[19], (HEAD_DIM,), 0.02),
        "l1_k_norm_g": 1.0 + nrm(ks[20], (HEAD_DIM,), 0.02),
        "l1_w_out": nrm(ks[21], (E, D), E ** -0.5),
        "final_norm_g": 1.0 + nrm(ks[22], (D,), 0.02),
    }


def reference(x, c, ctx, c_ctx,
              l0_norm_g, l0_ada_w, l0_ada_b, l0_w_in, l0_b_in, l0_dw_w, l0_dw_b,
              l0_ln_g, l0_ln_b, l0_w_out, l0_b_out,
              l1_norm_g, l1_ada_w, l1_ada_b, l1_w_in, l1_q_norm_g, l1_k_norm_g, l1_w_out,
              final_norm_g):
    layers = [
        (l0_norm_g, l0_ada_w, l0_ada_b,
         (l0_w_in, l0_b_in, l0_dw_w, l0_dw_b, l0_ln_g, l0_ln_b, l0_w_out, l0_b_out)),
        (l1_norm_g, l1_ada_w, l1_ada_b,
         (l1_w_in, l1_q_norm_g, l1_k_norm_g, l1_w_out)),
    ]
    for i in range(DEPTH):
        norm_g, ada_w, ada_b, mix = layers[i]
        last = i == DEPTH - 1
        shift, scale, gate = _adaln(c, ada_w, ada_b)
        shift_c, scale_c, gate_c = _adaln(c_ctx, ada_w, ada_b)
        h = _modulate(x, norm_g, shift, scale)
        h_c = _modulate(ctx, norm_g, shift_c, scale_c)
        if i % N_MIXERS == 0:
            y = _conformer_conv_mixer(h, *mix)
            y_c = None if last else _conformer_conv_mixer(h_c, *mix)
        else:
            y, y_c = _attn_mixer(h, h_c, *mix, ctx_out=not last)
        x = x + gate * y
        if not last:
            ctx = ctx + gate_c * y_c
    return _rmsnorm(x, final_norm_g)
```

```python
import contextlib
import numpy as np
import concourse.bass as bass
import concourse.mybir as mybir
from concourse.bass_utils import run_bass_kernel_spmd

F32 = mybir.dt.float32
BF16 = mybir.dt.bfloat16
ALU = mybir.AluOpType
AF = mybir.ActivationFunctionType
AX = mybir.AxisListType

D = 1024
E = 2048
NK = 8
NCH = 16
CW = 31
HALO = 15
CTX = 256
NQH = 16
NKV = 4
KVD = 512
W1C = 2 * E + 2 * KVD
RMS_EPS = 1e-6
LN_EPS = 1e-5
SEQ = 8192
NCORES = 8

ENGS = ["tensor", "vector", "scalar", "gpsimd", "sync"]
N_DMA_SEMS = {"sync": 14, "vector": 0, "scalar": 4, "gpsimd": 8, "tensor": 0}


class Buf:
    __slots__ = ("name", "last_write", "reads")

    def __init__(self, name=""):
        self.name = name
        self.last_write = None
        self.reads = []


class EngState:
    def __init__(self, name):
        self.name = name
        self.n = 0
        self.seen = {}
        self.queue = []
        self.dma_sems = []
        self.dma_rr = 0


class FW:
    def __init__(self, nc, stack):
        self.nc = nc
        self.sems = {}
        self.E = {}
        for e in ENGS:
            self.sems[f"tl_{e}"] = stack.enter_context(nc.semaphore(f"tl_{e}"))
            self.E[e] = EngState(e)
            for i in range(N_DMA_SEMS[e]):
                k = f"dq_{e}_{i}"
                self.sems[k] = stack.enter_context(nc.semaphore(k))
                self.E[e].dma_sems.append([k, 0])

    def _collect(self, eng, reads, writes):
        deps = {}

        def add(ev):
            if ev is None:
                return
            k, v = ev
            if deps.get(k, 0) < v:
                deps[k] = v
        for b in reads:
            add(b.last_write)
        for b in writes:
            add(b.last_write)
            for r in b.reads:
                add(r)
        st = self.E[eng]
        waits = []
        for k, v in deps.items():
            if eng == "tensor" and k == "tl_tensor":
                continue
            if st.seen.get(k, 0) >= v:
                continue
            st.seen[k] = v
            waits.append((k, v))
        return waits

    def _post(self, ev, reads, writes):
        for b in reads:
            b.reads.append(ev)
            if len(b.reads) > 64:
                b.reads = b.reads[-48:]
        for b in writes:
            b.last_write = ev
            b.reads = []

    def op(self, eng, name, kw, reads=(), writes=(), signal=True, args=()):
        fn = (lambda e, name=name, args=args, kw=kw: getattr(e, name)(*args, **kw))
        st = self.E[eng]
        waits = self._collect(eng, reads, writes)
        if signal:
            st.n += 1
            ev = (f"tl_{eng}", st.n)
        else:
            assert eng == "tensor"
            ev = (f"tl_{eng}", st.n + 1)
        st.queue.append((waits, fn, (f"tl_{eng}", 1) if signal else None))
        self._post(ev, reads, writes)
        return ev

    def dma(self, eng, out, in_, reads=(), writes=(), **kw):
        st = self.E[eng]
        slot = st.dma_sems[st.dma_rr % len(st.dma_sems)]
        st.dma_rr += 1
        k = slot[0]
        waits = self._collect(eng, reads, writes)
        if slot[1] > 0 and st.seen.get(k, 0) < slot[1]:
            st.seen[k] = slot[1]
            waits.append((k, slot[1]))
        slot[1] += 16
        ev = (k, slot[1])
        st.queue.append((waits, (lambda e, o=out, i=in_, kw=kw: e.dma_start(out=o, in_=i, **kw)), (k, 16)))
        self._post(ev, reads, writes)
        return ev

    def barrier(self):
        targets = {}
        for e in ENGS:
            st = self.E[e]
            if st.n > 0:
                targets[f"tl_{e}"] = st.n
            for k, c in st.dma_sems:
                if c > 0:
                    targets[k] = c
        for e in ENGS:
            st = self.E[e]
            waits = []
            for k, v in targets.items():
                if st.seen.get(k, 0) >= v:
                    continue
                st.seen[k] = v
                waits.append((k, v))
            if waits:
                st.queue.append((waits, None, None))

    def flush(self):
        nc = self.nc
        sems = self.sems
        with nc.Block() as block:
            def mk(e):
                st = self.E[e]

                def body(engine):
                    for waits, fn, inc in st.queue:
                        for k, v in waits:
                            engine.wait_ge(sems[k], v)
                        if fn is not None:
                            ins = fn(engine)
                            if inc is not None:
                                ins.then_inc(sems[inc[0]], inc[1])
                    st.queue = []
                return body
            block.tensor(mk("tensor"))
            block.vector(mk("vector"))
            block.scalar(mk("scalar"))
            block.gpsimd(mk("gpsimd"))
            block.sync(mk("sync"))


class Pool:
    def __init__(self, nc, stack, name, shape, dtype, n, psum=False):
        self.items = []
        for i in range(n):
            mk = nc.psum_tensor if psum else nc.sbuf_tensor
            t = stack.enter_context(mk(f"pl_{name}{i}", shape, dtype))
            self.items.append((t, Buf(f"{name}{i}")))
        self.i = 0

    def get(self):
        it = self.items[self.i % len(self.items)]
        self.i += 1
        return it


def build_program(L=SEQ, debug=False):
    assert L % 512 == 0
    NKEY = CTX + L
    NKC = NKEY // 128
    nc = bass.Bass("TRN2", target_bir_lowering=False)

    def din(name, shape):
        return nc.dram_tensor(name, list(shape), F32, kind="ExternalInput").ap()

    x_d = din("x", [L, D]); c_d = din("c", [D]); ctx_d = din("ctx", [CTX, D]); cctx_d = din("c_ctx", [D])
    l0_norm_g = din("l0_norm_g", [D]); l0_ada_w = din("l0_ada_w", [D, 3 * D]); l0_ada_b = din("l0_ada_b", [3 * D])
    l0_w_in = din("l0_w_in", [D, 3 * E]); l0_b_in = din("l0_b_in", [3 * E]); l0_dw_w = din("l0_dw_w", [CW, E])
    l0_dw_b = din("l0_dw_b", [E]); l0_ln_g = din("l0_ln_g", [E]); l0_ln_b = din("l0_ln_b", [E])
    l0_w_out = din("l0_w_out", [E, D]); l0_b_out = din("l0_b_out", [D])
    l1_norm_g = din("l1_norm_g", [D]); l1_ada_w = din("l1_ada_w", [D, 3 * D]); l1_ada_b = din("l1_ada_b", [3 * D])
    l1_w_in = din("l1_w_in", [D, W1C]); l1_q_norm_g = din("l1_q_norm_g", [128]); l1_k_norm_g = din("l1_k_norm_g", [128])
    l1_w_out = din("l1_w_out", [E, D]); final_norm_g = din("final_norm_g", [D])
    ident_d = din("ident", [128, 128]); cos_d = din("rope_cos", [L, 64]); sin_d = din("rope_sin", [L, 64])
    out_d = nc.dram_tensor("out", [L, D], F32, kind="ExternalOutput").ap()

    def dscr(name, shape, dt):
        return nc.dram_tensor(name, list(shape), dt).ap()

    w0in_bf = dscr("w0in_bf", [NCH, 128, NK, 3, 128], BF16)
    w0out_bf = dscr("w0out_bf", [128, NCH, D], BF16)
    diag_bf = dscr("diag_bf", [NCH, 128, CW * 128], BF16)
    w1in_bf = dscr("w1in_bf", [128, NK, W1C], BF16)
    w1out_bf = dscr("w1out_bf", [128, NCH, D], BF16)
    modrow = dscr("modrow", [2, 2, 3 * D], F32)
    if debug:
        x1_d = nc.dram_tensor("x1", [L, D], F32, kind="ExternalOutput").ap()
        ctx1_d = nc.dram_tensor("ctx1", [CTX, D], F32, kind="ExternalOutput").ap()
    else:
        x1_d = dscr("x1", [L, D], F32)
        ctx1_d = dscr("ctx1", [CTX, D], F32)
    qT_d = dscr("qT", [NQH, 128, L], BF16)
    kT_d = dscr("kT", [NKV, 128, NKEY], BF16)
    V_d = dscr("V", [NKEY, KVD], BF16)
    szT_d = dscr("szT", [NQH, 128, L], BF16)
    wT_d = dscr("wT", [NQH, 128, L], BF16)

    with contextlib.ExitStack() as top:
        fw = FW(nc, top)

        def T(stack, name, shape, dt):
            return stack.enter_context(nc.sbuf_tensor("sb_" + name, list(shape), dt))

        ident = T(top, "ident", [128, 128], F32); identb = T(top, "identb", [128, 128], BF16)
        onesb = T(top, "onesb", [128, 128], BF16)
        g0c = T(top, "g0c", [128, NK], F32); g1c = T(top, "g1c", [128, NK], F32)
        binc = T(top, "binc", [128, 3 * NCH], F32); dwT = T(top, "dwT", [128, NCH, CW], BF16)
        binh = T(top, "binh", [128, NCH], F32)
        dwbc = T(top, "dwbc", [128, NCH], F32); lngc = T(top, "lngc", [128, NCH], F32); lnbc = T(top, "lnbc", [128, NCH], F32)
        mod = [T(top, f"mod{l}", [128, 24, 2], F32) for l in range(2)]
        gmul = [T(top, f"gmul{l}", [128, NK, 2], F32) for l in range(2)]
        epsc = T(top, "epsc", [128, 2], F32)
        mhalf = T(top, "mhalf", [128, 512], F32)
        Bc = Buf("consts")
        PSALL = top.enter_context(nc.psum_tensor("psall", [128, 4096], F32))
        PSW = [PSALL[:, i * 1024:(i + 1) * 1024] for i in range(4)]
        PS = [PSALL[:, i * 512:(i + 1) * 512] for i in range(8)]
        PB = [Buf(f"ps{i}") for i in range(8)]

        with contextlib.ExitStack() as ph:
            stage = Pool(nc, ph, "stage", [48, 128], F32, 2)
            dwr = T(ph, "dwr", [CW, E], F32); Bdwr = Buf()
            adaw = T(ph, "adaw", [128, NK, 3 * D], F32); Badaw = Buf()
            craw = T(ph, "craw", [128, 2, NK], F32); condT = T(ph, "condT", [128, 2, NK], F32); Bcond = Buf()
            adabc = T(ph, "adabc", [128, 24], F32)
            adabr = T(ph, "adabr", [2, 3 * D], F32); rowt = T(ph, "rowt", [2, 3 * D], F32); Brow = Buf(); Badabr = Buf()

            fw.dma("sync", ident[:], ident_d, writes=[Bc])
            fw.op("vector", "tensor_copy", dict(out=identb[:], in_=ident[:]), reads=[Bc], writes=[Bc])
            fw.op("vector", "memset", dict(ap=onesb[:], constant=1.0), writes=[Bc])
            fw.op("vector", "memset", dict(ap=mhalf[:, :], constant=-0.5), writes=[Bc])
            fw.op("vector", "memset", dict(ap=epsc[:, 0:1], constant=RMS_EPS), writes=[Bc])
            fw.op("vector", "memset", dict(ap=epsc[:, 1:2], constant=LN_EPS), writes=[Bc])

            def to_cols(vec, n, dst):
                st_t, st_b = stage.get()
                fw.dma("sync", st_t[0:n, :], vec.rearrange("(n p) -> n p", p=128), writes=[st_b])
                fw.op("tensor", "transpose", dict(out=PS[0][:, 0:n], in_=st_t[0:n, :], identity=ident[0:n, 0:n]),
                      reads=[st_b, Bc], writes=[PB[0]])
                fw.op("vector", "tensor_copy", dict(out=dst, in_=PS[0][:, 0:n]), reads=[PB[0]], writes=[Bc])

            to_cols(l0_norm_g, NK, g0c[:]); to_cols(l1_norm_g, NK, g1c[:]); to_cols(l0_b_in, 3 * NCH, binc[:])
            to_cols(l0_dw_b, NCH, dwbc[:]); to_cols(l0_ln_g, NCH, lngc[:]); to_cols(l0_ln_b, NCH, lnbc[:])
            fw.op("vector", "tensor_scalar", dict(out=binh[:], in0=binc[:, NCH:2 * NCH], scalar1=0.5, scalar2=None, op0=ALU.mult), reads=[Bc], writes=[Bc])
            fw.dma("sync", dwr[:], l0_dw_w, writes=[Bdwr])
            for cch in range(NCH):
                fw.op("tensor", "transpose", dict(out=PS[1][:, cch * CW:(cch + 1) * CW], in_=dwr[:, cch * 128:(cch + 1) * 128], identity=ident[0:CW, 0:CW]),
                      reads=[Bdwr, Bc], writes=[PB[1]], signal=(cch == NCH - 1))
            fw.op("vector", "tensor_copy", dict(out=dwT[:].rearrange("p c k -> p (c k)"), in_=PS[1][:, 0:NCH * CW]),
                  reads=[PB[1]], writes=[Bc])
            fw.dma("sync", craw[:, 0, :], c_d.rearrange("(p k) -> p k", k=NK), writes=[Bcond])
            fw.dma("sync", craw[:, 1, :], cctx_d.rearrange("(p k) -> p k", k=NK), writes=[Bcond])
            fw.op("scalar", "activation", dict(out=condT[:], in_=craw[:], func=AF.Silu), reads=[Bcond], writes=[Bcond])
            for l, (aw, ab, gc) in enumerate([(l0_ada_w, l0_ada_b, g0c), (l1_ada_w, l1_ada_b, g1c)]):
                for h in range(2):
                    fw.dma("sync", adaw[:, h * 4:(h + 1) * 4, :], aw.rearrange("(p k) n -> p k n", k=NK)[:, h * 4:(h + 1) * 4, :],
                           writes=[Badaw])
                fw.dma("sync", adabr[:], ab.partition_broadcast(2), writes=[Badabr])
                to_cols(ab, 24, adabc[:])
                for j in range(24):
                    for k in range(NK):
                        fw.op("tensor", "matmul", dict(out=PS[2][:, 2 * j:2 * j + 2], lhsT=adaw[:, k, j * 128:(j + 1) * 128], rhs=condT[:, :, k], start=(k == 0), stop=(k == NK - 1)),
                              reads=[Badaw, Bcond], writes=[PB[2]], signal=(j == 23 and k == NK - 1))
                fw.op("vector", "tensor_tensor", dict(out=mod[l][:], in0=PS[2][:, 0:48].rearrange("p (j c) -> p j c", c=2), in1=adabc[:].unsqueeze(2).to_broadcast([128, 24, 2]), op=ALU.add),
                      reads=[PB[2], Bc], writes=[Bc])
                fw.op("vector", "scalar_tensor_tensor", dict(out=gmul[l][:], in0=mod[l][:, 8:16, :], scalar=1.0, in1=gc[:].unsqueeze(2).to_broadcast([128, NK, 2]), op0=ALU.add, op1=ALU.mult),
                      reads=[Bc], writes=[Bc])
                for n in range(6):
                    pb = 3 + (n % 2)
                    for k in range(NK):
                        fw.op("tensor", "matmul", dict(out=PS[pb][0:2, :], lhsT=condT[:, :, k], rhs=adaw[:, k, n * 512:(n + 1) * 512], start=(k == 0), stop=(k == NK - 1)),
                              reads=[Badaw, Bcond], writes=[PB[pb]], signal=(k == NK - 1))
                    fw.op("vector", "tensor_tensor", dict(out=rowt[:, n * 512:(n + 1) * 512], in0=PS[pb][0:2, :], in1=adabr[:, n * 512:(n + 1) * 512], op=ALU.add),
                          reads=[PB[pb], Badabr], writes=[Brow])
                fw.dma("sync", modrow[l], rowt[:], reads=[Brow])
            fw.barrier()
            fw.flush()

        with contextlib.ExitStack() as ph:
            slab = Pool(nc, ph, "slab", [128, 3 * E], F32, 2)
            slabb = Pool(nc, ph, "slabb", [128, 3 * E], BF16, 2)
            cast_i = [0]

            def cast(dst, src, rb, wb):
                eng = ["vector", "gpsimd", "scalar"][cast_i[0] % 3]
                cast_i[0] += 1
                if eng == "scalar":
                    fw.op(eng, "copy", dict(out=dst, in_=src), reads=[rb], writes=[wb])
                else:
                    fw.op(eng, "tensor_copy", dict(out=dst, in_=src), reads=[rb], writes=[wb])

            for c in range(NCH):
                t, tb = slabb.get()
                fw.op("vector", "tensor_tensor", dict(out=t[:, 0:CW * 128].rearrange("p (k j) -> p k j", j=128),
                                                      in0=identb[:].unsqueeze(1).to_broadcast([128, CW, 128]),
                                                      in1=dwT[:, c, :].unsqueeze(2).to_broadcast([128, CW, 128]), op=ALU.mult),
                      reads=[Bc], writes=[tb])
                fw.dma("gpsimd", diag_bf[c], t[:, 0:CW * 128], reads=[tb])
            for k in range(NK):
                s, sb = slab.get(); t, tb = slabb.get()
                fw.dma("sync", s[:, :], l0_w_in[k * 128:(k + 1) * 128, :], writes=[sb])
                for h in range(3):
                    cast(t[:, h * E:(h + 1) * E], s[:, h * E:(h + 1) * E], sb, tb)
                for tt in range(3):
                    fw.dma("gpsimd", w0in_bf[:, :, k, tt, :].rearrange("c p j -> p c j"),
                           t[:, tt * E:(tt + 1) * E].rearrange("p (c j) -> p c j", j=128), reads=[tb])
            for k in range(NK):
                s, sb = slab.get(); t, tb = slabb.get()
                fw.dma("sync", s[:, 0:W1C], l1_w_in[k * 128:(k + 1) * 128, :], writes=[sb])
                for h in range(2):
                    cast(t[:, h * 2560:(h + 1) * 2560], s[:, h * 2560:(h + 1) * 2560], sb, tb)
                fw.dma("gpsimd", w1in_bf[:, k, :], t[:, 0:W1C], reads=[tb])
            for wsrc, wdst in ((l0_w_out, w0out_bf), (l1_w_out, w1out_bf)):
                for i in range(4):
                    s, sb = slab.get(); t, tb = slabb.get()
                    fw.dma("sync", s[:, 0:4096].rearrange("p (c n) -> p c n", n=D),
                           wsrc.rearrange("(c p) n -> p c n", p=128)[:, 4 * i:4 * i + 4, :], writes=[sb])
                    for h in range(2):
                        cast(t[:, h * 2048:(h + 1) * 2048], s[:, h * 2048:(h + 1) * 2048], sb, tb)
                    fw.dma("gpsimd", wdst[:, 4 * i:4 * i + 4, :], t[:, 0:4096].rearrange("p (c n) -> p c n", n=D), reads=[tb])
            fw.barrier()
            fw.flush()

        with contextlib.ExitStack() as ph:
            w0out = T(ph, "w0out", [128, NCH, D], BF16); Bw0out = Buf()
            gate_t = T(ph, "gate_b", [128, D], F32)
            gbo_t = T(ph, "gbo_b", [128, D], F32)
            gate_b = [gate_t, gate_t]; gbo_b = [gbo_t, gbo_t]
            Bg = Buf()
            fw.dma("sync", w0out[:], w0out_bf, writes=[Bw0out])

            def set_gate(ci):
                fw.dma("sync", gate_t[:], modrow[0, ci, 2 * D:3 * D].partition_broadcast(128), writes=[Bg])
                fw.dma("sync", gbo_t[:], l0_b_out.partition_broadcast(128), writes=[Bg])
                fw.op("vector", "tensor_tensor", dict(out=gbo_t[:], in0=gbo_t[:], in1=gate_t[:], op=ALU.mult), reads=[Bg], writes=[Bg])
            xt_p = Pool(nc, ph, "xt", [128, D], F32, 5)
            ssq = Pool(nc, ph, "ssq", [128, 8], F32, 2)
            hT_p = Pool(nc, ph, "hT", [128, NK, 512 + 2 * HALO], BF16, 2)
            wch_p = Pool(nc, ph, "wch", [128, NK, 3, 128], BF16, 2)
            diag_p = Pool(nc, ph, "diag", [128, CW, 128], BF16, 2)
            sg_p = Pool(nc, ph, "sg", [128, 512 + 2 * HALO], F32, 2)
            v_p = Pool(nc, ph, "v", [128, 512 + 2 * HALO], BF16, 2)
            co = T(ph, "co", [128, NCH, 512], F32); Bco = [Buf() for _ in range(NCH)]
            cob_p = Pool(nc, ph, "cob", [128, 512], BF16, 3)
            sqb_p = Pool(nc, ph, "sqb", [128, 512], BF16, 3)
            sz = T(ph, "sz", [128, NCH, 512], BF16); Bsz = [Buf() for _ in range(NCH)]
            wg = T(ph, "wg", [128, NCH, 512], BF16); Bwg = [Buf() for _ in range(NCH)]
            mean = T(ph, "mean", [128, 512], F32); rstd = T(ph, "rstd", [128, 512], F32); Bst = Buf()
            tmp_p = Pool(nc, ph, "tmpn", [128, 512], F32, 2)
            sil_p = Pool(nc, ph, "sil", [128, 512], BF16, 5)
            xr_p = Pool(nc, ph, "xr", [128, D], F32, 1)
            xo_p = Pool(nc, ph, "xo", [128, D], F32, 1)

            cur_gate = [None]

            def l0_block(src, dst, s0, NT, ci, total):
                left_pad = (s0 == 0)
                right_pad = (s0 + NT == total)
                ntile = NT // 128
                nt1 = ntile + 1
                WT = NT + 2 * HALO
                S = {}
                hT, BhT = hT_p.get()

                def front_load():
                    xts = []
                    for j in range(nt1):
                        xt, Bxt = xt_p.get()
                        xts.append((xt, Bxt))
                        if j < ntile:
                            fw.dma("sync", xt[:, :], src[s0 + j * 128:s0 + (j + 1) * 128, :], writes=[Bxt])
                        else:
                            fw.op("gpsimd", "memset", dict(ap=xt[0:64, :], constant=0.0), writes=[Bxt])
                            if not left_pad:
                                fw.dma("sync", xt[0:HALO, :], src[s0 - HALO:s0, :], writes=[Bxt])
                            if not right_pad:
                                fw.dma("sync", xt[32:32 + HALO, :], src[s0 + NT:s0 + NT + HALO, :], writes=[Bxt])
                    S["xts"] = xts

                def front_pre():
                    ss, Bss = ssq.get()
                    S["ss"], S["Bss"] = ss, Bss
                    fw.op("vector", "memset", dict(ap=ss[:, :], constant=0.0), writes=[Bss])
                    xts = S["xts"]
                    for j in range(nt1):
                        xt, Bxt = xts[j]
                        nr = 128 if j < ntile else 64
                        jt, Bjt = tmp_p.get()
                        fw.op("scalar", "activation", dict(out=jt[0:nr, :].bitcast(BF16), in_=xt[0:nr, :], func=AF.Square, accum_out=ss[0:nr, j:j + 1]),
                              reads=[Bxt], writes=[Bjt, Bss])
                    fw.op("vector", "tensor_scalar", dict(out=ss[:, 0:nt1], in0=ss[:, 0:nt1], scalar1=1.0 / D, scalar2=RMS_EPS, op0=ALU.mult, op1=ALU.add),
                          reads=[Bss], writes=[Bss])
                    fw.op("gpsimd", "tensor_tensor", dict(out=ss[:, 0:nt1], in0=ss[:, 0:nt1], in1=mhalf[:, 0:nt1], op=ALU.pow),
                          reads=[Bss, Bc], writes=[Bss])
                    for j in range(nt1):
                        xt, Bxt = xts[j]
                        nr = 128 if j < ntile else 64
                        fw.op("vector", "tensor_scalar", dict(out=xt[0:nr, :], in0=xt[0:nr, :], scalar1=ss[0:nr, j:j + 1], scalar2=None, op0=ALU.mult),
                              reads=[Bxt, Bss], writes=[Bxt])
                    S["xts"] = xts

                def front_tr():
                    xts = S["xts"]
                    for j in range(nt1):
                        xt, Bxt = xts[j]
                        nr = 128 if j < ntile else 64
                        for half in range(2):
                            pb = (2 * j + half) % 4
                            for kk in range(4):
                                k = half * 4 + kk
                                fw.op("tensor", "transpose", dict(out=PS[pb][:, kk * 128:kk * 128 + nr], in_=xt[0:nr, k * 128:(k + 1) * 128], identity=ident[0:nr, 0:nr]),
                                      reads=[Bxt, Bc], writes=[PB[pb]], signal=(kk == 3))
                            for kk in range(4):
                                k = half * 4 + kk
                                eng = "scalar" if kk % 2 == 0 else "vector"
                                pieces = ([(hT[:, k, HALO + j * 128:HALO + (j + 1) * 128], PS[pb][:, kk * 128:(kk + 1) * 128])] if j < ntile else
                                          [(hT[:, k, 0:HALO], PS[pb][:, kk * 128:kk * 128 + HALO]),
                                           (hT[:, k, HALO + NT:HALO + NT + HALO], PS[pb][:, kk * 128 + 32:kk * 128 + 32 + HALO])])
                                for (o_, i_) in pieces:
                                    if eng == "scalar":
                                        fw.op("scalar", "activation", dict(out=o_, in_=i_, func=AF.Identity, scale=gmul[0][:, k, ci:ci + 1], bias=mod[0][:, k, ci:ci + 1]),
                                              reads=[PB[pb], Bc], writes=[BhT])
                                    else:
                                        fw.op("vector", "tensor_scalar", dict(out=o_, in0=i_, scalar1=gmul[0][:, k, ci:ci + 1], scalar2=mod[0][:, k, ci:ci + 1],
                                                                              op0=ALU.mult, op1=ALU.add),
                                              reads=[PB[pb], Bc], writes=[BhT])

                def p_stage(c):
                    wch, Bwch = wch_p.get()
                    fw.dma("sync", wch[:].rearrange("p k t j -> p (k t j)"), w0in_bf[c].rearrange("p k t j -> p (k t j)"), writes=[Bwch])
                    dg, Bdg = diag_p.get()
                    fw.dma("sync", dg[:].rearrange("p k j -> p (k j)"), diag_bf[c], writes=[Bdg])
                    sg, Bsg = sg_p.get(); v, Bv = v_p.get()
                    w1 = min(WT, 512)
                    rem = WT - w1
                    for k in range(NK):
                        fw.op("tensor", "matmul", dict(out=PS[2][:, 0:w1], lhsT=wch[:, k, 1, :], rhs=hT[:, k, 0:w1], start=(k == 0), stop=(k == NK - 1)),
                              reads=[Bwch, BhT], writes=[PB[2]], signal=(k == NK - 1))
                    if rem > 0:
                        for k in range(NK):
                            fw.op("tensor", "matmul", dict(out=PS[4][:, 0:rem], lhsT=wch[:, k, 1, :], rhs=hT[:, k, 512:WT], start=(k == 0), stop=(k == NK - 1)),
                                  reads=[Bwch, BhT], writes=[PB[4]], signal=(k == NK - 1))
                    fw.op("scalar", "activation", dict(out=sg[:, 0:w1], in_=PS[2][:, 0:w1], func=AF.Tanh, scale=0.5, bias=binh[:, c:c + 1]),
                          reads=[PB[2], Bc], writes=[Bsg])
                    if rem > 0:
                        fw.op("scalar", "activation", dict(out=sg[:, 512:WT], in_=PS[4][:, 0:rem], func=AF.Tanh, scale=0.5, bias=binh[:, c:c + 1]),
                              reads=[PB[4], Bc], writes=[Bsg])
                    fw.op("vector", "tensor_scalar", dict(out=sg[:, 0:WT], in0=sg[:, 0:WT], scalar1=0.5, scalar2=0.5, op0=ALU.mult, op1=ALU.add),
                          reads=[Bsg], writes=[Bsg])
                    for k in range(NK):
                        fw.op("tensor", "matmul", dict(out=PS[3][:, 0:w1], lhsT=wch[:, k, 0, :], rhs=hT[:, k, 0:w1], start=(k == 0), stop=(k == NK - 1)),
                              reads=[Bwch, BhT], writes=[PB[3]], signal=(k == NK - 1))
                    if rem > 0:
                        for k in range(NK):
                            fw.op("tensor", "matmul", dict(out=PS[4][:, 64:64 + rem], lhsT=wch[:, k, 0, :], rhs=hT[:, k, 512:WT], start=(k == 0), stop=(k == NK - 1)),
                                  reads=[Bwch, BhT], writes=[PB[4]], signal=(k == NK - 1))
                    fw.op("vector", "scalar_tensor_tensor", dict(out=v[:, 0:w1], in0=PS[3][:, 0:w1], scalar=binc[:, c:c + 1], in1=sg[:, 0:w1], op0=ALU.add, op1=ALU.mult),
                          reads=[PB[3], Bsg, Bc], writes=[Bv])
                    if rem > 0:
                        fw.op("vector", "scalar_tensor_tensor", dict(out=v[:, 512:WT], in0=PS[4][:, 64:64 + rem], scalar=binc[:, c:c + 1], in1=sg[:, 512:WT], op0=ALU.add, op1=ALU.mult),
                              reads=[PB[4], Bsg, Bc], writes=[Bv])
                    if left_pad:
                        fw.op("vector", "memset", dict(ap=v[:, 0:HALO], constant=0.0), writes=[Bv])
                    if right_pad:
                        fw.op("vector", "memset", dict(ap=v[:, HALO + NT:WT], constant=0.0), writes=[Bv])
                    for k in range(NK):
                        fw.op("tensor", "matmul", dict(out=PS[c % 2][:, 0:NT], lhsT=wch[:, k, 2, :], rhs=hT[:, k, HALO:HALO + NT], start=(k == 0), stop=(k == NK - 1)),
                              reads=[Bwch, BhT], writes=[PB[c % 2]], signal=(k == NK - 1))
                    fw.op("scalar", "activation", dict(out=sz[:, c, 0:NT], in_=PS[c % 2][:, 0:NT], func=AF.Silu, bias=binc[:, 2 * NCH + c:2 * NCH + c + 1]),
                          reads=[PB[c % 2], Bc], writes=[Bsz[c]])
                    return (v, Bv, dg, Bdg)

                def c_stage(c, st):
                    v, Bv, dg, Bdg = st
                    for k in range(CW):
                        fw.op("tensor", "matmul", dict(out=PS[5][:, 0:NT], lhsT=dg[:, k, :], rhs=v[:, k:k + NT], start=(k == 0), stop=(k == CW - 1)),
                              reads=[Bdg, Bv], writes=[PB[5]], signal=(k == CW - 1))
                    fw.op("scalar", "activation", dict(out=co[:, c, 0:NT], in_=PS[5][:, 0:NT], func=AF.Identity, bias=dwbc[:, c:c + 1]),
                          reads=[PB[5], Bc], writes=[Bco[c]])
                    cob, Bcob = cob_p.get(); sqb, Bsqb = sqb_p.get()
                    fw.op("vector", "tensor_copy", dict(out=cob[:, 0:NT], in_=co[:, c, 0:NT]), reads=[Bco[c]], writes=[Bcob])
                    fw.op("gpsimd", "tensor_tensor", dict(out=sqb[:, 0:NT], in0=co[:, c, 0:NT], in1=co[:, c, 0:NT], op=ALU.mult),
                          reads=[Bco[c]], writes=[Bsqb])
                    S["pend_stats"] = (c, cob, Bcob, sqb, Bsqb)

                def stats_mm():
                    if S.get("pend_stats") is None:
                        return
                    c, cob, Bcob, sqb, Bsqb = S["pend_stats"]
                    S["pend_stats"] = None
                    fw.op("tensor", "matmul", dict(out=PS[6][:, 0:NT], lhsT=onesb[:], rhs=cob[:, 0:NT], start=(c == 0), stop=(c == NCH - 1)),
                          reads=[Bcob, Bc], writes=[PB[6]])
                    fw.op("tensor", "matmul", dict(out=PS[7][:, 0:NT], lhsT=onesb[:], rhs=sqb[:, 0:NT], start=(c == 0), stop=(c == NCH - 1)),
                          reads=[Bsqb, Bc], writes=[PB[7]])

                S["prev"] = None

                def mid(c0, c1):
                    for c in range(c0, c1):
                        cur = p_stage(c)
                        stats_mm()
                        if S["prev"] is not None:
                            c_stage(c - 1, S["prev"])
                        S["prev"] = cur
                    if c1 == NCH:
                        stats_mm()
                        c_stage(NCH - 1, S["prev"])
                        stats_mm()

                def stats_n():
                    fw.op("vector", "tensor_scalar", dict(out=mean[:, 0:NT], in0=PS[6][:, 0:NT], scalar1=1.0 / E, scalar2=None, op0=ALU.mult),
                          reads=[PB[6]], writes=[Bst])
                    fw.op("vector", "tensor_tensor", dict(out=rstd[:, 0:NT], in0=mean[:, 0:NT], in1=mean[:, 0:NT], op=ALU.mult),
                          reads=[Bst], writes=[Bst])
                    fw.op("vector", "scalar_tensor_tensor", dict(out=rstd[:, 0:NT], in0=PS[7][:, 0:NT], scalar=1.0 / E, in1=rstd[:, 0:NT], op0=ALU.mult, op1=ALU.subtract),
                          reads=[PB[7], Bst], writes=[Bst])
                    fw.op("scalar", "activation", dict(out=rstd[:, 0:NT], in_=rstd[:, 0:NT], func=AF.Sqrt, bias=epsc[:, 1:2]),
                          reads=[Bst, Bc], writes=[Bst])
                    fw.op("vector", "reciprocal", dict(out=rstd[:, 0:NT], in_=rstd[:, 0:NT]), reads=[Bst], writes=[Bst])

                def n_pre(c0, c1):
                    for c in range(c0, c1):
                        fw.op("vector", "tensor_tensor", dict(out=co[:, c, 0:NT], in0=co[:, c, 0:NT], in1=mean[:, 0:NT], op=ALU.subtract),
                              reads=[Bst], writes=[Bco[c]])
                        fw.op("vector", "tensor_tensor", dict(out=co[:, c, 0:NT], in0=co[:, c, 0:NT], in1=rstd[:, 0:NT], op=ALU.mult),
                              reads=[Bst], writes=[Bco[c]])

                def n_post(c0, c1):
                    for c in range(c0, c1):
                        sl, Bsl = sil_p.get()
                        fw.op("scalar", "activation", dict(out=sl[:, 0:NT], in_=co[:, c, 0:NT], func=AF.Silu, scale=lngc[:, c:c + 1], bias=lnbc[:, c:c + 1]),
                              reads=[Bco[c], Bc], writes=[Bsl])
                        fw.op("gpsimd", "tensor_tensor", dict(out=wg[:, c, 0:NT], in0=sl[:, 0:NT], in1=sz[:, c, 0:NT], op=ALU.mult),
                              reads=[Bsl, Bsz[c]], writes=[Bwg[c]])

                def o_stage():
                    if cur_gate[0] != ci:
                        set_gate(ci)
                        cur_gate[0] = ci
                    for j in range(ntile):
                        xr, Bxr = xr_p.get(); xo, Bxo = xo_p.get()
                        fw.dma("gpsimd", xr[:, :], src[s0 + j * 128:s0 + (j + 1) * 128, :], writes=[Bxr])
                        fw.op("gpsimd", "tensor_tensor", dict(out=xr[:, :], in0=xr[:, :], in1=gbo_b[ci][:], op=ALU.add),
                              reads=[Bxr, Bg], writes=[Bxr])
                        for half in range(2):
                            pb = half
                            for c in range(NCH):
                                fw.op("tensor", "matmul", dict(out=PS[pb][:, :], lhsT=wg[:, c, j * 128:(j + 1) * 128], rhs=w0out[:, c, half * 512:(half + 1) * 512], start=(c == 0), stop=(c == NCH - 1)),
                                    reads=[Bwg[c], Bw0out], writes=[PB[pb]], signal=(c == NCH - 1))
                            fw.op("vector", "tensor_tensor", dict(out=xo[:, half * 512:(half + 1) * 512], in0=PS[pb][:, :], in1=gate_b[ci][:, half * 512:(half + 1) * 512], op=ALU.mult),
                                reads=[PB[pb], Bg], writes=[Bxo])
                        fw.op("vector", "tensor_tensor", dict(out=xo[:, :], in0=xo[:, :], in1=xr[:, :], op=ALU.add),
                              reads=[Bxo, Bxr], writes=[Bxo])
                        fw.dma("gpsimd", dst[s0 + j * 128:s0 + (j + 1) * 128, :], xo[:, :], reads=[Bxo])

                return dict(front_load=front_load, front_pre=front_pre, front_tr=front_tr, mid=mid, stats_n=stats_n, n_pre=n_pre, n_post=n_post, o_stage=o_stage)

            specs = [(ctx_d, ctx1_d, 0, CTX, 1, CTX)] + [(x_d, x1_d, blk * 512, 512, 0, L) for blk in range(L // 512)]
            blk_objs = {}

            def getb(bi):
                if bi not in blk_objs:
                    blk_objs[bi] = l0_block(*specs[bi])
                return blk_objs[bi]
            nb = len(specs)
            A = getb(0)
            A["front_load"](); A["front_pre"](); A["front_tr"](); A["mid"](0, 4)
            if nb > 1:
                getb(1)["front_load"]()
            A["mid"](4, 12)
            for bi in range(1, nb):
                A = getb(bi - 1); Bk = getb(bi)
                Bk["front_pre"]()
                A["mid"](12, NCH)
                Bk["front_tr"]()
                if bi + 1 < nb:
                    getb(bi + 1)["front_load"]()
                A["stats_n"]()
                A["n_pre"](0, 2)
                for i in range(8):
                    A["n_post"](2 * i, 2 * i + 2)
                    Bk["mid"](i, i + 1)
                    if i < 7:
                        A["n_pre"](2 * i + 2, 2 * i + 4)
                A["o_stage"]()
                Bk["mid"](8, 12)
            A = getb(nb - 1)
            A["mid"](12, NCH); A["stats_n"](); A["n_pre"](0, NCH); A["n_post"](0, NCH); A["o_stage"]()
            fw.barrier()
            fw.flush()

        if debug == "l0":
            with contextlib.ExitStack() as ph:
                t = T(ph, "dbg", [128, D], F32); Bt = Buf()
                for j in range(L // 128):
                    fw.dma("sync", t[:], x1_d[j * 128:(j + 1) * 128, :], reads=[], writes=[Bt])
                    fw.dma("sync", out_d[j * 128:(j + 1) * 128, :], t[:], reads=[Bt])
                fw.barrier()
                fw.flush()
            return nc

        with contextlib.ExitStack() as ph:
            w1in = T(ph, "w1in", [128, NK, 3072], BF16); Bw1 = Buf()
            fw.dma("sync", w1in[:, 0:4, :], w1in_bf[:, 0:4, 0:3072], writes=[Bw1])
            fw.dma("sync", w1in[:, 4:8, :], w1in_bf[:, 4:8, 0:3072], writes=[Bw1])
            wz_p = Pool(nc, ph, "wz", [128, NK, 128], BF16, 3)
            gqk = T(ph, "gqk", [128, 20, 128], F32); Bgqk = Buf()
            fw.dma("sync", gqk[:, 0, :], l1_q_norm_g.partition_broadcast(128), writes=[Bgqk])
            fw.dma("sync", gqk[:, 16, :], l1_k_norm_g.partition_broadcast(128), writes=[Bgqk])
            fw.op("vector", "tensor_copy", dict(out=gqk[:, 1:16, :], in_=gqk[:, 0:1, :].to_broadcast([128, 15, 128])),
                  reads=[Bgqk], writes=[Bgqk])
            fw.op("vector", "tensor_copy", dict(out=gqk[:, 17:20, :], in_=gqk[:, 16:17, :].to_broadcast([128, 3, 128])),
                  reads=[Bgqk], writes=[Bgqk])
            xt_p = Pool(nc, ph, "xt2", [128, D], F32, 8)
            junk = T(ph, "junk2", [128, D], BF16); Bjunk = Buf()
            ssq = Pool(nc, ph, "ssq2", [128, 4], F32, 3)
            hT_p = Pool(nc, ph, "hT2", [128, NK, 512], BF16, 2)
            qk_p = Pool(nc, ph, "qk", [128, 20, 128], F32, 2)
            scr_p = Pool(nc, ph, "scr", [128, 20, 128], F32, 2)
            hs_p = Pool(nc, ph, "hs", [128, 20], F32, 3)
            cs_p = Pool(nc, ph, "cs", [128, 2, 64], F32, 3)
            qr_p = Pool(nc, ph, "qr", [128, 20, 128], BF16, 2)
            qst_p = Pool(nc, ph, "qst", [128, 20, 128], BF16, 3)
            vst_p = Pool(nc, ph, "vst", [128, KVD], BF16, 3)
            szst_p = Pool(nc, ph, "szst", [128, 4, 512], BF16, 3)

            def x_pre(src, s0, NT):
                ntile = NT // 128
                ss, Bss = ssq.get()
                fw.op("vector", "memset", dict(ap=ss[:, :], constant=0.0), writes=[Bss])
                xts = []
                for j in range(ntile):
                    xt, Bxt = xt_p.get(); xts.append((xt, Bxt))
                    fw.dma("sync", xt[:, :], src[s0 + j * 128:s0 + (j + 1) * 128, :], writes=[Bxt])
                    fw.op("scalar", "activation", dict(out=junk[:, :], in_=xt[:, :], func=AF.Square, accum_out=ss[:, j:j + 1]),
                          reads=[Bxt], writes=[Bjunk, Bss])
                fw.op("vector", "tensor_scalar", dict(out=ss[:, 0:ntile], in0=ss[:, 0:ntile], scalar1=1.0 / D, scalar2=RMS_EPS, op0=ALU.mult, op1=ALU.add),
                      reads=[Bss], writes=[Bss])
                fw.op("gpsimd", "tensor_tensor", dict(out=ss[:, 0:ntile], in0=ss[:, 0:ntile], in1=mhalf[:, 0:ntile], op=ALU.pow),
                      reads=[Bss, Bc], writes=[Bss])
                for j in range(ntile):
                    xt, Bxt = xts[j]
                    fw.op("vector", "tensor_scalar", dict(out=xt[:, :], in0=xt[:, :], scalar1=ss[:, j:j + 1], scalar2=None, op0=ALU.mult),
                          reads=[Bxt, Bss], writes=[Bxt])
                return xts

            def x_tr(xtb, j, hT, BhT, ci):
                xt, Bxt = xtb
                for half in range(2):
                    pb = half
                    for kk in range(4):
                        k = half * 4 + kk
                        fw.op("tensor", "transpose", dict(out=PS[pb][:, kk * 128:(kk + 1) * 128], in_=xt[:, k * 128:(k + 1) * 128], identity=ident[:, :]),
                              reads=[Bxt, Bc], writes=[PB[pb]], signal=(kk == 3))
                    for kk in range(4):
                        k = half * 4 + kk
                        if True:
                            fw.op("scalar", "activation", dict(out=hT[:, k, j * 128:(j + 1) * 128], in_=PS[pb][:, kk * 128:(kk + 1) * 128],
                                                               func=AF.Identity, scale=gmul[1][:, k, ci:ci + 1], bias=mod[1][:, k, ci:ci + 1]),
                                  reads=[PB[pb], Bc], writes=[BhT])
                        else:
                            fw.op("vector", "tensor_scalar", dict(out=hT[:, k, j * 128:(j + 1) * 128], in0=PS[pb][:, kk * 128:(kk + 1) * 128],
                                                                  scalar1=gmul[1][:, k, ci:ci + 1], scalar2=mod[1][:, k, ci:ci + 1],
                                                                  op0=ALU.mult, op1=ALU.add),
                                  reads=[PB[pb], Bc], writes=[BhT])

            def s1_stage(tl):
                hT, BhT, j, is_ctx = tl["hT"], tl["BhT"], tl["j"], tl["is_ctx"]
                nh0 = 16 if is_ctx else 0
                qk, Bqk = qk_p.get(); scr, Bscr = scr_p.get(); vst, Bvst = vst_p.get()
                tl.update(qk=qk, Bqk=Bqk, scr=scr, Bscr=Bscr, vst=vst, Bvst=Bvst, nh0=nh0)
                groups = ([] if is_ctx else [0, 1, 2, 3]) + [4, 5]
                for gi, g in enumerate(groups):
                    pb = 2 + (gi % 2)
                    for k in range(NK):
                        fw.op("tensor", "matmul", dict(out=PS[pb][:, :], lhsT=hT[:, k, j * 128:(j + 1) * 128], rhs=w1in[:, k, g * 512:(g + 1) * 512],
                                                       start=(k == 0), stop=(k == NK - 1)),
                              reads=[BhT, Bw1], writes=[PB[pb]], signal=(k == NK - 1))
                    if g < 5:
                        fw.op("scalar", "copy", dict(out=qk[:, g * 4:(g + 1) * 4, :].rearrange("p h d -> p (h d)"), in_=PS[pb][:, :]),
                              reads=[PB[pb]], writes=[Bqk])
                        fw.op("scalar", "activation", dict(out=scr[:, g * 4:(g + 1) * 4, :].rearrange("p h d -> p (h d)"), in_=PS[pb][:, :], func=AF.Square),
                              reads=[PB[pb]], writes=[Bscr])
                    else:
                        fw.op("vector", "tensor_copy", dict(out=vst[:, :], in_=PS[pb][:, :]), reads=[PB[pb]], writes=[Bvst])

            def s2_stage(tl):
                qk, Bqk, scr, Bscr, nh0, is_ctx = tl["qk"], tl["Bqk"], tl["scr"], tl["Bscr"], tl["nh0"], tl["is_ctx"]
                nh = 20 - nh0
                hs, Bhs = hs_p.get()
                fw.op("vector", "tensor_reduce", dict(out=hs[:, nh0:20], in_=scr[:, nh0:20, :], axis=AX.X, op=ALU.add),
                      reads=[Bscr], writes=[Bhs])
                fw.op("vector", "tensor_scalar", dict(out=hs[:, nh0:20], in0=hs[:, nh0:20], scalar1=1.0 / 128, scalar2=RMS_EPS, op0=ALU.mult, op1=ALU.add),
                      reads=[Bhs], writes=[Bhs])
                fw.op("scalar", "activation", dict(out=hs[:, nh0:20], in_=hs[:, nh0:20], func=AF.Sqrt), reads=[Bhs], writes=[Bhs])
                fw.op("vector", "reciprocal", dict(out=hs[:, nh0:20], in_=hs[:, nh0:20]), reads=[Bhs], writes=[Bhs])
                fw.op("vector", "tensor_tensor", dict(out=qk[:, nh0:20, :], in0=qk[:, nh0:20, :],
                                                      in1=hs[:, nh0:20].unsqueeze(2).to_broadcast([128, nh, 128]), op=ALU.mult),
                      reads=[Bqk, Bhs], writes=[Bqk])
                qr, Bqr = qr_p.get()
                tl.update(qr=qr, Bqr=Bqr)
                if is_ctx:
                    fw.op("vector", "tensor_tensor", dict(out=qr[:, nh0:20, :], in0=qk[:, nh0:20, :], in1=gqk[:, nh0:20, :], op=ALU.mult),
                          reads=[Bqk, Bgqk], writes=[Bqr])
                    return
                fw.op("vector", "tensor_tensor", dict(out=qk[:, :, :], in0=qk[:, :, :], in1=gqk[:, :, :], op=ALU.mult),
                      reads=[Bqk, Bgqk], writes=[Bqk])
                cs, Bcs = cs_p.get()
                t0 = tl["t0"]
                fw.dma("sync", cs[:, 0, :], cos_d[t0:t0 + 128, :], writes=[Bcs])
                fw.dma("sync", cs[:, 1, :], sin_d[t0:t0 + 128, :], writes=[Bcs])
                qv = qk[:, :, :].rearrange("p h (a b i) -> p h a b i", a=2, b=2)
                qo = qr[:, :, :].rearrange("p h (a b i) -> p h a b i", a=2, b=2)
                x1v = qv[:, :, :, 0, :]; x2v = qv[:, :, :, 1, :]
                Cb = cs[:, 0, :].rearrange("p (a i) -> p a i", a=2).unsqueeze(1).to_broadcast([128, 20, 2, 32])
                Sb = cs[:, 1, :].rearrange("p (a i) -> p a i", a=2).unsqueeze(1).to_broadcast([128, 20, 2, 32])
                t1 = scr[:, 0:10, :].rearrange("p h (a i) -> p (h a) i", a=4).rearrange("p (h a) i -> p h a i", a=2)
                t2 = scr[:, 10:20, :].rearrange("p h (a i) -> p (h a) i", a=4).rearrange("p (h a) i -> p h a i", a=2)
                fw.op("vector", "tensor_tensor", dict(out=t1, in0=x1v, in1=Cb, op=ALU.mult), reads=[Bqk, Bcs], writes=[Bscr])
                fw.op("vector", "tensor_tensor", dict(out=t2, in0=x2v, in1=Sb, op=ALU.mult), reads=[Bqk, Bcs], writes=[Bscr])
                fw.op("vector", "tensor_tensor", dict(out=qo[:, :, :, 0, :], in0=t1, in1=t2, op=ALU.subtract), reads=[Bscr], writes=[Bqr])
                fw.op("vector", "tensor_tensor", dict(out=t1, in0=x1v, in1=Sb, op=ALU.mult), reads=[Bqk, Bcs], writes=[Bscr])
                fw.op("vector", "tensor_tensor", dict(out=t2, in0=x2v, in1=Cb, op=ALU.mult), reads=[Bqk, Bcs], writes=[Bscr])
                fw.op("vector", "tensor_tensor", dict(out=qo[:, :, :, 1, :], in0=t1, in1=t2, op=ALU.add), reads=[Bscr], writes=[Bqr])

            def s3_stage(tl):
                qr, Bqr, nh0, is_ctx = tl["qr"], tl["Bqr"], tl["nh0"], tl["is_ctx"]
                qst, Bqst = qst_p.get()
                hgroups = [(16, 20, 4)] if is_ctx else [(0, 8, 4), (8, 16, 5), (16, 20, 4)]
                for gi, (h0, h1, pb) in enumerate(hgroups):
                    ptv = PS[pb].bitcast(BF16)
                    for h in range(h0, h1):
                        fw.op("tensor", "transpose", dict(out=ptv[:, (h - h0) * 128:(h - h0 + 1) * 128], in_=qr[:, h, :], identity=identb[:, :]),
                              reads=[Bqr, Bc], writes=[PB[pb]], signal=(h == h1 - 1))
                    nw = (h1 - h0) * 128
                    fw.op("scalar", "copy", dict(out=qst[:, h0:h1, :].rearrange("p h t -> p (h t)"), in_=ptv[:, 0:nw]),
                          reads=[PB[pb]], writes=[Bqst])
                t0, key0 = tl["t0"], tl["key0"]
                if not is_ctx:
                    fw.dma("gpsimd", qT_d[:, :, t0:t0 + 128].rearrange("h p t -> p h t"), qst[:, 0:16, :], reads=[Bqst])
                fw.dma("gpsimd", kT_d[:, :, key0:key0 + 128].rearrange("h p t -> p h t"), qst[:, 16:20, :], reads=[Bqst])
                fw.dma("gpsimd", V_d[key0:key0 + 128, :], tl["vst"][:, :], reads=[tl["Bvst"]])

            def z_stage(hT, BhT, s0):
                for c4 in range(4):
                    szst, Bszst = szst_p.get()
                    for cc in range(4):
                        c = c4 * 4 + cc
                        pb = 6 + (c % 2)
                        wz, Bwz = wz_p.get()
                        fw.dma("sync", wz[:, :, :], w1in_bf[:, :, 3072 + c * 128:3072 + (c + 1) * 128], writes=[Bwz])
                        for k in range(NK):
                            fw.op("tensor", "matmul", dict(out=PS[pb][:, :], lhsT=wz[:, k, :], rhs=hT[:, k, :], start=(k == 0), stop=(k == NK - 1)),
                                  reads=[BhT, Bwz], writes=[PB[pb]], signal=(k == NK - 1))
                        fw.op("scalar", "activation", dict(out=szst[:, cc, :], in_=PS[pb][:, :], func=AF.Silu),
                              reads=[PB[pb]], writes=[Bszst])
                    fw.dma("gpsimd", szT_d[c4 * 4:(c4 + 1) * 4, :, s0:s0 + 512].rearrange("c p t -> p c t"), szst[:, :, :], reads=[Bszst])

            blocks = [(ctx1_d, 0, CTX, 1, True, 0)] + [(x1_d, blk * 512, 512, 0, False, CTX + blk * 512) for blk in range(L // 512)]
            tiles = []
            for bi, (src, s0, NT, ci, is_ctx, key0) in enumerate(blocks):
                for j in range(NT // 128):
                    tiles.append(dict(bi=bi, j=j, first=(j == 0), last=(j == NT // 128 - 1), is_ctx=is_ctx,
                                      t0=s0 + j * 128, key0=key0 + j * 128))
            hTs = {}
            for bi in (0, 1):
                src, s0, NT, ci, is_ctx, key0 = blocks[bi]
                xts = x_pre(src, s0, NT)
                hTs[bi] = hT_p.get()
                for j in range(NT // 128):
                    x_tr(xts[j], j, hTs[bi][0], hTs[bi][1], ci)
            nxt = None
            for gidx in range(len(tiles) + 2):
                if gidx < len(tiles):
                    tl = tiles[gidx]
                    bi = tl["bi"]
                    src, s0, NT, ci, is_ctx, key0 = blocks[bi]
                    if tl["first"] and bi >= 1 and bi + 1 < len(blocks):
                        nsrc, ns0, nNT, nci, _, _ = blocks[bi + 1]
                        nxt = (x_pre(nsrc, ns0, nNT), nci)
                        hTs[bi + 1] = hT_p.get()
                    tl["hT"], tl["BhT"] = hTs[bi]
                    if gidx >= 1:
                        s2_stage(tiles[gidx - 1])
                    s1_stage(tl)
                    if gidx >= 2:
                        s3_stage(tiles[gidx - 2])
                    if bi >= 1 and bi + 1 < len(blocks):
                        x_tr(nxt[0][tl["j"]], tl["j"], hTs[bi + 1][0], hTs[bi + 1][1], nxt[1])
                else:
                    if gidx == len(tiles):
                        s2_stage(tiles[gidx - 1])
                    s3_stage(tiles[gidx - 2])
                if gidx < len(tiles) and tiles[gidx]["last"] and not tiles[gidx]["is_ctx"]:
                    bi = tiles[gidx]["bi"]
                    z_stage(hTs[bi][0], hTs[bi][1], blocks[bi][1])
            fw.barrier()
            fw.flush()

        with contextlib.ExitStack() as ph:
            kT_p = Pool(nc, ph, "kTh", [128, NKEY], BF16, 2)
            V_p = Pool(nc, ph, "Vh", [128, NKC, 128], BF16, 2)
            q_p = Pool(nc, ph, "qblk", [128, 512], BF16, 3)
            szb_p = Pool(nc, ph, "szblk", [128, 512], BF16, 3)
            p_p = Pool(nc, ph, "pT", [128, 1024], BF16, 8)
            accA_p = Pool(nc, ph, "accA", [128, 1024], BF16, 2)
            accB_p = Pool(nc, ph, "accB", [128, 1024], BF16, 2)
            rden_p = Pool(nc, ph, "rden", [128, 512], F32, 2)
            o_p = Pool(nc, ph, "osb", [128, 512], F32, 2)
            w_p = Pool(nc, ph, "wsb", [128, 512], BF16, 3)
            scale = 128.0 ** -0.5
            NQB = L // 512
            NT2 = NKC // 2
            assert NKC % 2 == 0 and NT2 >= 3
            SW = [0, 1, 2]
            BSW = [Buf(), Buf(), Buf()]
            srr = [0]

            def s_get():
                i = srr[0] % 3
                srr[0] += 1
                return PSW[SW[i]], BSW[i]
            unit = 0
            pending = [None]
            for hk in range(NKV):
                kTh, BkT = kT_p.get(); Vh, BV = V_p.get()
                fw.dma("sync", kTh[:, :], kT_d[hk], writes=[BkT])
                fw.dma("sync", Vh[:, :, :], V_d[:, hk * 128:(hk + 1) * 128].rearrange("(c p) d -> p c d", p=128), writes=[BV])
                for g in range(4):
                    h = hk * 4 + g
                    for qb in range(NQB):
                        qblk, Bq = q_p.get(); szb, Bszb = szb_p.get()
                        fw.dma("sync", qblk[:, :], qT_d[h, :, qb * 512:(qb + 1) * 512], writes=[Bq])
                        fw.dma("sync", szb[:, :], szT_d[h, :, qb * 512:(qb + 1) * 512], writes=[Bszb])
                        po = 6 + (unit % 2)
                        unit += 1
                        accA, BaA = accA_p.get(); accB, BaB = accB_p.get()
                        pts = {}
                        nB = [0]

                        def s_stage(t):
                            st_, Bst_ = s_get()
                            for i in range(2):
                                kc = 2 * t + i
                                fw.op("tensor", "matmul", dict(out=st_[:, i * 512:(i + 1) * 512], lhsT=kTh[:, kc * 128:(kc + 1) * 128], rhs=qblk[:, :],
                                                               start=True, stop=True),
                                      reads=[BkT, Bq], writes=[Bst_], signal=(i == 1))
                            pT, BpT = p_p.get()
                            fw.op("scalar", "activation", dict(out=pT[:, :], in_=st_[:, :], func=AF.Exp, scale=scale),
                                  reads=[Bst_], writes=[BpT])
                            pts[t] = (pT, BpT)
                            if False:
                                pass
                            elif t == 1:
                                p0, Bp0 = pts[0]
                                fw.op("vector", "tensor_tensor", dict(out=accA[:, :], in0=p0[:, :], in1=pT[:, :], op=ALU.add),
                                      reads=[Bp0, BpT], writes=[BaA])
                            elif t > 1:
                                fw.op("vector", "tensor_tensor", dict(out=accA[:, :], in0=accA[:, :], in1=pT[:, :], op=ALU.add),
                                      reads=[BpT, BaA], writes=[BaA])

                        def pv_stage(t):
                            pT, BpT = pts.pop(t)
                            for i in range(2):
                                kc = 2 * t + i
                                fw.op("tensor", "matmul", dict(out=PS[po][:, :], lhsT=Vh[:, kc, :], rhs=pT[:, i * 512:(i + 1) * 512],
                                                               start=(kc == 0), stop=(kc == NKC - 1)),
                                      reads=[BV, BpT], writes=[PB[po]], signal=(i == 1))

                        LAG = 2
                        for t in range(NT2 + LAG):
                            if t < NT2:
                                s_stage(t)
                            if t >= LAG:
                                pv_stage(t - LAG)
                            if t == 1 and pending[0] is not None:
                                pending[0]()
                                pending[0] = None

                        def finalize(accA=accA, BaA=BaA, po=po, szb=szb, Bszb=Bszb, h=h, qb=qb):
                            pdt, Bpd = s_get()
                            for i in range(2):
                                fw.op("tensor", "matmul", dict(out=pdt[:, 0:512], lhsT=onesb[:, :], rhs=accA[:, i * 512:(i + 1) * 512],
                                                               start=(i == 0), stop=(i == 1)),
                                      reads=[Bc, BaA], writes=[Bpd], signal=(i == 1))
                            rden, Brd = rden_p.get(); osb, Bo = o_p.get(); wsb, Bw = w_p.get()
                            fw.op("vector", "reciprocal", dict(out=rden[:, :], in_=pdt[:, 0:512]), reads=[Bpd], writes=[Brd])
                            fw.op("vector", "tensor_tensor", dict(out=osb[:, :], in0=PS[po][:, :], in1=rden[:, :], op=ALU.mult),
                                  reads=[PB[po], Brd], writes=[Bo])
                            fw.op("gpsimd", "tensor_tensor", dict(out=wsb[:, :], in0=osb[:, :], in1=szb[:, :], op=ALU.mult),
                                  reads=[Bo, Bszb], writes=[Bw])
                            fw.dma("gpsimd", wT_d[h, :, qb * 512:(qb + 1) * 512], wsb[:, :], reads=[Bw])
                        pending[0] = finalize
            if pending[0] is not None:
                pending[0]()
            fw.barrier()
            fw.flush()

        with contextlib.ExitStack() as ph:
            w1out = T(ph, "w1out", [128, NCH, D], BF16); Bw = Buf()
            fw.dma("sync", w1out[:], w1out_bf, writes=[Bw])
            gate1 = T(ph, "gate1", [128, D], F32); fng = T(ph, "fng", [128, D], F32); Bg = Buf()
            fw.dma("sync", gate1[:], modrow[1, 0, 2 * D:3 * D].partition_broadcast(128), writes=[Bg])
            fw.dma("sync", fng[:], final_norm_g.partition_broadcast(128), writes=[Bg])
            wt_p = Pool(nc, ph, "wTt", [128, NCH, 512], BF16, 2)
            xr_p = Pool(nc, ph, "xr3", [128, D], F32, 3)
            xo_p = Pool(nc, ph, "xo3", [128, D], F32, 3)
            junk = T(ph, "junk3", [128, D], BF16); Bjunk = Buf()
            ss_p = Pool(nc, ph, "ss3", [128, 1], F32, 3)
            for blk in range(L // 512):
                wt, Bwt = wt_p.get()
                fw.dma("sync", wt[:, :, :], wT_d[:, :, blk * 512:(blk + 1) * 512].rearrange("h p t -> p h t"), writes=[Bwt])
                for j in range(4):
                    t0 = blk * 512 + j * 128
                    xr, Bxr = xr_p.get(); xo, Bxo = xo_p.get(); ss, Bss = ss_p.get()
                    fw.dma("sync", xr[:, :], x1_d[t0:t0 + 128, :], writes=[Bxr])
                    for half in range(2):
                        pb = (j % 2) * 2 + half
                        for c in range(NCH):
                            fw.op("tensor", "matmul", dict(out=PS[pb][:, :], lhsT=wt[:, c, j * 128:(j + 1) * 128], rhs=w1out[:, c, half * 512:(half + 1) * 512], start=(c == 0), stop=(c == NCH - 1)),
                                reads=[Bwt, Bw], writes=[PB[pb]], signal=(c == NCH - 1))
                        fw.op("vector", "tensor_tensor", dict(out=xo[:, half * 512:(half + 1) * 512], in0=PS[pb][:, :], in1=gate1[:, half * 512:(half + 1) * 512], op=ALU.mult),
                            reads=[PB[pb], Bg], writes=[Bxo])
                    fw.op("gpsimd", "tensor_tensor", dict(out=xo[:, :], in0=xo[:, :], in1=xr[:, :], op=ALU.add),
                          reads=[Bxo, Bxr], writes=[Bxo])
                    fw.op("scalar", "activation", dict(out=junk[:, :], in_=xo[:, :], func=AF.Square, accum_out=ss[:, 0:1]),
                          reads=[Bxo], writes=[Bjunk, Bss])
                    fw.op("vector", "tensor_scalar", dict(out=ss[:, :], in0=ss[:, :], scalar1=1.0 / D, scalar2=RMS_EPS, op0=ALU.mult, op1=ALU.add),
                          reads=[Bss], writes=[Bss])
                    fw.op("gpsimd", "tensor_tensor", dict(out=ss[:, :], in0=ss[:, :], in1=mhalf[:, 0:1], op=ALU.pow),
                          reads=[Bss, Bc], writes=[Bss])
                    fw.op("vector", "scalar_tensor_tensor", dict(out=xo[:, :], in0=xo[:, :], scalar=ss[:, 0:1], in1=fng[:, :], op0=ALU.mult, op1=ALU.mult),
                          reads=[Bxo, Bss, Bg], writes=[Bxo])
                    fw.dma("gpsimd", out_d[t0:t0 + 128, :], xo[:, :], reads=[Bxo])
            fw.barrier()
            fw.flush()
    return nc


def _rope_tables(L):
    rows = L // 64
    row = np.broadcast_to(np.arange(rows)[:, None], (rows, 64)).reshape(-1).astype(np.float32)
    col = np.broadcast_to(np.arange(64)[None, :], (rows, 64)).reshape(-1).astype(np.float32)
    inv_freq = (np.float32(10000.0) ** (-np.arange(0, 64, 2, dtype=np.float32) / np.float32(64))).astype(np.float32)
    ang = np.concatenate([row[:, None] * inv_freq[None, :], col[:, None] * inv_freq[None, :]], axis=1).astype(np.float32)
    return np.cos(ang).astype(np.float32), np.sin(ang).astype(np.float32)


_CACHE = {}


def make_in_maps(inputs, L, ncores):
    cos, sin = _rope_tables(L)
    ident = np.eye(128, dtype=np.float32)
    maps = []
    f = lambda a: np.ascontiguousarray(np.asarray(a, dtype=np.float32))
    shared = {k: f(v) for k, v in inputs.items() if k not in ("x", "c", "ctx")}
    for b in range(ncores):
        m = dict(shared)
        m["x"] = f(inputs["x"][b]); m["c"] = f(inputs["c"][b]); m["ctx"] = f(inputs["ctx"][b])
        m["ident"] = ident; m["rope_cos"] = cos; m["rope_sin"] = sin
        maps.append(m)
    return maps


def kernel(**inputs):
    x = np.asarray(inputs["x"])
    B, L, _ = x.shape
    key = (L,)
    if key not in _CACHE:
        _CACHE[key] = build_program(L)
    nc = _CACHE[key]
    maps = make_in_maps(inputs, L, B)
    res = run_bass_kernel_spmd(nc, maps, core_ids=list(range(B)))
    return np.stack([np.asarray(r["out"], dtype=np.float32) for r in res.results], axis=0)
```

```python
import contextlib
import numpy as np
import concourse.bass as bass
import concourse.mybir as mybir
from concourse.bass_utils import run_bass_kernel_spmd

F32 = mybir.dt.float32
BF16 = mybir.dt.bfloat16
ALU = mybir.AluOpType
AF = mybir.ActivationFunctionType
AX = mybir.AxisListType

D = 1024
E = 2048
NK = 8
NCH = 16
CW = 31
HALO = 15
CTX = 256
NQH = 16
NKV = 4
KVD = 512
W1C = 2 * E + 2 * KVD
RMS_EPS = 1e-6
LN_EPS = 1e-5
SEQ = 8192
NCORES = 8

ENGS = ["tensor", "vector", "scalar", "gpsimd", "sync"]
N_DMA_SEMS = {"sync": 14, "vector": 0, "scalar": 4, "gpsimd": 8, "tensor": 0}


class Buf:
    __slots__ = ("name", "last_write", "reads")

    def __init__(self, name=""):
        self.name = name
        self.last_write = None
        self.reads = []


class EngState:
    def __init__(self, name):
        self.name = name
        self.n = 0
        self.seen = {}
        self.queue = []
        self.dma_sems = []
        self.dma_rr = 0


class FW:
    def __init__(self, nc, stack):
        self.nc = nc
        self.sems = {}
        self.E = {}
        for e in ENGS:
            self.sems[f"tl_{e}"] = stack.enter_context(nc.semaphore(f"tl_{e}"))
            self.E[e] = EngState(e)
            for i in range(N_DMA_SEMS[e]):
                k = f"dq_{e}_{i}"
                self.sems[k] = stack.enter_context(nc.semaphore(k))
                self.E[e].dma_sems.append([k, 0])

    def _collect(self, eng, reads, writes):
        deps = {}

        def add(ev):
            if ev is None:
                return
            k, v = ev
            if deps.get(k, 0) < v:
                deps[k] = v
        for b in reads:
            add(b.last_write)
        for b in writes:
            add(b.last_write)
            for r in b.reads:
                add(r)
        st = self.E[eng]
        waits = []
        for k, v in deps.items():
            if eng == "tensor" and k == "tl_tensor":
                continue
            if st.seen.get(k, 0) >= v:
                continue
            st.seen[k] = v
            waits.append((k, v))
        return waits

    def _post(self, ev, reads, writes):
        for b in reads:
            b.reads.append(ev)
            if len(b.reads) > 64:
                b.reads = b.reads[-48:]
        for b in writes:
            b.last_write = ev
            b.reads = []

    def op(self, eng, name, kw, reads=(), writes=(), signal=True, args=()):
        fn = (lambda e, name=name, args=args, kw=kw: getattr(e, name)(*args, **kw))
        st = self.E[eng]
        waits = self._collect(eng, reads, writes)
        if signal:
            st.n += 1
            ev = (f"tl_{eng}", st.n)
        else:
            assert eng == "tensor"
            ev = (f"tl_{eng}", st.n + 1)
        st.queue.append((waits, fn, (f"tl_{eng}", 1) if signal else None))
        self._post(ev, reads, writes)
        return ev

    def dma(self, eng, out, in_, reads=(), writes=(), **kw):
        st = self.E[eng]
        slot = st.dma_sems[st.dma_rr % len(st.dma_sems)]
        st.dma_rr += 1
        k = slot[0]
        waits = self._collect(eng, reads, writes)
        if slot[1] > 0 and st.seen.get(k, 0) < slot[1]:
            st.seen[k] = slot[1]
            waits.append((k, slot[1]))
        slot[1] += 16
        ev = (k, slot[1])
        st.queue.append((waits, (lambda e, o=out, i=in_, kw=kw: e.dma_start(out=o, in_=i, **kw)), (k, 16)))
        self._post(ev, reads, writes)
        return ev

    def barrier(self):
        targets = {}
        for e in ENGS:
            st = self.E[e]
            if st.n > 0:
                targets[f"tl_{e}"] = st.n
            for k, c in st.dma_sems:
                if c > 0:
                    targets[k] = c
        for e in ENGS:
            st = self.E[e]
            waits = []
            for k, v in targets.items():
                if st.seen.get(k, 0) >= v:
                    continue
                st.seen[k] = v
                waits.append((k, v))
            if waits:
                st.queue.append((waits, None, None))

    def flush(self):
        nc = self.nc
        sems = self.sems
        with nc.Block() as block:
            def mk(e):
                st = self.E[e]

                def body(engine):
                    for waits, fn, inc in st.queue:
                        for k, v in waits:
                            engine.wait_ge(sems[k], v)
                        if fn is not None:
                            ins = fn(engine)
                            if inc is not None:
                                ins.then_inc(sems[inc[0]], inc[1])
                    st.queue = []
                return body
            block.tensor(mk("tensor"))
            block.vector(mk("vector"))
            block.scalar(mk("scalar"))
            block.gpsimd(mk("gpsimd"))
            block.sync(mk("sync"))


class Pool:
    def __init__(self, nc, stack, name, shape, dtype, n, psum=False):
        self.items = []
        for i in range(n):
            mk = nc.psum_tensor if psum else nc.sbuf_tensor
            t = stack.enter_context(mk(f"pl_{name}{i}", shape, dtype))
            self.items.append((t, Buf(f"{name}{i}")))
        self.i = 0

    def get(self):
        it = self.items[self.i % len(self.items)]
        self.i += 1
        return it


def build_program(L=SEQ, debug=False):
    assert L % 512 == 0
    NKEY = CTX + L
    NKC = NKEY // 128
    nc = bass.Bass("TRN2", target_bir_lowering=False)

    def din(name, shape):
        return nc.dram_tensor(name, list(shape), F32, kind="ExternalInput").ap()

    x_d = din("x", [L, D]); c_d = din("c", [D]); ctx_d = din("ctx", [CTX, D]); cctx_d = din("c_ctx", [D])
    l0_norm_g = din("l0_norm_g", [D]); l0_ada_w = din("l0_ada_w", [D, 3 * D]); l0_ada_b = din("l0_ada_b", [3 * D])
    l0_w_in = din("l0_w_in", [D, 3 * E]); l0_b_in = din("l0_b_in", [3 * E]); l0_dw_w = din("l0_dw_w", [CW, E])
    l0_dw_b = din("l0_dw_b", [E]); l0_ln_g = din("l0_ln_g", [E]); l0_ln_b = din("l0_ln_b", [E])
    l0_w_out = din("l0_w_out", [E, D]); l0_b_out = din("l0_b_out", [D])
    l1_norm_g = din("l1_norm_g", [D]); l1_ada_w = din("l1_ada_w", [D, 3 * D]); l1_ada_b = din("l1_ada_b", [3 * D])
    l1_w_in = din("l1_w_in", [D, W1C]); l1_q_norm_g = din("l1_q_norm_g", [128]); l1_k_norm_g = din("l1_k_norm_g", [128])
    l1_w_out = din("l1_w_out", [E, D]); final_norm_g = din("final_norm_g", [D])
    ident_d = din("ident", [128, 128]); cos_d = din("rope_cos", [L, 64]); sin_d = din("rope_sin", [L, 64])
    out_d = nc.dram_tensor("out", [L, D], F32, kind="ExternalOutput").ap()

    def dscr(name, shape, dt):
        return nc.dram_tensor(name, list(shape), dt).ap()

    w0in_bf = dscr("w0in_bf", [NCH, 128, NK, 3, 128], BF16)
    w0out_bf = dscr("w0out_bf", [128, NCH, D], BF16)
    diag_bf = dscr("diag_bf", [NCH, 128, CW * 128], BF16)
    w1in_bf = dscr("w1in_bf", [128, NK, W1C], BF16)
    w1out_bf = dscr("w1out_bf", [128, NCH, D], BF16)
    modrow = dscr("modrow", [2, 2, 3 * D], F32)
    if debug:
        x1_d = nc.dram_tensor("x1", [L, D], F32, kind="ExternalOutput").ap()
        ctx1_d = nc.dram_tensor("ctx1", [CTX, D], F32, kind="ExternalOutput").ap()
    else:
        x1_d = dscr("x1", [L, D], F32)
        ctx1_d = dscr("ctx1", [CTX, D], F32)
    qT_d = dscr("qT", [NQH, 128, L], BF16)
    kT_d = dscr("kT", [NKV, 128, NKEY], BF16)
    V_d = dscr("V", [NKEY, KVD], BF16)
    szT_d = dscr("szT", [NQH, 128, L], BF16)
    wT_d = dscr("wT", [NQH, 128, L], BF16)

    with contextlib.ExitStack() as top:
        fw = FW(nc, top)

        def T(stack, name, shape, dt):
            return stack.enter_context(nc.sbuf_tensor("sb_" + name, list(shape), dt))

        ident = T(top, "ident", [128, 128], F32); identb = T(top, "identb", [128, 128], BF16)
        onesb = T(top, "onesb", [128, 128], BF16)
        g0c = T(top, "g0c", [128, NK], F32); g1c = T(top, "g1c", [128, NK], F32)
        binc = T(top, "binc", [128, 3 * NCH], F32); dwT = T(top, "dwT", [128, NCH, CW], BF16)
        binh = T(top, "binh", [128, NCH], F32)
        dwbc = T(top, "dwbc", [128, NCH], F32); lngc = T(top, "lngc", [128, NCH], F32); lnbc = T(top, "lnbc", [128, NCH], F32)
        mod = [T(top, f"mod{l}", [128, 24, 2], F32) for l in range(2)]
        gmul = [T(top, f"gmul{l}", [128, NK, 2], F32) for l in range(2)]
        epsc = T(top, "epsc", [128, 2], F32)
        mhalf = T(top, "mhalf", [128, 512], F32)
        Bc = Buf("consts")
        PSALL = top.enter_context(nc.psum_tensor("psall", [128, 4096], F32))
        PSW = [PSALL[:, i * 1024:(i + 1) * 1024] for i in range(4)]
        PS = [PSALL[:, i * 512:(i + 1) * 512] for i in range(8)]
        PB = [Buf(f"ps{i}") for i in range(8)]

        with contextlib.ExitStack() as ph:
            stage = Pool(nc, ph, "stage", [48, 128], F32, 2)
            dwr = T(ph, "dwr", [CW, E], F32); Bdwr = Buf()
            adaw = T(ph, "adaw", [128, NK, 3 * D], F32); Badaw = Buf()
            craw = T(ph, "craw", [128, 2, NK], F32); condT = T(ph, "condT", [128, 2, NK], F32); Bcond = Buf()
            adabc = T(ph, "adabc", [128, 24], F32)
            adabr = T(ph, "adabr", [2, 3 * D], F32); rowt = T(ph, "rowt", [2, 3 * D], F32); Brow = Buf(); Badabr = Buf()

            fw.dma("sync", ident[:], ident_d, writes=[Bc])
            fw.op("vector", "tensor_copy", dict(out=identb[:], in_=ident[:]), reads=[Bc], writes=[Bc])
            fw.op("vector", "memset", dict(ap=onesb[:], constant=1.0), writes=[Bc])
            fw.op("vector", "memset", dict(ap=mhalf[:, :], constant=-0.5), writes=[Bc])
            fw.op("vector", "memset", dict(ap=epsc[:, 0:1], constant=RMS_EPS), writes=[Bc])
            fw.op("vector", "memset", dict(ap=epsc[:, 1:2], constant=LN_EPS), writes=[Bc])

            def to_cols(vec, n, dst):
                st_t, st_b = stage.get()
                fw.dma("sync", st_t[0:n, :], vec.rearrange("(n p) -> n p", p=128), writes=[st_b])
                fw.op("tensor", "transpose", dict(out=PS[0][:, 0:n], in_=st_t[0:n, :], identity=ident[0:n, 0:n]),
                      reads=[st_b, Bc], writes=[PB[0]])
                fw.op("vector", "tensor_copy", dict(out=dst, in_=PS[0][:, 0:n]), reads=[PB[0]], writes=[Bc])

            to_cols(l0_norm_g, NK, g0c[:]); to_cols(l1_norm_g, NK, g1c[:]); to_cols(l0_b_in, 3 * NCH, binc[:])
            to_cols(l0_dw_b, NCH, dwbc[:]); to_cols(l0_ln_g, NCH, lngc[:]); to_cols(l0_ln_b, NCH, lnbc[:])
            fw.op("vector", "tensor_scalar", dict(out=binh[:], in0=binc[:, NCH:2 * NCH], scalar1=0.5, scalar2=None, op0=ALU.mult), reads=[Bc], writes=[Bc])
            fw.dma("sync", dwr[:], l0_dw_w, writes=[Bdwr])
            for cch in range(NCH):
                fw.op("tensor", "transpose", dict(out=PS[1][:, cch * CW:(cch + 1) * CW], in_=dwr[:, cch * 128:(cch + 1) * 128], identity=ident[0:CW, 0:CW]),
                      reads=[Bdwr, Bc], writes=[PB[1]], signal=(cch == NCH - 1))
            fw.op("vector", "tensor_copy", dict(out=dwT[:].rearrange("p c k -> p (c k)"), in_=PS[1][:, 0:NCH * CW]),
                  reads=[PB[1]], writes=[Bc])
            fw.dma("sync", craw[:, 0, :], c_d.rearrange("(p k) -> p k", k=NK), writes=[Bcond])
            fw.dma("sync", craw[:, 1, :], cctx_d.rearrange("(p k) -> p k", k=NK), writes=[Bcond])
            fw.op("scalar", "activation", dict(out=condT[:], in_=craw[:], func=AF.Silu), reads=[Bcond], writes=[Bcond])
            for l, (aw, ab, gc) in enumerate([(l0_ada_w, l0_ada_b, g0c), (l1_ada_w, l1_ada_b, g1c)]):
                for h in range(2):
                    fw.dma("sync", adaw[:, h * 4:(h + 1) * 4, :], aw.rearrange("(p k) n -> p k n", k=NK)[:, h * 4:(h + 1) * 4, :],
                           writes=[Badaw])
                fw.dma("sync", adabr[:], ab.partition_broadcast(2), writes=[Badabr])
                to_cols(ab, 24, adabc[:])
                for j in range(24):
                    for k in range(NK):
                        fw.op("tensor", "matmul", dict(out=PS[2][:, 2 * j:2 * j + 2], lhsT=adaw[:, k, j * 128:(j + 1) * 128], rhs=condT[:, :, k], start=(k == 0), stop=(k == NK - 1)),
                              reads=[Badaw, Bcond], writes=[PB[2]], signal=(j == 23 and k == NK - 1))
                fw.op("vector", "tensor_tensor", dict(out=mod[l][:], in0=PS[2][:, 0:48].rearrange("p (j c) -> p j c", c=2), in1=adabc[:].unsqueeze(2).to_broadcast([128, 24, 2]), op=ALU.add),
                      reads=[PB[2], Bc], writes=[Bc])
                fw.op("vector", "scalar_tensor_tensor", dict(out=gmul[l][:], in0=mod[l][:, 8:16, :], scalar=1.0, in1=gc[:].unsqueeze(2).to_broadcast([128, NK, 2]), op0=ALU.add, op1=ALU.mult),
                      reads=[Bc], writes=[Bc])
                for n in range(6):
                    pb = 3 + (n % 2)
                    for k in range(NK):
                        fw.op("tensor", "matmul", dict(out=PS[pb][0:2, :], lhsT=condT[:, :, k], rhs=adaw[:, k, n * 512:(n + 1) * 512], start=(k == 0), stop=(k == NK - 1)),
                              reads=[Badaw, Bcond], writes=[PB[pb]], signal=(k == NK - 1))
                    fw.op("vector", "tensor_tensor", dict(out=rowt[:, n * 512:(n + 1) * 512], in0=PS[pb][0:2, :], in1=adabr[:, n * 512:(n + 1) * 512], op=ALU.add),
                          reads=[PB[pb], Badabr], writes=[Brow])
                fw.dma("sync", modrow[l], rowt[:], reads=[Brow])
            fw.barrier()
            fw.flush()

        with contextlib.ExitStack() as ph:
            slab = Pool(nc, ph, "slab", [128, 3 * E], F32, 2)
            slabb = Pool(nc, ph, "slabb", [128, 3 * E], BF16, 2)
            cast_i = [0]

            def cast(dst, src, rb, wb):
                eng = ["vector", "gpsimd", "scalar"][cast_i[0] % 3]
                cast_i[0] += 1
                if eng == "scalar":
                    fw.op(eng, "copy", dict(out=dst, in_=src), reads=[rb], writes=[wb])
                else:
                    fw.op(eng, "tensor_copy", dict(out=dst, in_=src), reads=[rb], writes=[wb])

            for c in range(NCH):
                t, tb = slabb.get()
                fw.op("vector", "tensor_tensor", dict(out=t[:, 0:CW * 128].rearrange("p (k j) -> p k j", j=128),
                                                      in0=identb[:].unsqueeze(1).to_broadcast([128, CW, 128]),
                                                      in1=dwT[:, c, :].unsqueeze(2).to_broadcast([128, CW, 128]), op=ALU.mult),
                      reads=[Bc], writes=[tb])
                fw.dma("gpsimd", diag_bf[c], t[:, 0:CW * 128], reads=[tb])
            for k in range(NK):
                s, sb = slab.get(); t, tb = slabb.get()
                fw.dma("sync", s[:, :], l0_w_in[k * 128:(k + 1) * 128, :], writes=[sb])
                for h in range(3):
                    cast(t[:, h * E:(h + 1) * E], s[:, h * E:(h + 1) * E], sb, tb)
                for tt in range(3):
                    fw.dma("gpsimd", w0in_bf[:, :, k, tt, :].rearrange("c p j -> p c j"),
                           t[:, tt * E:(tt + 1) * E].rearrange("p (c j) -> p c j", j=128), reads=[tb])
            for k in range(NK):
                s, sb = slab.get(); t, tb = slabb.get()
                fw.dma("sync", s[:, 0:W1C], l1_w_in[k * 128:(k + 1) * 128, :], writes=[sb])
                for h in range(2):
                    cast(t[:, h * 2560:(h + 1) * 2560], s[:, h * 2560:(h + 1) * 2560], sb, tb)
                fw.dma("gpsimd", w1in_bf[:, k, :], t[:, 0:W1C], reads=[tb])
            for wsrc, wdst in ((l0_w_out, w0out_bf), (l1_w_out, w1out_bf)):
                for i in range(4):
                    s, sb = slab.get(); t, tb = slabb.get()
                    fw.dma("sync", s[:, 0:4096].rearrange("p (c n) -> p c n", n=D),
                           wsrc.rearrange("(c p) n -> p c n", p=128)[:, 4 * i:4 * i + 4, :], writes=[sb])
                    for h in range(2):
                        cast(t[:, h * 2048:(h + 1) * 2048], s[:, h * 2048:(h + 1) * 2048], sb, tb)
                    fw.dma("gpsimd", wdst[:, 4 * i:4 * i + 4, :], t[:, 0:4096].rearrange("p (c n) -> p c n", n=D), reads=[tb])
            fw.barrier()
            fw.flush()

        with contextlib.ExitStack() as ph:
            w0out = T(ph, "w0out", [128, NCH, D], BF16); Bw0out = Buf()
            gate_t = T(ph, "gate_b", [128, D], F32)
            gbo_t = T(ph, "gbo_b", [128, D], F32)
            gate_b = [gate_t, gate_t]; gbo_b = [gbo_t, gbo_t]
            Bg = Buf()
            fw.dma("sync", w0out[:], w0out_bf, writes=[Bw0out])

            def set_gate(ci):
                fw.dma("sync", gate_t[:], modrow[0, ci, 2 * D:3 * D].partition_broadcast(128), writes=[Bg])
                fw.dma("sync", gbo_t[:], l0_b_out.partition_broadcast(128), writes=[Bg])
                fw.op("vector", "tensor_tensor", dict(out=gbo_t[:], in0=gbo_t[:], in1=gate_t[:], op=ALU.mult), reads=[Bg], writes=[Bg])
            xt_p = Pool(nc, ph, "xt", [128, D], F32, 5)
            ssq = Pool(nc, ph, "ssq", [128, 8], F32, 2)
            hT_p = Pool(nc, ph, "hT", [128, NK, 512 + 2 * HALO], BF16, 2)
            wch_p = Pool(nc, ph, "wch", [128, NK, 3, 128], BF16, 2)
            diag_p = Pool(nc, ph, "diag", [128, CW, 128], BF16, 2)
            sg_p = Pool(nc, ph, "sg", [128, 512 + 2 * HALO], F32, 2)
            v_p = Pool(nc, ph, "v", [128, 512 + 2 * HALO], BF16, 2)
            co = T(ph, "co", [128, NCH, 512], F32); Bco = [Buf() for _ in range(NCH)]
            cob_p = Pool(nc, ph, "cob", [128, 512], BF16, 3)
            sqb_p = Pool(nc, ph, "sqb", [128, 512], BF16, 3)
            sz = T(ph, "sz", [128, NCH, 512], BF16); Bsz = [Buf() for _ in range(NCH)]
            wg = T(ph, "wg", [128, NCH, 512], BF16); Bwg = [Buf() for _ in range(NCH)]
            mean = T(ph, "mean", [128, 512], F32); rstd = T(ph, "rstd", [128, 512], F32); Bst = Buf()
            tmp_p = Pool(nc, ph, "tmpn", [128, 512], F32, 2)
            sil_p = Pool(nc, ph, "sil", [128, 512], BF16, 5)
            xr_p = Pool(nc, ph, "xr", [128, D], F32, 1)
            xo_p = Pool(nc, ph, "xo", [128, D], F32, 1)

            cur_gate = [None]

            def l0_block(src, dst, s0, NT, ci, total):
                left_pad = (s0 == 0)
                right_pad = (s0 + NT == total)
                ntile = NT // 128
                nt1 = ntile + 1
                WT = NT + 2 * HALO
                S = {}
                hT, BhT = hT_p.get()

                def front_load():
                    xts = []
                    for j in range(nt1):
                        xt, Bxt = xt_p.get()
                        xts.append((xt, Bxt))
                        if j < ntile:
                            fw.dma("sync", xt[:, :], src[s0 + j * 128:s0 + (j + 1) * 128, :], writes=[Bxt])
                        else:
                            fw.op("gpsimd", "memset", dict(ap=xt[0:64, :], constant=0.0), writes=[Bxt])
                            if not left_pad:
                                fw.dma("sync", xt[0:HALO, :], src[s0 - HALO:s0, :], writes=[Bxt])
                            if not right_pad:
                                fw.dma("sync", xt[32:32 + HALO, :], src[s0 + NT:s0 + NT + HALO, :], writes=[Bxt])
                    S["xts"] = xts

                def front_pre():
                    ss, Bss = ssq.get()
                    S["ss"], S["Bss"] = ss, Bss
                    fw.op("vector", "memset", dict(ap=ss[:, :], constant=0.0), writes=[Bss])
                    xts = S["xts"]
                    for j in range(nt1):
                        xt, Bxt = xts[j]
                        nr = 128 if j < ntile else 64
                        jt, Bjt = tmp_p.get()
                        fw.op("scalar", "activation", dict(out=jt[0:nr, :].bitcast(BF16), in_=xt[0:nr, :], func=AF.Square, accum_out=ss[0:nr, j:j + 1]),
                              reads=[Bxt], writes=[Bjt, Bss])
                    fw.op("vector", "tensor_scalar", dict(out=ss[:, 0:nt1], in0=ss[:, 0:nt1], scalar1=1.0 / D, scalar2=RMS_EPS, op0=ALU.mult, op1=ALU.add),
                          reads=[Bss], writes=[Bss])
                    fw.op("gpsimd", "tensor_tensor", dict(out=ss[:, 0:nt1], in0=ss[:, 0:nt1], in1=mhalf[:, 0:nt1], op=ALU.pow),
                          reads=[Bss, Bc], writes=[Bss])
                    for j in range(nt1):
                        xt, Bxt = xts[j]
                        nr = 128 if j < ntile else 64
                        fw.op("vector", "tensor_scalar", dict(out=xt[0:nr, :], in0=xt[0:nr, :], scalar1=ss[0:nr, j:j + 1], scalar2=None, op0=ALU.mult),
                              reads=[Bxt, Bss], writes=[Bxt])
                    S["xts"] = xts

                def front_tr():
                    xts = S["xts"]
                    for j in range(nt1):
                        xt, Bxt = xts[j]
                        nr = 128 if j < ntile else 64
                        for half in range(2):
                            pb = (2 * j + half) % 4
                            for kk in range(4):
                                k = half * 4 + kk
                                fw.op("tensor", "transpose", dict(out=PS[pb][:, kk * 128:kk * 128 + nr], in_=xt[0:nr, k * 128:(k + 1) * 128], identity=ident[0:nr, 0:nr]),
                                      reads=[Bxt, Bc], writes=[PB[pb]], signal=(kk == 3))
                            for kk in range(4):
                                k = half * 4 + kk
                                eng = "scalar" if kk % 2 == 0 else "vector"
                                pieces = ([(hT[:, k, HALO + j * 128:HALO + (j + 1) * 128], PS[pb][:, kk * 128:(kk + 1) * 128])] if j < ntile else
                                          [(hT[:, k, 0:HALO], PS[pb][:, kk * 128:kk * 128 + HALO]),
                                           (hT[:, k, HALO + NT:HALO + NT + HALO], PS[pb][:, kk * 128 + 32:kk * 128 + 32 + HALO])])
                                for (o_, i_) in pieces:
                                    if eng == "scalar":
                                        fw.op("scalar", "activation", dict(out=o_, in_=i_, func=AF.Identity, scale=gmul[0][:, k, ci:ci + 1], bias=mod[0][:, k, ci:ci + 1]),
                                              reads=[PB[pb], Bc], writes=[BhT])
                                    else:
                                        fw.op("vector", "tensor_scalar", dict(out=o_, in0=i_, scalar1=gmul[0][:, k, ci:ci + 1], scalar2=mod[0][:, k, ci:ci + 1],
                                                                              op0=ALU.mult, op1=ALU.add),
                                              reads=[PB[pb], Bc], writes=[BhT])

                def p_stage(c):
                    wch, Bwch = wch_p.get()
                    fw.dma("sync", wch[:].rearrange("p k t j -> p (k t j)"), w0in_bf[c].rearrange("p k t j -> p (k t j)"), writes=[Bwch])
                    dg, Bdg = diag_p.get()
                    fw.dma("sync", dg[:].rearrange("p k j -> p (k j)"), diag_bf[c], writes=[Bdg])
                    sg, Bsg = sg_p.get(); v, Bv = v_p.get()
                    w1 = min(WT, 512)
                    rem = WT - w1
                    for k in range(NK):
                        fw.op("tensor", "matmul", dict(out=PS[2][:, 0:w1], lhsT=wch[:, k, 1, :], rhs=hT[:, k, 0:w1], start=(k == 0), stop=(k == NK - 1)),
                              reads=[Bwch, BhT], writes=[PB[2]], signal=(k == NK - 1))
                    if rem > 0:
                        for k in range(NK):
                            fw.op("tensor", "matmul", dict(out=PS[4][:, 0:rem], lhsT=wch[:, k, 1, :], rhs=hT[:, k, 512:WT], start=(k == 0), stop=(k == NK - 1)),
                                  reads=[Bwch, BhT], writes=[PB[4]], signal=(k == NK - 1))
                    fw.op("scalar", "activation", dict(out=sg[:, 0:w1], in_=PS[2][:, 0:w1], func=AF.Tanh, scale=0.5, bias=binh[:, c:c + 1]),
                          reads=[PB[2], Bc], writes=[Bsg])
                    if rem > 0:
                        fw.op("scalar", "activation", dict(out=sg[:, 512:WT], in_=PS[4][:, 0:rem], func=AF.Tanh, scale=0.5, bias=binh[:, c:c + 1]),
                              reads=[PB[4], Bc], writes=[Bsg])
                    fw.op("vector", "tensor_scalar", dict(out=sg[:, 0:WT], in0=sg[:, 0:WT], scalar1=0.5, scalar2=0.5, op0=ALU.mult, op1=ALU.add),
                          reads=[Bsg], writes=[Bsg])
                    for k in range(NK):
                        fw.op("tensor", "matmul", dict(out=PS[3][:, 0:w1], lhsT=wch[:, k, 0, :], rhs=hT[:, k, 0:w1], start=(k == 0), stop=(k == NK - 1)),
                              reads=[Bwch, BhT], writes=[PB[3]], signal=(k == NK - 1))
                    if rem > 0:
                        for k in range(NK):
                            fw.op("tensor", "matmul", dict(out=PS[4][:, 64:64 + rem], lhsT=wch[:, k, 0, :], rhs=hT[:, k, 512:WT], start=(k == 0), stop=(k == NK - 1)),
                                  reads=[Bwch, BhT], writes=[PB[4]], signal=(k == NK - 1))
                    fw.op("vector", "scalar_tensor_tensor", dict(out=v[:, 0:w1], in0=PS[3][:, 0:w1], scalar=binc[:, c:c + 1], in1=sg[:, 0:w1], op0=ALU.add, op1=ALU.mult),
                          reads=[PB[3], Bsg, Bc], writes=[Bv])
                    if rem > 0:
                        fw.op("vector", "scalar_tensor_tensor", dict(out=v[:, 512:WT], in0=PS[4][:, 64:64 + rem], scalar=binc[:, c:c + 1], in1=sg[:, 512:WT], op0=ALU.add, op1=ALU.mult),
                              reads=[PB[4], Bsg, Bc], writes=[Bv])
                    if left_pad:
                        fw.op("vector", "memset", dict(ap=v[:, 0:HALO], constant=0.0), writes=[Bv])
                    if right_pad:
                        fw.op("vector", "memset", dict(ap=v[:, HALO + NT:WT], constant=0.0), writes=[Bv])
                    for k in range(NK):
                        fw.op("tensor", "matmul", dict(out=PS[c % 2][:, 0:NT], lhsT=wch[:, k, 2, :], rhs=hT[:, k, HALO:HALO + NT], start=(k == 0), stop=(k == NK - 1)),
                              reads=[Bwch, BhT], writes=[PB[c % 2]], signal=(k == NK - 1))
                    fw.op("scalar", "activation", dict(out=sz[:, c, 0:NT], in_=PS[c % 2][:, 0:NT], func=AF.Silu, bias=binc[:, 2 * NCH + c:2 * NCH + c + 1]),
                          reads=[PB[c % 2], Bc], writes=[Bsz[c]])
                    return (v, Bv, dg, Bdg)

                def c_stage(c, st):
                    v, Bv, dg, Bdg = st
                    for k in range(CW):
                        fw.op("tensor", "matmul", dict(out=PS[5][:, 0:NT], lhsT=dg[:, k, :], rhs=v[:, k:k + NT], start=(k == 0), stop=(k == CW - 1)),
                              reads=[Bdg, Bv], writes=[PB[5]], signal=(k == CW - 1))
                    fw.op("scalar", "activation", dict(out=co[:, c, 0:NT], in_=PS[5][:, 0:NT], func=AF.Identity, bias=dwbc[:, c:c + 1]),
                          reads=[PB[5], Bc], writes=[Bco[c]])
                    cob, Bcob = cob_p.get(); sqb, Bsqb = sqb_p.get()
                    fw.op("vector", "tensor_copy", dict(out=cob[:, 0:NT], in_=co[:, c, 0:NT]), reads=[Bco[c]], writes=[Bcob])
                    fw.op("gpsimd", "tensor_tensor", dict(out=sqb[:, 0:NT], in0=co[:, c, 0:NT], in1=co[:, c, 0:NT], op=ALU.mult),
                          reads=[Bco[c]], writes=[Bsqb])
                    S["pend_stats"] = (c, cob, Bcob, sqb, Bsqb)

                def stats_mm():
                    if S.get("pend_stats") is None:
                        return
                    c, cob, Bcob, sqb, Bsqb = S["pend_stats"]
                    S["pend_stats"] = None
                    fw.op("tensor", "matmul", dict(out=PS[6][:, 0:NT], lhsT=onesb[:], rhs=cob[:, 0:NT], start=(c == 0), stop=(c == NCH - 1)),
                          reads=[Bcob, Bc], writes=[PB[6]])
                    fw.op("tensor", "matmul", dict(out=PS[7][:, 0:NT], lhsT=onesb[:], rhs=sqb[:, 0:NT], start=(c == 0), stop=(c == NCH - 1)),
                          reads=[Bsqb, Bc], writes=[PB[7]])

                S["prev"] = None

                def mid(c0, c1):
                    for c in range(c0, c1):
                        cur = p_stage(c)
                        stats_mm()
                        if S["prev"] is not None:
                            c_stage(c - 1, S["prev"])
                        S["prev"] = cur
                    if c1 == NCH:
                        stats_mm()
                        c_stage(NCH - 1, S["prev"])
                        stats_mm()

                def stats_n():
                    fw.op("vector", "tensor_scalar", dict(out=mean[:, 0:NT], in0=PS[6][:, 0:NT], scalar1=1.0 / E, scalar2=None, op0=ALU.mult),
                          reads=[PB[6]], writes=[Bst])
                    fw.op("vector", "tensor_tensor", dict(out=rstd[:, 0:NT], in0=mean[:, 0:NT], in1=mean[:, 0:NT], op=ALU.mult),
                          reads=[Bst], writes=[Bst])
                    fw.op("vector", "scalar_tensor_tensor", dict(out=rstd[:, 0:NT], in0=PS[7][:, 0:NT], scalar=1.0 / E, in1=rstd[:, 0:NT], op0=ALU.mult, op1=ALU.subtract),
                          reads=[PB[7], Bst], writes=[Bst])
                    fw.op("scalar", "activation", dict(out=rstd[:, 0:NT], in_=rstd[:, 0:NT], func=AF.Sqrt, bias=epsc[:, 1:2]),
                          reads=[Bst, Bc], writes=[Bst])
                    fw.op("vector", "reciprocal", dict(out=rstd[:, 0:NT], in_=rstd[:, 0:NT]), reads=[Bst], writes=[Bst])

                def n_pre(c0, c1):
                    for c in range(c0, c1):
                        fw.op("vector", "tensor_tensor", dict(out=co[:, c, 0:NT], in0=co[:, c, 0:NT], in1=mean[:, 0:NT], op=ALU.subtract),
                              reads=[Bst], writes=[Bco[c]])
                        fw.op("vector", "tensor_tensor", dict(out=co[:, c, 0:NT], in0=co[:, c, 0:NT], in1=rstd[:, 0:NT], op=ALU.mult),
                              reads=[Bst], writes=[Bco[c]])

                def n_post(c0, c1):
                    for c in range(c0, c1):
                        sl, Bsl = sil_p.get()
                        fw.op("scalar", "activation", dict(out=sl[:, 0:NT], in_=co[:, c, 0:NT], func=AF.Silu, scale=lngc[:, c:c + 1], bias=lnbc[:, c:c + 1]),
                              reads=[Bco[c], Bc], writes=[Bsl])
                        fw.op("gpsimd", "tensor_tensor", dict(out=wg[:, c, 0:NT], in0=sl[:, 0:NT], in1=sz[:, c, 0:NT], op=ALU.mult),
                              reads=[Bsl, Bsz[c]], writes=[Bwg[c]])

                def o_stage():
                    if cur_gate[0] != ci:
                        set_gate(ci)
                        cur_gate[0] = ci
                    for j in range(ntile):
                        xr, Bxr = xr_p.get(); xo, Bxo = xo_p.get()
                        fw.dma("gpsimd", xr[:, :], src[s0 + j * 128:s0 + (j + 1) * 128, :], writes=[Bxr])
                        fw.op("gpsimd", "tensor_tensor", dict(out=xr[:, :], in0=xr[:, :], in1=gbo_b[ci][:], op=ALU.add),
                              reads=[Bxr, Bg], writes=[Bxr])
                        for half in range(2):
                            pb = half
                            for c in range(NCH):
                                fw.op("tensor", "matmul", dict(out=PS[pb][:, :], lhsT=wg[:, c, j * 128:(j + 1) * 128], rhs=w0out[:, c, half * 512:(half + 1) * 512], start=(c == 0), stop=(c == NCH - 1)),
                                    reads=[Bwg[c], Bw0out], writes=[PB[pb]], signal=(c == NCH - 1))
                            fw.op("vector", "tensor_tensor", dict(out=xo[:, half * 512:(half + 1) * 512], in0=PS[pb][:, :], in1=gate_b[ci][:, half * 512:(half + 1) * 512], op=ALU.mult),
                                reads=[PB[pb], Bg], writes=[Bxo])
                        fw.op("vector", "tensor_tensor", dict(out=xo[:, :], in0=xo[:, :], in1=xr[:, :], op=ALU.add),
                              reads=[Bxo, Bxr], writes=[Bxo])
                        fw.dma("gpsimd", dst[s0 + j * 128:s0 + (j + 1) * 128, :], xo[:, :], reads=[Bxo])

                return dict(front_load=front_load, front_pre=front_pre, front_tr=front_tr, mid=mid, stats_n=stats_n, n_pre=n_pre, n_post=n_post, o_stage=o_stage)

            specs = [(ctx_d, ctx1_d, 0, CTX, 1, CTX)] + [(x_d, x1_d, blk * 512, 512, 0, L) for blk in range(L // 512)]
            blk_objs = {}

            def getb(bi):
                if bi not in blk_objs:
                    blk_objs[bi] = l0_block(*specs[bi])
                return blk_objs[bi]
            nb = len(specs)
            A = getb(0)
            A["front_load"](); A["front_pre"](); A["front_tr"](); A["mid"](0, 4)
            if nb > 1:
                getb(1)["front_load"]()
            A["mid"](4, 12)
            for bi in range(1, nb):
                A = getb(bi - 1); Bk = getb(bi)
                Bk["front_pre"]()
                A["mid"](12, NCH)
                Bk["front_tr"]()
                if bi + 1 < nb:
                    getb(bi + 1)["front_load"]()
                A["stats_n"]()
                A["n_pre"](0, 2)
                for i in range(8):
                    A["n_post"](2 * i, 2 * i + 2)
                    Bk["mid"](i, i + 1)
                    if i < 7:
                        A["n_pre"](2 * i + 2, 2 * i + 4)
                A["o_stage"]()
                Bk["mid"](8, 12)
            A = getb(nb - 1)
            A["mid"](12, NCH); A["stats_n"](); A["n_pre"](0, NCH); A["n_post"](0, NCH); A["o_stage"]()
            fw.barrier()
            fw.flush()

        if debug == "l0":
            with contextlib.ExitStack() as ph:
                t = T(ph, "dbg", [128, D], F32); Bt = Buf()
                for j in range(L // 128):
                    fw.dma("sync", t[:], x1_d[j * 128:(j + 1) * 128, :], reads=[], writes=[Bt])
                    fw.dma("sync", out_d[j * 128:(j + 1) * 128, :], t[:], reads=[Bt])
                fw.barrier()
                fw.flush()
            return nc

        with contextlib.ExitStack() as ph:
            w1in = T(ph, "w1in", [128, NK, 3072], BF16); Bw1 = Buf()
            fw.dma("sync", w1in[:, 0:4, :], w1in_bf[:, 0:4, 0:3072], writes=[Bw1])
            fw.dma("sync", w1in[:, 4:8, :], w1in_bf[:, 4:8, 0:3072], writes=[Bw1])
            wz_p = Pool(nc, ph, "wz", [128, NK, 128], BF16, 3)
            gqk = T(ph, "gqk", [128, 20, 128], F32); Bgqk = Buf()
            fw.dma("sync", gqk[:, 0, :], l1_q_norm_g.partition_broadcast(128), writes=[Bgqk])
            fw.dma("sync", gqk[:, 16, :], l1_k_norm_g.partition_broadcast(128), writes=[Bgqk])
            fw.op("vector", "tensor_copy", dict(out=gqk[:, 1:16, :], in_=gqk[:, 0:1, :].to_broadcast([128, 15, 128])),
                  reads=[Bgqk], writes=[Bgqk])
            fw.op("vector", "tensor_copy", dict(out=gqk[:, 17:20, :], in_=gqk[:, 16:17, :].to_broadcast([128, 3, 128])),
                  reads=[Bgqk], writes=[Bgqk])
            xt_p = Pool(nc, ph, "xt2", [128, D], F32, 8)
            junk = T(ph, "junk2", [128, D], BF16); Bjunk = Buf()
            ssq = Pool(nc, ph, "ssq2", [128, 4], F32, 3)
            hT_p = Pool(nc, ph, "hT2", [128, NK, 512], BF16, 2)
            qk_p = Pool(nc, ph, "qk", [128, 20, 128], F32, 2)
            scr_p = Pool(nc, ph, "scr", [128, 20, 128], F32, 2)
            hs_p = Pool(nc, ph, "hs", [128, 20], F32, 3)
            cs_p = Pool(nc, ph, "cs", [128, 2, 64], F32, 3)
            qr_p = Pool(nc, ph, "qr", [128, 20, 128], BF16, 2)
            qst_p = Pool(nc, ph, "qst", [128, 20, 128], BF16, 3)
            vst_p = Pool(nc, ph, "vst", [128, KVD], BF16, 3)
            szst_p = Pool(nc, ph, "szst", [128, 4, 512], BF16, 3)

            def x_pre(src, s0, NT):
                ntile = NT // 128
                ss, Bss = ssq.get()
                fw.op("vector", "memset", dict(ap=ss[:, :], constant=0.0), writes=[Bss])
                xts = []
                for j in range(ntile):
                    xt, Bxt = xt_p.get(); xts.append((xt, Bxt))
                    fw.dma("sync", xt[:, :], src[s0 + j * 128:s0 + (j + 1) * 128, :], writes=[Bxt])
                    fw.op("scalar", "activation", dict(out=junk[:, :], in_=xt[:, :], func=AF.Square, accum_out=ss[:, j:j + 1]),
                          reads=[Bxt], writes=[Bjunk, Bss])
                fw.op("vector", "tensor_scalar", dict(out=ss[:, 0:ntile], in0=ss[:, 0:ntile], scalar1=1.0 / D, scalar2=RMS_EPS, op0=ALU.mult, op1=ALU.add),
                      reads=[Bss], writes=[Bss])
                fw.op("gpsimd", "tensor_tensor", dict(out=ss[:, 0:ntile], in0=ss[:, 0:ntile], in1=mhalf[:, 0:ntile], op=ALU.pow),
                      reads=[Bss, Bc], writes=[Bss])
                for j in range(ntile):
                    xt, Bxt = xts[j]
                    fw.op("vector", "tensor_scalar", dict(out=xt[:, :], in0=xt[:, :], scalar1=ss[:, j:j + 1], scalar2=None, op0=ALU.mult),
                          reads=[Bxt, Bss], writes=[Bxt])
                return xts

            def x_tr(xtb, j, hT, BhT, ci):
                xt, Bxt = xtb
                for half in range(2):
                    pb = half
                    for kk in range(4):
                        k = half * 4 + kk
                        fw.op("tensor", "transpose", dict(out=PS[pb][:, kk * 128:(kk + 1) * 128], in_=xt[:, k * 128:(k + 1) * 128], identity=ident[:, :]),
                              reads=[Bxt, Bc], writes=[PB[pb]], signal=(kk == 3))
                    for kk in range(4):
                        k = half * 4 + kk
                        if True:
                            fw.op("scalar", "activation", dict(out=hT[:, k, j * 128:(j + 1) * 128], in_=PS[pb][:, kk * 128:(kk + 1) * 128],
                                                               func=AF.Identity, scale=gmul[1][:, k, ci:ci + 1], bias=mod[1][:, k, ci:ci + 1]),
                                  reads=[PB[pb], Bc], writes=[BhT])
                        else:
                            fw.op("vector", "tensor_scalar", dict(out=hT[:, k, j * 128:(j + 1) * 128], in0=PS[pb][:, kk * 128:(kk + 1) * 128],
                                                                  scalar1=gmul[1][:, k, ci:ci + 1], scalar2=mod[1][:, k, ci:ci + 1],
                                                                  op0=ALU.mult, op1=ALU.add),
                                  reads=[PB[pb], Bc], writes=[BhT])

            def s1_stage(tl):
                hT, BhT, j, is_ctx = tl["hT"], tl["BhT"], tl["j"], tl["is_ctx"]
                nh0 = 16 if is_ctx else 0
                qk, Bqk = qk_p.get(); scr, Bscr = scr_p.get(); vst, Bvst = vst_p.get()
                tl.update(qk=qk, Bqk=Bqk, scr=scr, Bscr=Bscr, vst=vst, Bvst=Bvst, nh0=nh0)
                groups = ([] if is_ctx else [0, 1, 2, 3]) + [4, 5]
                for gi, g in enumerate(groups):
                    pb = 2 + (gi % 2)
                    for k in range(NK):
                        fw.op("tensor", "matmul", dict(out=PS[pb][:, :], lhsT=hT[:, k, j * 128:(j + 1) * 128], rhs=w1in[:, k, g * 512:(g + 1) * 512],
                                                       start=(k == 0), stop=(k == NK - 1)),
                              reads=[BhT, Bw1], writes=[PB[pb]], signal=(k == NK - 1))
                    if g < 5:
                        fw.op("scalar", "copy", dict(out=qk[:, g * 4:(g + 1) * 4, :].rearrange("p h d -> p (h d)"), in_=PS[pb][:, :]),
                              reads=[PB[pb]], writes=[Bqk])
                        fw.op("scalar", "activation", dict(out=scr[:, g * 4:(g + 1) * 4, :].rearrange("p h d -> p (h d)"), in_=PS[pb][:, :], func=AF.Square),
                              reads=[PB[pb]], writes=[Bscr])
                    else:
                        fw.op("vector", "tensor_copy", dict(out=vst[:, :], in_=PS[pb][:, :]), reads=[PB[pb]], writes=[Bvst])

            def s2_stage(tl):
                qk, Bqk, scr, Bscr, nh0, is_ctx = tl["qk"], tl["Bqk"], tl["scr"], tl["Bscr"], tl["nh0"], tl["is_ctx"]
                nh = 20 - nh0
                hs, Bhs = hs_p.get()
                fw.op("vector", "tensor_reduce", dict(out=hs[:, nh0:20], in_=scr[:, nh0:20, :], axis=AX.X, op=ALU.add),
                      reads=[Bscr], writes=[Bhs])
                fw.op("vector", "tensor_scalar", dict(out=hs[:, nh0:20], in0=hs[:, nh0:20], scalar1=1.0 / 128, scalar2=RMS_EPS, op0=ALU.mult, op1=ALU.add),
                      reads=[Bhs], writes=[Bhs])
                fw.op("scalar", "activation", dict(out=hs[:, nh0:20], in_=hs[:, nh0:20], func=AF.Sqrt), reads=[Bhs], writes=[Bhs])
                fw.op("vector", "reciprocal", dict(out=hs[:, nh0:20], in_=hs[:, nh0:20]), reads=[Bhs], writes=[Bhs])
                fw.op("vector", "tensor_tensor", dict(out=qk[:, nh0:20, :], in0=qk[:, nh0:20, :],
                                                      in1=hs[:, nh0:20].unsqueeze(2).to_broadcast([128, nh, 128]), op=ALU.mult),
                      reads=[Bqk, Bhs], writes=[Bqk])
                qr, Bqr = qr_p.get()
                tl.update(qr=qr, Bqr=Bqr)
                if is_ctx:
                    fw.op("vector", "tensor_tensor", dict(out=qr[:, nh0:20, :], in0=qk[:, nh0:20, :], in1=gqk[:, nh0:20, :], op=ALU.mult),
                          reads=[Bqk, Bgqk], writes=[Bqr])
                    return
                fw.op("vector", "tensor_tensor", dict(out=qk[:, :, :], in0=qk[:, :, :], in1=gqk[:, :, :], op=ALU.mult),
                      reads=[Bqk, Bgqk], writes=[Bqk])
                cs, Bcs = cs_p.get()
                t0 = tl["t0"]
                fw.dma("sync", cs[:, 0, :], cos_d[t0:t0 + 128, :], writes=[Bcs])
                fw.dma("sync", cs[:, 1, :], sin_d[t0:t0 + 128, :], writes=[Bcs])
                qv = qk[:, :, :].rearrange("p h (a b i) -> p h a b i", a=2, b=2)
                qo = qr[:, :, :].rearrange("p h (a b i) -> p h a b i", a=2, b=2)
                x1v = qv[:, :, :, 0, :]; x2v = qv[:, :, :, 1, :]
                Cb = cs[:, 0, :].rearrange("p (a i) -> p a i", a=2).unsqueeze(1).to_broadcast([128, 20, 2, 32])
                Sb = cs[:, 1, :].rearrange("p (a i) -> p a i", a=2).unsqueeze(1).to_broadcast([128, 20, 2, 32])
                t1 = scr[:, 0:10, :].rearrange("p h (a i) -> p (h a) i", a=4).rearrange("p (h a) i -> p h a i", a=2)
                t2 = scr[:, 10:20, :].rearrange("p h (a i) -> p (h a) i", a=4).rearrange("p (h a) i -> p h a i", a=2)
                fw.op("vector", "tensor_tensor", dict(out=t1, in0=x1v, in1=Cb, op=ALU.mult), reads=[Bqk, Bcs], writes=[Bscr])
                fw.op("vector", "tensor_tensor", dict(out=t2, in0=x2v, in1=Sb, op=ALU.mult), reads=[Bqk, Bcs], writes=[Bscr])
                fw.op("vector", "tensor_tensor", dict(out=qo[:, :, :, 0, :], in0=t1, in1=t2, op=ALU.subtract), reads=[Bscr], writes=[Bqr])
                fw.op("vector", "tensor_tensor", dict(out=t1, in0=x1v, in1=Sb, op=ALU.mult), reads=[Bqk, Bcs], writes=[Bscr])
                fw.op("vector", "tensor_tensor", dict(out=t2, in0=x2v, in1=Cb, op=ALU.mult), reads=[Bqk, Bcs], writes=[Bscr])
                fw.op("vector", "tensor_tensor", dict(out=qo[:, :, :, 1, :], in0=t1, in1=t2, op=ALU.add), reads=[Bscr], writes=[Bqr])

            def s3_stage(tl):
                qr, Bqr, nh0, is_ctx = tl["qr"], tl["Bqr"], tl["nh0"], tl["is_ctx"]
                qst, Bqst = qst_p.get()
                hgroups = [(16, 20, 4)] if is_ctx else [(0, 8, 4), (8, 16, 5), (16, 20, 4)]
                for gi, (h0, h1, pb) in enumerate(hgroups):
                    ptv = PS[pb].bitcast(BF16)
                    for h in range(h0, h1):
                        fw.op("tensor", "transpose", dict(out=ptv[:, (h - h0) * 128:(h - h0 + 1) * 128], in_=qr[:, h, :], identity=identb[:, :]),
                              reads=[Bqr, Bc], writes=[PB[pb]], signal=(h == h1 - 1))
                    nw = (h1 - h0) * 128
                    fw.op("scalar", "copy", dict(out=qst[:, h0:h1, :].rearrange("p h t -> p (h t)"), in_=ptv[:, 0:nw]),
                          reads=[PB[pb]], writes=[Bqst])
                t0, key0 = tl["t0"], tl["key0"]
                if not is_ctx:
                    fw.dma("gpsimd", qT_d[:, :, t0:t0 + 128].rearrange("h p t -> p h t"), qst[:, 0:16, :], reads=[Bqst])
                fw.dma("gpsimd", kT_d[:, :, key0:key0 + 128].rearrange("h p t -> p h t"), qst[:, 16:20, :], reads=[Bqst])
                fw.dma("gpsimd", V_d[key0:key0 + 128, :], tl["vst"][:, :], reads=[tl["Bvst"]])

            def z_stage(hT, BhT, s0):
                for c4 in range(4):
                    szst, Bszst = szst_p.get()
                    for cc in range(4):
                        c = c4 * 4 + cc
                        pb = 6 + (c % 2)
                        wz, Bwz = wz_p.get()
                        fw.dma("sync", wz[:, :, :], w1in_bf[:, :, 3072 + c * 128:3072 + (c + 1) * 128], writes=[Bwz])
                        for k in range(NK):
                            fw.op("tensor", "matmul", dict(out=PS[pb][:, :], lhsT=wz[:, k, :], rhs=hT[:, k, :], start=(k == 0), stop=(k == NK - 1)),
                                  reads=[BhT, Bwz], writes=[PB[pb]], signal=(k == NK - 1))
                        fw.op("scalar", "activation", dict(out=szst[:, cc, :], in_=PS[pb][:, :], func=AF.Silu),
                              reads=[PB[pb]], writes=[Bszst])
                    fw.dma("gpsimd", szT_d[c4 * 4:(c4 + 1) * 4, :, s0:s0 + 512].rearrange("c p t -> p c t"), szst[:, :, :], reads=[Bszst])

            blocks = [(ctx1_d, 0, CTX, 1, True, 0)] + [(x1_d, blk * 512, 512, 0, False, CTX + blk * 512) for blk in range(L // 512)]
            tiles = []
            for bi, (src, s0, NT, ci, is_ctx, key0) in enumerate(blocks):
                for j in range(NT // 128):
                    tiles.append(dict(bi=bi, j=j, first=(j == 0), last=(j == NT // 128 - 1), is_ctx=is_ctx,
                                      t0=s0 + j * 128, key0=key0 + j * 128))
            hTs = {}
            for bi in (0, 1):
                src, s0, NT, ci, is_ctx, key0 = blocks[bi]
                xts = x_pre(src, s0, NT)
                hTs[bi] = hT_p.get()
                for j in range(NT // 128):
                    x_tr(xts[j], j, hTs[bi][0], hTs[bi][1], ci)
            nxt = None
            for gidx in range(len(tiles) + 2):
                if gidx < len(tiles):
                    tl = tiles[gidx]
                    bi = tl["bi"]
                    src, s0, NT, ci, is_ctx, key0 = blocks[bi]
                    if tl["first"] and bi >= 1 and bi + 1 < len(blocks):
                        nsrc, ns0, nNT, nci, _, _ = blocks[bi + 1]
                        nxt = (x_pre(nsrc, ns0, nNT), nci)
                        hTs[bi + 1] = hT_p.get()
                    tl["hT"], tl["BhT"] = hTs[bi]
                    if gidx >= 1:
                        s2_stage(tiles[gidx - 1])
                    s1_stage(tl)
                    if gidx >= 2:
                        s3_stage(tiles[gidx - 2])
                    if bi >= 1 and bi + 1 < len(blocks):
                        x_tr(nxt[0][tl["j"]], tl["j"], hTs[bi + 1][0], hTs[bi + 1][1], nxt[1])
                else:
                    if gidx == len(tiles):
                        s2_stage(tiles[gidx - 1])
                    s3_stage(tiles[gidx - 2])
                if gidx < len(tiles) and tiles[gidx]["last"] and not tiles[gidx]["is_ctx"]:
                    bi = tiles[gidx]["bi"]
                    z_stage(hTs[bi][0], hTs[bi][1], blocks[bi][1])
            fw.barrier()
            fw.flush()

        with contextlib.ExitStack() as ph:
            kT_p = Pool(nc, ph, "kTh", [128, NKEY], BF16, 2)
            V_p = Pool(nc, ph, "Vh", [128, NKC, 128], BF16, 2)
            q_p = Pool(nc, ph, "qblk", [128, 512], BF16, 3)
            szb_p = Pool(nc, ph, "szblk", [128, 512], BF16, 3)
            p_p = Pool(nc, ph, "pT", [128, 1024], BF16, 8)
            accA_p = Pool(nc, ph, "accA", [128, 1024], BF16, 2)
            accB_p = Pool(nc, ph, "accB", [128, 1024], BF16, 2)
            rden_p = Pool(nc, ph, "rden", [128, 512], F32, 2)
            o_p = Pool(nc, ph, "osb", [128, 512], F32, 2)
            w_p = Pool(nc, ph, "wsb", [128, 512], BF16, 3)
            scale = 128.0 ** -0.5
            NQB = L // 512
            NT2 = NKC // 2
            assert NKC % 2 == 0 and NT2 >= 3
            SW = [0, 1, 2]
            BSW = [Buf(), Buf(), Buf()]
            srr = [0]

            def s_get():
                i = srr[0] % 3
                srr[0] += 1
                return PSW[SW[i]], BSW[i]
            LAG = 2

            def make_unit(kTh, BkT, Vh, BV, h, qb, po):
                qblk, Bq = q_p.get(); szb, Bszb = szb_p.get()
                fw.dma("sync", qblk[:, :], qT_d[h, :, qb * 512:(qb + 1) * 512], writes=[Bq])
                fw.dma("sync", szb[:, :], szT_d[h, :, qb * 512:(qb + 1) * 512], writes=[Bszb])
                accA, BaA = accA_p.get()
                pts = {}

                def s_stage(t):
                    st_, Bst_ = s_get()
                    for i in range(2):
                        kc = 2 * t + i
                        fw.op("tensor", "matmul", dict(out=st_[:, i * 512:(i + 1) * 512], lhsT=kTh[:, kc * 128:(kc + 1) * 128], rhs=qblk[:, :],
                                                       start=True, stop=True),
                              reads=[BkT, Bq], writes=[Bst_], signal=(i == 1))
                    pT, BpT = p_p.get()
                    fw.op("scalar", "activation", dict(out=pT[:, :], in_=st_[:, :], func=AF.Exp, scale=scale),
                          reads=[Bst_], writes=[BpT])
                    pts[t] = (pT, BpT)
                    if t == 1:
                        p0, Bp0 = pts[0]
                        fw.op("vector", "tensor_tensor", dict(out=accA[:, :], in0=p0[:, :], in1=pT[:, :], op=ALU.add),
                              reads=[Bp0, BpT], writes=[BaA])
                    elif t > 1:
                        fw.op("vector", "tensor_tensor", dict(out=accA[:, :], in0=accA[:, :], in1=pT[:, :], op=ALU.add),
                              reads=[BpT, BaA], writes=[BaA])

                def pv_stage(t):
                    pT, BpT = pts.pop(t)
                    for i in range(2):
                        kc = 2 * t + i
                        fw.op("tensor", "matmul", dict(out=PS[po][:, :], lhsT=Vh[:, kc, :], rhs=pT[:, i * 512:(i + 1) * 512],
                                                       start=(kc == 0), stop=(kc == NKC - 1)),
                              reads=[BV, BpT], writes=[PB[po]], signal=(i == 1))

                def finalize():
                    pdt, Bpd = s_get()
                    for i in range(2):
                        fw.op("tensor", "matmul", dict(out=pdt[:, 0:512], lhsT=onesb[:, :], rhs=accA[:, i * 512:(i + 1) * 512],
                                                       start=(i == 0), stop=(i == 1)),
                              reads=[Bc, BaA], writes=[Bpd], signal=(i == 1))
                    rden, Brd = rden_p.get(); osb, Bo = o_p.get(); wsb, Bw = w_p.get()
                    fw.op("vector", "reciprocal", dict(out=rden[:, :], in_=pdt[:, 0:512]), reads=[Bpd], writes=[Brd])
                    fw.op("vector", "tensor_tensor", dict(out=osb[:, :], in0=PS[po][:, :], in1=rden[:, :], op=ALU.mult),
                          reads=[PB[po], Brd], writes=[Bo])
                    fw.op("gpsimd", "tensor_tensor", dict(out=wsb[:, :], in0=osb[:, :], in1=szb[:, :], op=ALU.mult),
                          reads=[Bo, Bszb], writes=[Bw])
                    fw.dma("gpsimd", wT_d[h, :, qb * 512:(qb + 1) * 512], wsb[:, :], reads=[Bw])
                return s_stage, pv_stage, finalize

            unit = 0
            prev = None
            for hk in range(NKV):
                kTh, BkT = kT_p.get(); Vh, BV = V_p.get()
                fw.dma("sync", kTh[:, :], kT_d[hk], writes=[BkT])
                fw.dma("sync", Vh[:, :, :], V_d[:, hk * 128:(hk + 1) * 128].rearrange("(c p) d -> p c d", p=128), writes=[BV])
                for g in range(4):
                    h = hk * 4 + g
                    for qb in range(NQB):
                        cur = make_unit(kTh, BkT, Vh, BV, h, qb, 6 + (unit % 2))
                        unit += 1
                        for t in range(NT2):
                            cur[0](t)
                            if t >= LAG:
                                cur[1](t - LAG)
                            elif prev is not None:
                                prev[1](NT2 - LAG + t)
                                if t == LAG - 1:
                                    prev[2]()
                        prev = cur
            for t in range(LAG):
                prev[1](NT2 - LAG + t)
            prev[2]()
            fw.barrier()
            fw.flush()

        with contextlib.ExitStack() as ph:
            w1out = T(ph, "w1out", [128, NCH, D], BF16); Bw = Buf()
            fw.dma("sync", w1out[:], w1out_bf, writes=[Bw])
            gate1 = T(ph, "gate1", [128, D], F32); fng = T(ph, "fng", [128, D], F32); Bg = Buf()
            fw.dma("sync", gate1[:], modrow[1, 0, 2 * D:3 * D].partition_broadcast(128), writes=[Bg])
            fw.dma("sync", fng[:], final_norm_g.partition_broadcast(128), writes=[Bg])
            wt_p = Pool(nc, ph, "wTt", [128, NCH, 512], BF16, 2)
            xr_p = Pool(nc, ph, "xr3", [128, D], F32, 3)
            xo_p = Pool(nc, ph, "xo3", [128, D], F32, 3)
            junk = T(ph, "junk3", [128, D], BF16); Bjunk = Buf()
            ss_p = Pool(nc, ph, "ss3", [128, 1], F32, 3)
            for blk in range(L // 512):
                wt, Bwt = wt_p.get()
                fw.dma("sync", wt[:, :, :], wT_d[:, :, blk * 512:(blk + 1) * 512].rearrange("h p t -> p h t"), writes=[Bwt])
                for j in range(4):
                    t0 = blk * 512 + j * 128
                    xr, Bxr = xr_p.get(); xo, Bxo = xo_p.get(); ss, Bss = ss_p.get()
                    fw.dma("sync", xr[:, :], x1_d[t0:t0 + 128, :], writes=[Bxr])
                    for half in range(2):
                        pb = (j % 2) * 2 + half
                        for c in range(NCH):
                            fw.op("tensor", "matmul", dict(out=PS[pb][:, :], lhsT=wt[:, c, j * 128:(j + 1) * 128], rhs=w1out[:, c, half * 512:(half + 1) * 512], start=(c == 0), stop=(c == NCH - 1)),
                                reads=[Bwt, Bw], writes=[PB[pb]], signal=(c == NCH - 1))
                        fw.op("vector", "tensor_tensor", dict(out=xo[:, half * 512:(half + 1) * 512], in0=PS[pb][:, :], in1=gate1[:, half * 512:(half + 1) * 512], op=ALU.mult),
                            reads=[PB[pb], Bg], writes=[Bxo])
                    fw.op("gpsimd", "tensor_tensor", dict(out=xo[:, :], in0=xo[:, :], in1=xr[:, :], op=ALU.add),
                          reads=[Bxo, Bxr], writes=[Bxo])
                    fw.op("scalar", "activation", dict(out=junk[:, :], in_=xo[:, :], func=AF.Square, accum_out=ss[:, 0:1]),
                          reads=[Bxo], writes=[Bjunk, Bss])
                    fw.op("vector", "tensor_scalar", dict(out=ss[:, :], in0=ss[:, :], scalar1=1.0 / D, scalar2=RMS_EPS, op0=ALU.mult, op1=ALU.add),
                          reads=[Bss], writes=[Bss])
                    fw.op("gpsimd", "tensor_tensor", dict(out=ss[:, :], in0=ss[:, :], in1=mhalf[:, 0:1], op=ALU.pow),
                          reads=[Bss, Bc], writes=[Bss])
                    fw.op("vector", "scalar_tensor_tensor", dict(out=xo[:, :], in0=xo[:, :], scalar=ss[:, 0:1], in1=fng[:, :], op0=ALU.mult, op1=ALU.mult),
                          reads=[Bxo, Bss, Bg], writes=[Bxo])
                    fw.dma("gpsimd", out_d[t0:t0 + 128, :], xo[:, :], reads=[Bxo])
            fw.barrier()
            fw.flush()
    return nc


def _rope_tables(L):
    rows = L // 64
    row = np.broadcast_to(np.arange(rows)[:, None], (rows, 64)).reshape(-1).astype(np.float32)
    col = np.broadcast_to(np.arange(64)[None, :], (rows, 64)).reshape(-1).astype(np.float32)
    inv_freq = (np.float32(10000.0) ** (-np.arange(0, 64, 2, dtype=np.float32) / np.float32(64))).astype(np.float32)
    ang = np.concatenate([row[:, None] * inv_freq[None, :], col[:, None] * inv_freq[None, :]], axis=1).astype(np.float32)
    return np.cos(ang).astype(np.float32), np.sin(ang).astype(np.float32)


_CACHE = {}


def make_in_maps(inputs, L, ncores):
    cos, sin = _rope_tables(L)
    ident = np.eye(128, dtype=np.float32)
    maps = []
    f = lambda a: np.ascontiguousarray(np.asarray(a, dtype=np.float32))
    shared = {k: f(v) for k, v in inputs.items() if k not in ("x", "c", "ctx")}
    for b in range(ncores):
        m = dict(shared)
        m["x"] = f(inputs["x"][b]); m["c"] = f(inputs["c"][b]); m["ctx"] = f(inputs["ctx"][b])
        m["ident"] = ident; m["rope_cos"] = cos; m["rope_sin"] = sin
        maps.append(m)
    return maps


def kernel(**inputs):
    x = np.asarray(inputs["x"])
    B, L, _ = x.shape
    key = (L,)
    if key not in _CACHE:
        _CACHE[key] = build_program(L)
    nc = _CACHE[key]
    maps = make_in_maps(inputs, L, B)
    res = run_bass_kernel_spmd(nc, maps, core_ids=list(range(B)))
    return np.stack([np.asarray(r["out"], dtype=np.float32) for r in res.results], axis=0)
```

```python
import contextlib
import numpy as np
import concourse.bass as bass
import concourse.mybir as mybir
from concourse.bass_utils import run_bass_kernel_spmd

F32 = mybir.dt.float32
BF16 = mybir.dt.bfloat16
ALU = mybir.AluOpType
AF = mybir.ActivationFunctionType
AX = mybir.AxisListType

D = 1024
E = 2048
NK = 8
NCH = 16
CW = 31
HALO = 15
CTX = 256
NQH = 16
NKV = 4
KVD = 512
W1C = 2 * E + 2 * KVD
RMS_EPS = 1e-6
LN_EPS = 1e-5
SEQ = 8192
NCORES = 8

ENGS = ["tensor", "vector", "scalar", "gpsimd", "sync"]
N_DMA_SEMS = {"sync": 14, "vector": 0, "scalar": 4, "gpsimd": 8, "tensor": 0}


class Buf:
    __slots__ = ("name", "last_write", "reads")

    def __init__(self, name=""):
        self.name = name
        self.last_write = None
        self.reads = []


class EngState:
    def __init__(self, name):
        self.name = name
        self.n = 0
        self.seen = {}
        self.queue = []
        self.dma_sems = []
        self.dma_rr = 0


class FW:
    def __init__(self, nc, stack):
        self.nc = nc
        self.sems = {}
        self.E = {}
        for e in ENGS:
            self.sems[f"tl_{e}"] = stack.enter_context(nc.semaphore(f"tl_{e}"))
            self.E[e] = EngState(e)
            for i in range(N_DMA_SEMS[e]):
                k = f"dq_{e}_{i}"
                self.sems[k] = stack.enter_context(nc.semaphore(k))
                self.E[e].dma_sems.append([k, 0])

    def _collect(self, eng, reads, writes):
        deps = {}

        def add(ev):
            if ev is None:
                return
            k, v = ev
            if deps.get(k, 0) < v:
                deps[k] = v
        for b in reads:
            add(b.last_write)
        for b in writes:
            add(b.last_write)
            for r in b.reads:
                add(r)
        st = self.E[eng]
        waits = []
        for k, v in deps.items():
            if eng == "tensor" and k == "tl_tensor":
                continue
            if st.seen.get(k, 0) >= v:
                continue
            st.seen[k] = v
            waits.append((k, v))
        return waits

    def _post(self, ev, reads, writes):
        for b in reads:
            b.reads.append(ev)
            if len(b.reads) > 64:
                b.reads = b.reads[-48:]
        for b in writes:
            b.last_write = ev
            b.reads = []

    def op(self, eng, name, kw, reads=(), writes=(), signal=True, args=()):
        fn = (lambda e, name=name, args=args, kw=kw: getattr(e, name)(*args, **kw))
        st = self.E[eng]
        waits = self._collect(eng, reads, writes)
        if signal:
            st.n += 1
            ev = (f"tl_{eng}", st.n)
        else:
            assert eng == "tensor"
            ev = (f"tl_{eng}", st.n + 1)
        st.queue.append((waits, fn, (f"tl_{eng}", 1) if signal else None))
        self._post(ev, reads, writes)
        return ev

    def dma(self, eng, out, in_, reads=(), writes=(), **kw):
        st = self.E[eng]
        slot = st.dma_sems[st.dma_rr % len(st.dma_sems)]
        st.dma_rr += 1
        k = slot[0]
        waits = self._collect(eng, reads, writes)
        if slot[1] > 0 and st.seen.get(k, 0) < slot[1]:
            st.seen[k] = slot[1]
            waits.append((k, slot[1]))
        slot[1] += 16
        ev = (k, slot[1])
        st.queue.append((waits, (lambda e, o=out, i=in_, kw=kw: e.dma_start(out=o, in_=i, **kw)), (k, 16)))
        self._post(ev, reads, writes)
        return ev

    def barrier(self):
        targets = {}
        for e in ENGS:
            st = self.E[e]
            if st.n > 0:
                targets[f"tl_{e}"] = st.n
            for k, c in st.dma_sems:
                if c > 0:
                    targets[k] = c
        for e in ENGS:
            st = self.E[e]
            waits = []
            for k, v in targets.items():
                if st.seen.get(k, 0) >= v:
                    continue
                st.seen[k] = v
                waits.append((k, v))
            if waits:
                st.queue.append((waits, None, None))

    def flush(self):
        nc = self.nc
        sems = self.sems
        with nc.Block() as block:
            def mk(e):
                st = self.E[e]

                def body(engine):
                    for waits, fn, inc in st.queue:
                        for k, v in waits:
                            engine.wait_ge(sems[k], v)
                        if fn is not None:
                            ins = fn(engine)
                            if inc is not None:
                                ins.then_inc(sems[inc[0]], inc[1])
                    st.queue = []
                return body
            block.tensor(mk("tensor"))
            block.vector(mk("vector"))
            block.scalar(mk("scalar"))
            block.gpsimd(mk("gpsimd"))
            block.sync(mk("sync"))


class Pool:
    def __init__(self, nc, stack, name, shape, dtype, n, psum=False):
        self.items = []
        for i in range(n):
            mk = nc.psum_tensor if psum else nc.sbuf_tensor
            t = stack.enter_context(mk(f"pl_{name}{i}", shape, dtype))
            self.items.append((t, Buf(f"{name}{i}")))
        self.i = 0

    def get(self):
        it = self.items[self.i % len(self.items)]
        self.i += 1
        return it


def build_program(L=SEQ, debug=False):
    assert L % 512 == 0
    NKEY = CTX + L
    NKC = NKEY // 128
    nc = bass.Bass("TRN2", target_bir_lowering=False)

    def din(name, shape):
        return nc.dram_tensor(name, list(shape), F32, kind="ExternalInput").ap()

    x_d = din("x", [L, D]); c_d = din("c", [D]); ctx_d = din("ctx", [CTX, D]); cctx_d = din("c_ctx", [D])
    l0_norm_g = din("l0_norm_g", [D]); l0_ada_w = din("l0_ada_w", [D, 3 * D]); l0_ada_b = din("l0_ada_b", [3 * D])
    l0_w_in = din("l0_w_in", [D, 3 * E]); l0_b_in = din("l0_b_in", [3 * E]); l0_dw_w = din("l0_dw_w", [CW, E])
    l0_dw_b = din("l0_dw_b", [E]); l0_ln_g = din("l0_ln_g", [E]); l0_ln_b = din("l0_ln_b", [E])
    l0_w_out = din("l0_w_out", [E, D]); l0_b_out = din("l0_b_out", [D])
    l1_norm_g = din("l1_norm_g", [D]); l1_ada_w = din("l1_ada_w", [D, 3 * D]); l1_ada_b = din("l1_ada_b", [3 * D])
    l1_w_in = din("l1_w_in", [D, W1C]); l1_q_norm_g = din("l1_q_norm_g", [128]); l1_k_norm_g = din("l1_k_norm_g", [128])
    l1_w_out = din("l1_w_out", [E, D]); final_norm_g = din("final_norm_g", [D])
    ident_d = din("ident", [128, 128]); cos_d = din("rope_cos", [L, 64]); sin_d = din("rope_sin", [L, 64])
    out_d = nc.dram_tensor("out", [L, D], F32, kind="ExternalOutput").ap()

    def dscr(name, shape, dt):
        return nc.dram_tensor(name, list(shape), dt).ap()

    w0in_bf = dscr("w0in_bf", [NCH, 128, NK, 3, 128], BF16)
    w0out_bf = dscr("w0out_bf", [128, NCH, D], BF16)
    diag_bf = dscr("diag_bf", [NCH, 128, CW * 128], BF16)
    w1in_bf = dscr("w1in_bf", [128, NK, W1C], BF16)
    w1out_bf = dscr("w1out_bf", [128, NCH, D], BF16)
    modrow = dscr("modrow", [2, 2, 3 * D], F32)
    if debug:
        x1_d = nc.dram_tensor("x1", [L, D], F32, kind="ExternalOutput").ap()
        ctx1_d = nc.dram_tensor("ctx1", [CTX, D], F32, kind="ExternalOutput").ap()
    else:
        x1_d = dscr("x1", [L, D], F32)
        ctx1_d = dscr("ctx1", [CTX, D], F32)
    qT_d = dscr("qT", [NQH, 128, L], BF16)
    kT_d = dscr("kT", [NKV, 128, NKEY], BF16)
    V_d = dscr("V", [NKEY, KVD], BF16)
    szT_d = dscr("szT", [NQH, 128, L], BF16)
    wT_d = dscr("wT", [NQH, 128, L], BF16)

    with contextlib.ExitStack() as top:
        fw = FW(nc, top)

        def T(stack, name, shape, dt):
            return stack.enter_context(nc.sbuf_tensor("sb_" + name, list(shape), dt))

        ident = T(top, "ident", [128, 128], F32); identb = T(top, "identb", [128, 128], BF16)
        onesb = T(top, "onesb", [128, 128], BF16)
        g0c = T(top, "g0c", [128, NK], F32); g1c = T(top, "g1c", [128, NK], F32)
        binc = T(top, "binc", [128, 3 * NCH], F32); dwT = T(top, "dwT", [128, NCH, CW], BF16)
        binh = T(top, "binh", [128, NCH], F32)
        dwbc = T(top, "dwbc", [128, NCH], F32); lngc = T(top, "lngc", [128, NCH], F32); lnbc = T(top, "lnbc", [128, NCH], F32)
        mod = [T(top, f"mod{l}", [128, 24, 2], F32) for l in range(2)]
        gmul = [T(top, f"gmul{l}", [128, NK, 2], F32) for l in range(2)]
        epsc = T(top, "epsc", [128, 2], F32)
        mhalf = T(top, "mhalf", [128, 512], F32)
        Bc = Buf("consts")
        PSALL = top.enter_context(nc.psum_tensor("psall", [128, 4096], F32))
        PSW = [PSALL[:, i * 1024:(i + 1) * 1024] for i in range(4)]
        PS = [PSALL[:, i * 512:(i + 1) * 512] for i in range(8)]
        PB = [Buf(f"ps{i}") for i in range(8)]

        with contextlib.ExitStack() as ph:
            stage = Pool(nc, ph, "stage", [48, 128], F32, 2)
            dwr = T(ph, "dwr", [CW, E], F32); Bdwr = Buf()
            adaw = T(ph, "adaw", [128, NK, 3 * D], F32); Badaw = Buf()
            craw = T(ph, "craw", [128, 2, NK], F32); condT = T(ph, "condT", [128, 2, NK], F32); Bcond = Buf()
            adabc = T(ph, "adabc", [128, 24], F32)
            adabr = T(ph, "adabr", [2, 3 * D], F32); rowt = T(ph, "rowt", [2, 3 * D], F32); Brow = Buf(); Badabr = Buf()

            fw.dma("sync", ident[:], ident_d, writes=[Bc])
            fw.op("vector", "tensor_copy", dict(out=identb[:], in_=ident[:]), reads=[Bc], writes=[Bc])
            fw.op("vector", "memset", dict(ap=onesb[:], constant=1.0), writes=[Bc])
            fw.op("vector", "memset", dict(ap=mhalf[:, :], constant=-0.5), writes=[Bc])
            fw.op("vector", "memset", dict(ap=epsc[:, 0:1], constant=RMS_EPS), writes=[Bc])
            fw.op("vector", "memset", dict(ap=epsc[:, 1:2], constant=LN_EPS), writes=[Bc])

            def to_cols(vec, n, dst):
                st_t, st_b = stage.get()
                fw.dma("sync", st_t[0:n, :], vec.rearrange("(n p) -> n p", p=128), writes=[st_b])
                fw.op("tensor", "transpose", dict(out=PS[0][:, 0:n], in_=st_t[0:n, :], identity=ident[0:n, 0:n]),
                      reads=[st_b, Bc], writes=[PB[0]])
                fw.op("vector", "tensor_copy", dict(out=dst, in_=PS[0][:, 0:n]), reads=[PB[0]], writes=[Bc])

            to_cols(l0_norm_g, NK, g0c[:]); to_cols(l1_norm_g, NK, g1c[:]); to_cols(l0_b_in, 3 * NCH, binc[:])
            to_cols(l0_dw_b, NCH, dwbc[:]); to_cols(l0_ln_g, NCH, lngc[:]); to_cols(l0_ln_b, NCH, lnbc[:])
            fw.op("vector", "tensor_scalar", dict(out=binh[:], in0=binc[:, NCH:2 * NCH], scalar1=0.5, scalar2=None, op0=ALU.mult), reads=[Bc], writes=[Bc])
            fw.dma("sync", dwr[:], l0_dw_w, writes=[Bdwr])
            for cch in range(NCH):
                fw.op("tensor", "transpose", dict(out=PS[1][:, cch * CW:(cch + 1) * CW], in_=dwr[:, cch * 128:(cch + 1) * 128], identity=ident[0:CW, 0:CW]),
                      reads=[Bdwr, Bc], writes=[PB[1]], signal=(cch == NCH - 1))
            fw.op("vector", "tensor_copy", dict(out=dwT[:].rearrange("p c k -> p (c k)"), in_=PS[1][:, 0:NCH * CW]),
                  reads=[PB[1]], writes=[Bc])
            fw.dma("sync", craw[:, 0, :], c_d.rearrange("(p k) -> p k", k=NK), writes=[Bcond])
            fw.dma("sync", craw[:, 1, :], cctx_d.rearrange("(p k) -> p k", k=NK), writes=[Bcond])
            fw.op("scalar", "activation", dict(out=condT[:], in_=craw[:], func=AF.Silu), reads=[Bcond], writes=[Bcond])
            for l, (aw, ab, gc) in enumerate([(l0_ada_w, l0_ada_b, g0c), (l1_ada_w, l1_ada_b, g1c)]):
                for h in range(2):
                    fw.dma("sync", adaw[:, h * 4:(h + 1) * 4, :], aw.rearrange("(p k) n -> p k n", k=NK)[:, h * 4:(h + 1) * 4, :],
                           writes=[Badaw])
                fw.dma("sync", adabr[:], ab.partition_broadcast(2), writes=[Badabr])
                to_cols(ab, 24, adabc[:])
                for j in range(24):
                    for k in range(NK):
                        fw.op("tensor", "matmul", dict(out=PS[2][:, 2 * j:2 * j + 2], lhsT=adaw[:, k, j * 128:(j + 1) * 128], rhs=condT[:, :, k], start=(k == 0), stop=(k == NK - 1)),
                              reads=[Badaw, Bcond], writes=[PB[2]], signal=(j == 23 and k == NK - 1))
                fw.op("vector", "tensor_tensor", dict(out=mod[l][:], in0=PS[2][:, 0:48].rearrange("p (j c) -> p j c", c=2), in1=adabc[:].unsqueeze(2).to_broadcast([128, 24, 2]), op=ALU.add),
                      reads=[PB[2], Bc], writes=[Bc])
                fw.op("vector", "scalar_tensor_tensor", dict(out=gmul[l][:], in0=mod[l][:, 8:16, :], scalar=1.0, in1=gc[:].unsqueeze(2).to_broadcast([128, NK, 2]), op0=ALU.add, op1=ALU.mult),
                      reads=[Bc], writes=[Bc])
                for n in range(6):
                    pb = 3 + (n % 2)
                    for k in range(NK):
                        fw.op("tensor", "matmul", dict(out=PS[pb][0:2, :], lhsT=condT[:, :, k], rhs=adaw[:, k, n * 512:(n + 1) * 512], start=(k == 0), stop=(k == NK - 1)),
                              reads=[Badaw, Bcond], writes=[PB[pb]], signal=(k == NK - 1))
                    fw.op("vector", "tensor_tensor", dict(out=rowt[:, n * 512:(n + 1) * 512], in0=PS[pb][0:2, :], in1=adabr[:, n * 512:(n + 1) * 512], op=ALU.add),
                          reads=[PB[pb], Badabr], writes=[Brow])
                fw.dma("sync", modrow[l], rowt[:], reads=[Brow])
            fw.barrier()
            fw.flush()

        with contextlib.ExitStack() as ph:
            slab = Pool(nc, ph, "slab", [128, 3 * E], F32, 2)
            slabb = Pool(nc, ph, "slabb", [128, 3 * E], BF16, 2)
            cast_i = [0]

            def cast(dst, src, rb, wb):
                eng = ["vector", "gpsimd", "scalar"][cast_i[0] % 3]
                cast_i[0] += 1
                if eng == "scalar":
                    fw.op(eng, "copy", dict(out=dst, in_=src), reads=[rb], writes=[wb])
                else:
                    fw.op(eng, "tensor_copy", dict(out=dst, in_=src), reads=[rb], writes=[wb])

            for c in range(NCH):
                t, tb = slabb.get()
                fw.op("vector", "tensor_tensor", dict(out=t[:, 0:CW * 128].rearrange("p (k j) -> p k j", j=128),
                                                      in0=identb[:].unsqueeze(1).to_broadcast([128, CW, 128]),
                                                      in1=dwT[:, c, :].unsqueeze(2).to_broadcast([128, CW, 128]), op=ALU.mult),
                      reads=[Bc], writes=[tb])
                fw.dma("gpsimd", diag_bf[c], t[:, 0:CW * 128], reads=[tb])
            for k in range(NK):
                s, sb = slab.get(); t, tb = slabb.get()
                fw.dma("sync", s[:, :], l0_w_in[k * 128:(k + 1) * 128, :], writes=[sb])
                for h in range(3):
                    cast(t[:, h * E:(h + 1) * E], s[:, h * E:(h + 1) * E], sb, tb)
                for tt in range(3):
                    fw.dma("gpsimd", w0in_bf[:, :, k, tt, :].rearrange("c p j -> p c j"),
                           t[:, tt * E:(tt + 1) * E].rearrange("p (c j) -> p c j", j=128), reads=[tb])
            for k in range(NK):
                s, sb = slab.get(); t, tb = slabb.get()
                fw.dma("sync", s[:, 0:W1C], l1_w_in[k * 128:(k + 1) * 128, :], writes=[sb])
                for h in range(2):
                    cast(t[:, h * 2560:(h + 1) * 2560], s[:, h * 2560:(h + 1) * 2560], sb, tb)
                fw.dma("gpsimd", w1in_bf[:, k, :], t[:, 0:W1C], reads=[tb])
            for wsrc, wdst in ((l0_w_out, w0out_bf), (l1_w_out, w1out_bf)):
                for i in range(4):
                    s, sb = slab.get(); t, tb = slabb.get()
                    fw.dma("sync", s[:, 0:4096].rearrange("p (c n) -> p c n", n=D),
                           wsrc.rearrange("(c p) n -> p c n", p=128)[:, 4 * i:4 * i + 4, :], writes=[sb])
                    for h in range(2):
                        cast(t[:, h * 2048:(h + 1) * 2048], s[:, h * 2048:(h + 1) * 2048], sb, tb)
                    fw.dma("gpsimd", wdst[:, 4 * i:4 * i + 4, :], t[:, 0:4096].rearrange("p (c n) -> p c n", n=D), reads=[tb])
            fw.barrier()
            fw.flush()

        with contextlib.ExitStack() as ph:
            w0out = T(ph, "w0out", [128, NCH, D], BF16); Bw0out = Buf()
            gate_t = T(ph, "gate_b", [128, D], F32)
            gbo_t = T(ph, "gbo_b", [128, D], F32)
            gate_b = [gate_t, gate_t]; gbo_b = [gbo_t, gbo_t]
            Bg = Buf()
            fw.dma("sync", w0out[:], w0out_bf, writes=[Bw0out])

            def set_gate(ci):
                fw.dma("sync", gate_t[:], modrow[0, ci, 2 * D:3 * D].partition_broadcast(128), writes=[Bg])
                fw.dma("sync", gbo_t[:], l0_b_out.partition_broadcast(128), writes=[Bg])
                fw.op("vector", "tensor_tensor", dict(out=gbo_t[:], in0=gbo_t[:], in1=gate_t[:], op=ALU.mult), reads=[Bg], writes=[Bg])
            xt_p = Pool(nc, ph, "xt", [128, D], F32, 5)
            ssq = Pool(nc, ph, "ssq", [128, 8], F32, 2)
            hT_p = Pool(nc, ph, "hT", [128, NK, 512 + 2 * HALO], BF16, 2)
            wch_p = Pool(nc, ph, "wch", [128, NK, 3, 128], BF16, 2)
            diag_p = Pool(nc, ph, "diag", [128, CW, 128], BF16, 2)
            sg_p = Pool(nc, ph, "sg", [128, 512 + 2 * HALO], F32, 2)
            v_p = Pool(nc, ph, "v", [128, 512 + 2 * HALO], BF16, 2)
            co = T(ph, "co", [128, NCH, 512], F32); Bco = [Buf() for _ in range(NCH)]
            cob_p = Pool(nc, ph, "cob", [128, 512], BF16, 3)
            sqb_p = Pool(nc, ph, "sqb", [128, 512], BF16, 3)
            sz = T(ph, "sz", [128, NCH, 512], BF16); Bsz = [Buf() for _ in range(NCH)]
            wg = T(ph, "wg", [128, NCH, 512], BF16); Bwg = [Buf() for _ in range(NCH)]
            mean = T(ph, "mean", [128, 512], F32); rstd = T(ph, "rstd", [128, 512], F32); Bst = Buf()
            tmp_p = Pool(nc, ph, "tmpn", [128, 512], F32, 2)
            sil_p = Pool(nc, ph, "sil", [128, 512], BF16, 5)
            xr_p = Pool(nc, ph, "xr", [128, D], F32, 1)
            xo_p = Pool(nc, ph, "xo", [128, D], F32, 1)

            cur_gate = [None]

            def l0_block(src, dst, s0, NT, ci, total):
                left_pad = (s0 == 0)
                right_pad = (s0 + NT == total)
                ntile = NT // 128
                nt1 = ntile + 1
                WT = NT + 2 * HALO
                S = {}
                hT, BhT = hT_p.get()

                def front_load():
                    xts = []
                    for j in range(nt1):
                        xt, Bxt = xt_p.get()
                        xts.append((xt, Bxt))
                        if j < ntile:
                            fw.dma("sync", xt[:, :], src[s0 + j * 128:s0 + (j + 1) * 128, :], writes=[Bxt])
                        else:
                            fw.op("gpsimd", "memset", dict(ap=xt[0:64, :], constant=0.0), writes=[Bxt])
                            if not left_pad:
                                fw.dma("sync", xt[0:HALO, :], src[s0 - HALO:s0, :], writes=[Bxt])
                            if not right_pad:
                                fw.dma("sync", xt[32:32 + HALO, :], src[s0 + NT:s0 + NT + HALO, :], writes=[Bxt])
                    S["xts"] = xts

                def front_pre():
                    ss, Bss = ssq.get()
                    S["ss"], S["Bss"] = ss, Bss
                    fw.op("vector", "memset", dict(ap=ss[:, :], constant=0.0), writes=[Bss])
                    xts = S["xts"]
                    for j in range(nt1):
                        xt, Bxt = xts[j]
                        nr = 128 if j < ntile else 64
                        jt, Bjt = tmp_p.get()
                        fw.op("scalar", "activation", dict(out=jt[0:nr, :].bitcast(BF16), in_=xt[0:nr, :], func=AF.Square, accum_out=ss[0:nr, j:j + 1]),
                              reads=[Bxt], writes=[Bjt, Bss])
                    fw.op("vector", "tensor_scalar", dict(out=ss[:, 0:nt1], in0=ss[:, 0:nt1], scalar1=1.0 / D, scalar2=RMS_EPS, op0=ALU.mult, op1=ALU.add),
                          reads=[Bss], writes=[Bss])
                    fw.op("gpsimd", "tensor_tensor", dict(out=ss[:, 0:nt1], in0=ss[:, 0:nt1], in1=mhalf[:, 0:nt1], op=ALU.pow),
                          reads=[Bss, Bc], writes=[Bss])
                    for j in range(nt1):
                        xt, Bxt = xts[j]
                        nr = 128 if j < ntile else 64
                        fw.op("vector", "tensor_scalar", dict(out=xt[0:nr, :], in0=xt[0:nr, :], scalar1=ss[0:nr, j:j + 1], scalar2=None, op0=ALU.mult),
                              reads=[Bxt, Bss], writes=[Bxt])
                    S["xts"] = xts

                def front_tr():
                    xts = S["xts"]
                    for j in range(nt1):
                        xt, Bxt = xts[j]
                        nr = 128 if j < ntile else 64
                        for half in range(2):
                            pb = (2 * j + half) % 4
                            for kk in range(4):
                                k = half * 4 + kk
                                fw.op("tensor", "transpose", dict(out=PS[pb][:, kk * 128:kk * 128 + nr], in_=xt[0:nr, k * 128:(k + 1) * 128], identity=ident[0:nr, 0:nr]),
                                      reads=[Bxt, Bc], writes=[PB[pb]], signal=(kk == 3))
                            for kk in range(4):
                                k = half * 4 + kk
                                eng = "scalar" if kk % 2 == 0 else "vector"
                                pieces = ([(hT[:, k, HALO + j * 128:HALO + (j + 1) * 128], PS[pb][:, kk * 128:(kk + 1) * 128])] if j < ntile else
                                          [(hT[:, k, 0:HALO], PS[pb][:, kk * 128:kk * 128 + HALO]),
                                           (hT[:, k, HALO + NT:HALO + NT + HALO], PS[pb][:, kk * 128 + 32:kk * 128 + 32 + HALO])])
                                for (o_, i_) in pieces:
                                    if eng == "scalar":
                                        fw.op("scalar", "activation", dict(out=o_, in_=i_, func=AF.Identity, scale=gmul[0][:, k, ci:ci + 1], bias=mod[0][:, k, ci:ci + 1]),
                                              reads=[PB[pb], Bc], writes=[BhT])
                                    else:
                                        fw.op("vector", "tensor_scalar", dict(out=o_, in0=i_, scalar1=gmul[0][:, k, ci:ci + 1], scalar2=mod[0][:, k, ci:ci + 1],
                                                                              op0=ALU.mult, op1=ALU.add),
                                              reads=[PB[pb], Bc], writes=[BhT])

                def p_stage(c):
                    wch, Bwch = wch_p.get()
                    fw.dma("sync", wch[:].rearrange("p k t j -> p (k t j)"), w0in_bf[c].rearrange("p k t j -> p (k t j)"), writes=[Bwch])
                    dg, Bdg = diag_p.get()
                    fw.dma("sync", dg[:].rearrange("p k j -> p (k j)"), diag_bf[c], writes=[Bdg])
                    sg, Bsg = sg_p.get(); v, Bv = v_p.get()
                    w1 = min(WT, 512)
                    rem = WT - w1
                    for k in range(NK):
                        fw.op("tensor", "matmul", dict(out=PS[2][:, 0:w1], lhsT=wch[:, k, 1, :], rhs=hT[:, k, 0:w1], start=(k == 0), stop=(k == NK - 1)),
                              reads=[Bwch, BhT], writes=[PB[2]], signal=(k == NK - 1))
                    if rem > 0:
                        for k in range(NK):
                            fw.op("tensor", "matmul", dict(out=PS[4][:, 0:rem], lhsT=wch[:, k, 1, :], rhs=hT[:, k, 512:WT], start=(k == 0), stop=(k == NK - 1)),
                                  reads=[Bwch, BhT], writes=[PB[4]], signal=(k == NK - 1))
                    fw.op("scalar", "activation", dict(out=sg[:, 0:w1], in_=PS[2][:, 0:w1], func=AF.Tanh, scale=0.5, bias=binh[:, c:c + 1]),
                          reads=[PB[2], Bc], writes=[Bsg])
                    if rem > 0:
                        fw.op("scalar", "activation", dict(out=sg[:, 512:WT], in_=PS[4][:, 0:rem], func=AF.Tanh, scale=0.5, bias=binh[:, c:c + 1]),
                              reads=[PB[4], Bc], writes=[Bsg])
                    fw.op("vector", "tensor_scalar", dict(out=sg[:, 0:WT], in0=sg[:, 0:WT], scalar1=0.5, scalar2=0.5, op0=ALU.mult, op1=ALU.add),
                          reads=[Bsg], writes=[Bsg])
                    for k in range(NK):
                        fw.op("tensor", "matmul", dict(out=PS[3][:, 0:w1], lhsT=wch[:, k, 0, :], rhs=hT[:, k, 0:w1], start=(k == 0), stop=(k == NK - 1)),
                              reads=[Bwch, BhT], writes=[PB[3]], signal=(k == NK - 1))
                    if rem > 0:
                        for k in range(NK):
                            fw.op("tensor", "matmul", dict(out=PS[4][:, 64:64 + rem], lhsT=wch[:, k, 0, :], rhs=hT[:, k, 512:WT], start=(k == 0), stop=(k == NK - 1)),
                                  reads=[Bwch, BhT], writes=[PB[4]], signal=(k == NK - 1))
                    fw.op("vector", "scalar_tensor_tensor", dict(out=v[:, 0:w1], in0=PS[3][:, 0:w1], scalar=binc[:, c:c + 1], in1=sg[:, 0:w1], op0=ALU.add, op1=ALU.mult),
                          reads=[PB[3], Bsg, Bc], writes=[Bv])
                    if rem > 0:
                        fw.op("vector", "scalar_tensor_tensor", dict(out=v[:, 512:WT], in0=PS[4][:, 64:64 + rem], scalar=binc[:, c:c + 1], in1=sg[:, 512:WT], op0=ALU.add, op1=ALU.mult),
                              reads=[PB[4], Bsg, Bc], writes=[Bv])
                    if left_pad:
                        fw.op("vector", "memset", dict(ap=v[:, 0:HALO], constant=0.0), writes=[Bv])
                    if right_pad:
                        fw.op("vector", "memset", dict(ap=v[:, HALO + NT:WT], constant=0.0), writes=[Bv])
                    for k in range(NK):
                        fw.op("tensor", "matmul", dict(out=PS[c % 2][:, 0:NT], lhsT=wch[:, k, 2, :], rhs=hT[:, k, HALO:HALO + NT], start=(k == 0), stop=(k == NK - 1)),
                              reads=[Bwch, BhT], writes=[PB[c % 2]], signal=(k == NK - 1))
                    fw.op("scalar", "activation", dict(out=sz[:, c, 0:NT], in_=PS[c % 2][:, 0:NT], func=AF.Silu, bias=binc[:, 2 * NCH + c:2 * NCH + c + 1]),
                          reads=[PB[c % 2], Bc], writes=[Bsz[c]])
                    return (v, Bv, dg, Bdg)

                def c_stage(c, st):
                    v, Bv, dg, Bdg = st
                    for k in range(CW):
                        fw.op("tensor", "matmul", dict(out=PS[5][:, 0:NT], lhsT=dg[:, k, :], rhs=v[:, k:k + NT], start=(k == 0), stop=(k == CW - 1)),
                              reads=[Bdg, Bv], writes=[PB[5]], signal=(k == CW - 1))
                    fw.op("scalar", "activation", dict(out=co[:, c, 0:NT], in_=PS[5][:, 0:NT], func=AF.Identity, bias=dwbc[:, c:c + 1]),
                          reads=[PB[5], Bc], writes=[Bco[c]])
                    cob, Bcob = cob_p.get(); sqb, Bsqb = sqb_p.get()
                    fw.op("vector", "tensor_copy", dict(out=cob[:, 0:NT], in_=co[:, c, 0:NT]), reads=[Bco[c]], writes=[Bcob])
                    fw.op("gpsimd", "tensor_tensor", dict(out=sqb[:, 0:NT], in0=co[:, c, 0:NT], in1=co[:, c, 0:NT], op=ALU.mult),
                          reads=[Bco[c]], writes=[Bsqb])
                    S["pend_stats"] = (c, cob, Bcob, sqb, Bsqb)

                def stats_mm():
                    if S.get("pend_stats") is None:
                        return
                    c, cob, Bcob, sqb, Bsqb = S["pend_stats"]
                    S["pend_stats"] = None
                    fw.op("tensor", "matmul", dict(out=PS[6][:, 0:NT], lhsT=onesb[:], rhs=cob[:, 0:NT], start=(c == 0), stop=(c == NCH - 1)),
                          reads=[Bcob, Bc], writes=[PB[6]])
                    fw.op("tensor", "matmul", dict(out=PS[7][:, 0:NT], lhsT=onesb[:], rhs=sqb[:, 0:NT], start=(c == 0), stop=(c == NCH - 1)),
                          reads=[Bsqb, Bc], writes=[PB[7]])

                S["prev"] = None

                def mid(c0, c1):
                    for c in range(c0, c1):
                        cur = p_stage(c)
                        stats_mm()
                        if S["prev"] is not None:
                            c_stage(c - 1, S["prev"])
                        S["prev"] = cur
                    if c1 == NCH:
                        stats_mm()
                        c_stage(NCH - 1, S["prev"])
                        stats_mm()

                def stats_n():
                    fw.op("vector", "tensor_scalar", dict(out=mean[:, 0:NT], in0=PS[6][:, 0:NT], scalar1=1.0 / E, scalar2=None, op0=ALU.mult),
                          reads=[PB[6]], writes=[Bst])
                    fw.op("vector", "tensor_tensor", dict(out=rstd[:, 0:NT], in0=mean[:, 0:NT], in1=mean[:, 0:NT], op=ALU.mult),
                          reads=[Bst], writes=[Bst])
                    fw.op("vector", "scalar_tensor_tensor", dict(out=rstd[:, 0:NT], in0=PS[7][:, 0:NT], scalar=1.0 / E, in1=rstd[:, 0:NT], op0=ALU.mult, op1=ALU.subtract),
                          reads=[PB[7], Bst], writes=[Bst])
                    fw.op("scalar", "activation", dict(out=rstd[:, 0:NT], in_=rstd[:, 0:NT], func=AF.Sqrt, bias=epsc[:, 1:2]),
                          reads=[Bst, Bc], writes=[Bst])
                    fw.op("vector", "reciprocal", dict(out=rstd[:, 0:NT], in_=rstd[:, 0:NT]), reads=[Bst], writes=[Bst])

                def n_pre(c0, c1):
                    for c in range(c0, c1):
                        fw.op("vector", "tensor_tensor", dict(out=co[:, c, 0:NT], in0=co[:, c, 0:NT], in1=mean[:, 0:NT], op=ALU.subtract),
                              reads=[Bst], writes=[Bco[c]])
                        fw.op("vector", "tensor_tensor", dict(out=co[:, c, 0:NT], in0=co[:, c, 0:NT], in1=rstd[:, 0:NT], op=ALU.mult),
                              reads=[Bst], writes=[Bco[c]])

                def n_post(c0, c1):
                    for c in range(c0, c1):
                        sl, Bsl = sil_p.get()
                        fw.op("scalar", "activation", dict(out=sl[:, 0:NT], in_=co[:, c, 0:NT], func=AF.Silu, scale=lngc[:, c:c + 1], bias=lnbc[:, c:c + 1]),
                              reads=[Bco[c], Bc], writes=[Bsl])
                        fw.op("gpsimd", "tensor_tensor", dict(out=wg[:, c, 0:NT], in0=sl[:, 0:NT], in1=sz[:, c, 0:NT], op=ALU.mult),
                              reads=[Bsl, Bsz[c]], writes=[Bwg[c]])

                def o_stage():
                    if cur_gate[0] != ci:
                        set_gate(ci)
                        cur_gate[0] = ci
                    for j in range(ntile):
                        xr, Bxr = xr_p.get(); xo, Bxo = xo_p.get()
                        fw.dma("gpsimd", xr[:, :], src[s0 + j * 128:s0 + (j + 1) * 128, :], writes=[Bxr])
                        fw.op("gpsimd", "tensor_tensor", dict(out=xr[:, :], in0=xr[:, :], in1=gbo_b[ci][:], op=ALU.add),
                              reads=[Bxr, Bg], writes=[Bxr])
                        for half in range(2):
                            pb = half
                            for c in range(NCH):
                                fw.op("tensor", "matmul", dict(out=PS[pb][:, :], lhsT=wg[:, c, j * 128:(j + 1) * 128], rhs=w0out[:, c, half * 512:(half + 1) * 512], start=(c == 0), stop=(c == NCH - 1)),
                                    reads=[Bwg[c], Bw0out], writes=[PB[pb]], signal=(c == NCH - 1))
                            fw.op("vector", "tensor_tensor", dict(out=xo[:, half * 512:(half + 1) * 512], in0=PS[pb][:, :], in1=gate_b[ci][:, half * 512:(half + 1) * 512], op=ALU.mult),
                                reads=[PB[pb], Bg], writes=[Bxo])
                        fw.op("vector", "tensor_tensor", dict(out=xo[:, :], in0=xo[:, :], in1=xr[:, :], op=ALU.add),
                              reads=[Bxo, Bxr], writes=[Bxo])
                        fw.dma("gpsimd", dst[s0 + j * 128:s0 + (j + 1) * 128, :], xo[:, :], reads=[Bxo])

                return dict(front_load=front_load, front_pre=front_pre, front_tr=front_tr, mid=mid, stats_n=stats_n, n_pre=n_pre, n_post=n_post, o_stage=o_stage)

            specs = [(ctx_d, ctx1_d, 0, CTX, 1, CTX)] + [(x_d, x1_d, blk * 512, 512, 0, L) for blk in range(L // 512)]
            blk_objs = {}

            def getb(bi):
                if bi not in blk_objs:
                    blk_objs[bi] = l0_block(*specs[bi])
                return blk_objs[bi]
            nb = len(specs)
            A = getb(0)
            A["front_load"](); A["front_pre"](); A["front_tr"](); A["mid"](0, 4)
            if nb > 1:
                getb(1)["front_load"]()
            A["mid"](4, 12)
            for bi in range(1, nb):
                A = getb(bi - 1); Bk = getb(bi)
                Bk["front_pre"]()
                A["mid"](12, NCH)
                Bk["front_tr"]()
                if bi + 1 < nb:
                    getb(bi + 1)["front_load"]()
                A["stats_n"]()
                A["n_pre"](0, 3)
                A["n_post"](0, 1)
                npost, npre = 1, 3
                for i in range(8):
                    Bk["mid"](i, i + 1)
                    e = min(npost + 2, NCH)
                    A["n_post"](npost, e)
                    npost = e
                    e2 = min(npre + 2, NCH)
                    if e2 > npre:
                        A["n_pre"](npre, e2)
                        npre = e2
                assert npost == NCH and npre == NCH
                A["o_stage"]()
                Bk["mid"](8, 12)
            A = getb(nb - 1)
            A["mid"](12, NCH); A["stats_n"](); A["n_pre"](0, NCH); A["n_post"](0, NCH); A["o_stage"]()
            fw.barrier()
            fw.flush()

        if debug == "l0":
            with contextlib.ExitStack() as ph:
                t = T(ph, "dbg", [128, D], F32); Bt = Buf()
                for j in range(L // 128):
                    fw.dma("sync", t[:], x1_d[j * 128:(j + 1) * 128, :], reads=[], writes=[Bt])
                    fw.dma("sync", out_d[j * 128:(j + 1) * 128, :], t[:], reads=[Bt])
                fw.barrier()
                fw.flush()
            return nc

        with contextlib.ExitStack() as ph:
            w1in = T(ph, "w1in", [128, NK, 3072], BF16); Bw1 = Buf()
            fw.dma("sync", w1in[:, 0:4, :], w1in_bf[:, 0:4, 0:3072], writes=[Bw1])
            fw.dma("sync", w1in[:, 4:8, :], w1in_bf[:, 4:8, 0:3072], writes=[Bw1])
            wz_p = Pool(nc, ph, "wz", [128, NK, 128], BF16, 3)
            gqk = T(ph, "gqk", [128, 20, 128], F32); Bgqk = Buf()
            fw.dma("sync", gqk[:, 0, :], l1_q_norm_g.partition_broadcast(128), writes=[Bgqk])
            fw.dma("sync", gqk[:, 16, :], l1_k_norm_g.partition_broadcast(128), writes=[Bgqk])
            fw.op("vector", "tensor_copy", dict(out=gqk[:, 1:16, :], in_=gqk[:, 0:1, :].to_broadcast([128, 15, 128])),
                  reads=[Bgqk], writes=[Bgqk])
            fw.op("vector", "tensor_copy", dict(out=gqk[:, 17:20, :], in_=gqk[:, 16:17, :].to_broadcast([128, 3, 128])),
                  reads=[Bgqk], writes=[Bgqk])
            xt_p = Pool(nc, ph, "xt2", [128, D], F32, 8)
            junk = T(ph, "junk2", [128, D], BF16); Bjunk = Buf()
            ssq = Pool(nc, ph, "ssq2", [128, 4], F32, 3)
            hT_p = Pool(nc, ph, "hT2", [128, NK, 512], BF16, 2)
            qk_p = Pool(nc, ph, "qk", [128, 20, 128], F32, 2)
            scr_p = Pool(nc, ph, "scr", [128, 20, 128], F32, 2)
            hs_p = Pool(nc, ph, "hs", [128, 20], F32, 3)
            cs_p = Pool(nc, ph, "cs", [128, 2, 64], F32, 3)
            qr_p = Pool(nc, ph, "qr", [128, 20, 128], BF16, 2)
            qst_p = Pool(nc, ph, "qst", [128, 20, 128], BF16, 3)
            vst_p = Pool(nc, ph, "vst", [128, KVD], BF16, 3)
            szst_p = Pool(nc, ph, "szst", [128, 4, 512], BF16, 3)

            def x_pre(src, s0, NT):
                ntile = NT // 128
                ss, Bss = ssq.get()
                fw.op("vector", "memset", dict(ap=ss[:, :], constant=0.0), writes=[Bss])
                xts = []
                for j in range(ntile):
                    xt, Bxt = xt_p.get(); xts.append((xt, Bxt))
                    fw.dma("sync", xt[:, :], src[s0 + j * 128:s0 + (j + 1) * 128, :], writes=[Bxt])
                    fw.op("scalar", "activation", dict(out=junk[:, :], in_=xt[:, :], func=AF.Square, accum_out=ss[:, j:j + 1]),
                          reads=[Bxt], writes=[Bjunk, Bss])
                fw.op("vector", "tensor_scalar", dict(out=ss[:, 0:ntile], in0=ss[:, 0:ntile], scalar1=1.0 / D, scalar2=RMS_EPS, op0=ALU.mult, op1=ALU.add),
                      reads=[Bss], writes=[Bss])
                fw.op("gpsimd", "tensor_tensor", dict(out=ss[:, 0:ntile], in0=ss[:, 0:ntile], in1=mhalf[:, 0:ntile], op=ALU.pow),
                      reads=[Bss, Bc], writes=[Bss])
                for j in range(ntile):
                    xt, Bxt = xts[j]
                    fw.op("vector", "tensor_scalar", dict(out=xt[:, :], in0=xt[:, :], scalar1=ss[:, j:j + 1], scalar2=None, op0=ALU.mult),
                          reads=[Bxt, Bss], writes=[Bxt])
                return xts

            def x_tr(xtb, j, hT, BhT, ci):
                xt, Bxt = xtb
                for half in range(2):
                    pb = half
                    for kk in range(4):
                        k = half * 4 + kk
                        fw.op("tensor", "transpose", dict(out=PS[pb][:, kk * 128:(kk + 1) * 128], in_=xt[:, k * 128:(k + 1) * 128], identity=ident[:, :]),
                              reads=[Bxt, Bc], writes=[PB[pb]], signal=(kk == 3))
                    for kk in range(4):
                        k = half * 4 + kk
                        if True:
                            fw.op("scalar", "activation", dict(out=hT[:, k, j * 128:(j + 1) * 128], in_=PS[pb][:, kk * 128:(kk + 1) * 128],
                                                               func=AF.Identity, scale=gmul[1][:, k, ci:ci + 1], bias=mod[1][:, k, ci:ci + 1]),
                                  reads=[PB[pb], Bc], writes=[BhT])
                        else:
                            fw.op("vector", "tensor_scalar", dict(out=hT[:, k, j * 128:(j + 1) * 128], in0=PS[pb][:, kk * 128:(kk + 1) * 128],
                                                                  scalar1=gmul[1][:, k, ci:ci + 1], scalar2=mod[1][:, k, ci:ci + 1],
                                                                  op0=ALU.mult, op1=ALU.add),
                                  reads=[PB[pb], Bc], writes=[BhT])

            def s1_stage(tl):
                hT, BhT, j, is_ctx = tl["hT"], tl["BhT"], tl["j"], tl["is_ctx"]
                nh0 = 16 if is_ctx else 0
                qk, Bqk = qk_p.get(); scr, Bscr = scr_p.get(); vst, Bvst = vst_p.get()
                tl.update(qk=qk, Bqk=Bqk, scr=scr, Bscr=Bscr, vst=vst, Bvst=Bvst, nh0=nh0)
                groups = ([] if is_ctx else [0, 1, 2, 3]) + [4, 5]
                for gi, g in enumerate(groups):
                    pb = 2 + (gi % 2)
                    for k in range(NK):
                        fw.op("tensor", "matmul", dict(out=PS[pb][:, :], lhsT=hT[:, k, j * 128:(j + 1) * 128], rhs=w1in[:, k, g * 512:(g + 1) * 512],
                                                       start=(k == 0), stop=(k == NK - 1)),
                              reads=[BhT, Bw1], writes=[PB[pb]], signal=(k == NK - 1))
                    if g < 5:
                        fw.op("scalar", "copy", dict(out=qk[:, g * 4:(g + 1) * 4, :].rearrange("p h d -> p (h d)"), in_=PS[pb][:, :]),
                              reads=[PB[pb]], writes=[Bqk])
                        fw.op("scalar", "activation", dict(out=scr[:, g * 4:(g + 1) * 4, :].rearrange("p h d -> p (h d)"), in_=PS[pb][:, :], func=AF.Square),
                              reads=[PB[pb]], writes=[Bscr])
                    else:
                        fw.op("vector", "tensor_copy", dict(out=vst[:, :], in_=PS[pb][:, :]), reads=[PB[pb]], writes=[Bvst])

            def s2_stage(tl):
                qk, Bqk, scr, Bscr, nh0, is_ctx = tl["qk"], tl["Bqk"], tl["scr"], tl["Bscr"], tl["nh0"], tl["is_ctx"]
                nh = 20 - nh0
                hs, Bhs = hs_p.get()
                fw.op("vector", "tensor_reduce", dict(out=hs[:, nh0:20], in_=scr[:, nh0:20, :], axis=AX.X, op=ALU.add),
                      reads=[Bscr], writes=[Bhs])
                fw.op("vector", "tensor_scalar", dict(out=hs[:, nh0:20], in0=hs[:, nh0:20], scalar1=1.0 / 128, scalar2=RMS_EPS, op0=ALU.mult, op1=ALU.add),
                      reads=[Bhs], writes=[Bhs])
                fw.op("scalar", "activation", dict(out=hs[:, nh0:20], in_=hs[:, nh0:20], func=AF.Sqrt), reads=[Bhs], writes=[Bhs])
                fw.op("vector", "reciprocal", dict(out=hs[:, nh0:20], in_=hs[:, nh0:20]), reads=[Bhs], writes=[Bhs])
                fw.op("vector", "tensor_tensor", dict(out=qk[:, nh0:20, :], in0=qk[:, nh0:20, :],
                                                      in1=hs[:, nh0:20].unsqueeze(2).to_broadcast([128, nh, 128]), op=ALU.mult),
                      reads=[Bqk, Bhs], writes=[Bqk])
                qr, Bqr = qr_p.get()
                tl.update(qr=qr, Bqr=Bqr)
                if is_ctx:
                    fw.op("vector", "tensor_tensor", dict(out=qr[:, nh0:20, :], in0=qk[:, nh0:20, :], in1=gqk[:, nh0:20, :], op=ALU.mult),
                          reads=[Bqk, Bgqk], writes=[Bqr])
                    return
                fw.op("vector", "tensor_tensor", dict(out=qk[:, :, :], in0=qk[:, :, :], in1=gqk[:, :, :], op=ALU.mult),
                      reads=[Bqk, Bgqk], writes=[Bqk])
                cs, Bcs = cs_p.get()
                t0 = tl["t0"]
                fw.dma("sync", cs[:, 0, :], cos_d[t0:t0 + 128, :], writes=[Bcs])
                fw.dma("sync", cs[:, 1, :], sin_d[t0:t0 + 128, :], writes=[Bcs])
                qv = qk[:, :, :].rearrange("p h (a b i) -> p h a b i", a=2, b=2)
                qo = qr[:, :, :].rearrange("p h (a b i) -> p h a b i", a=2, b=2)
                x1v = qv[:, :, :, 0, :]; x2v = qv[:, :, :, 1, :]
                Cb = cs[:, 0, :].rearrange("p (a i) -> p a i", a=2).unsqueeze(1).to_broadcast([128, 20, 2, 32])
                Sb = cs[:, 1, :].rearrange("p (a i) -> p a i", a=2).unsqueeze(1).to_broadcast([128, 20, 2, 32])
                t1 = scr[:, 0:10, :].rearrange("p h (a i) -> p (h a) i", a=4).rearrange("p (h a) i -> p h a i", a=2)
                t2 = scr[:, 10:20, :].rearrange("p h (a i) -> p (h a) i", a=4).rearrange("p (h a) i -> p h a i", a=2)
                fw.op("vector", "tensor_tensor", dict(out=t1, in0=x1v, in1=Cb, op=ALU.mult), reads=[Bqk, Bcs], writes=[Bscr])
                fw.op("vector", "tensor_tensor", dict(out=t2, in0=x2v, in1=Sb, op=ALU.mult), reads=[Bqk, Bcs], writes=[Bscr])
                fw.op("vector", "tensor_tensor", dict(out=qo[:, :, :, 0, :], in0=t1, in1=t2, op=ALU.subtract), reads=[Bscr], writes=[Bqr])
                fw.op("vector", "tensor_tensor", dict(out=t1, in0=x1v, in1=Sb, op=ALU.mult), reads=[Bqk, Bcs], writes=[Bscr])
                fw.op("vector", "tensor_tensor", dict(out=t2, in0=x2v, in1=Cb, op=ALU.mult), reads=[Bqk, Bcs], writes=[Bscr])
                fw.op("vector", "tensor_tensor", dict(out=qo[:, :, :, 1, :], in0=t1, in1=t2, op=ALU.add), reads=[Bscr], writes=[Bqr])

            def s3_stage(tl):
                qr, Bqr, nh0, is_ctx = tl["qr"], tl["Bqr"], tl["nh0"], tl["is_ctx"]
                qst, Bqst = qst_p.get()
                hgroups = [(16, 20, 4)] if is_ctx else [(0, 8, 4), (8, 16, 5), (16, 20, 4)]
                for gi, (h0, h1, pb) in enumerate(hgroups):
                    ptv = PS[pb].bitcast(BF16)
                    for h in range(h0, h1):
                        fw.op("tensor", "transpose", dict(out=ptv[:, (h - h0) * 128:(h - h0 + 1) * 128], in_=qr[:, h, :], identity=identb[:, :]),
                              reads=[Bqr, Bc], writes=[PB[pb]], signal=(h == h1 - 1))
                    nw = (h1 - h0) * 128
                    fw.op("scalar", "copy", dict(out=qst[:, h0:h1, :].rearrange("p h t -> p (h t)"), in_=ptv[:, 0:nw]),
                          reads=[PB[pb]], writes=[Bqst])
                t0, key0 = tl["t0"], tl["key0"]
                if not is_ctx:
                    fw.dma("gpsimd", qT_d[:, :, t0:t0 + 128].rearrange("h p t -> p h t"), qst[:, 0:16, :], reads=[Bqst])
                fw.dma("gpsimd", kT_d[:, :, key0:key0 + 128].rearrange("h p t -> p h t"), qst[:, 16:20, :], reads=[Bqst])
                fw.dma("gpsimd", V_d[key0:key0 + 128, :], tl["vst"][:, :], reads=[tl["Bvst"]])

            def z_stage(hT, BhT, s0):
                for c4 in range(4):
                    szst, Bszst = szst_p.get()
                    for cc in range(4):
                        c = c4 * 4 + cc
                        pb = 6 + (c % 2)
                        wz, Bwz = wz_p.get()
                        fw.dma("sync", wz[:, :, :], w1in_bf[:, :, 3072 + c * 128:3072 + (c + 1) * 128], writes=[Bwz])
                        for k in range(NK):
                            fw.op("tensor", "matmul", dict(out=PS[pb][:, :], lhsT=wz[:, k, :], rhs=hT[:, k, :], start=(k == 0), stop=(k == NK - 1)),
                                  reads=[BhT, Bwz], writes=[PB[pb]], signal=(k == NK - 1))
                        fw.op("scalar", "activation", dict(out=szst[:, cc, :], in_=PS[pb][:, :], func=AF.Silu),
                              reads=[PB[pb]], writes=[Bszst])
                    fw.dma("gpsimd", szT_d[c4 * 4:(c4 + 1) * 4, :, s0:s0 + 512].rearrange("c p t -> p c t"), szst[:, :, :], reads=[Bszst])

            blocks = [(ctx1_d, 0, CTX, 1, True, 0)] + [(x1_d, blk * 512, 512, 0, False, CTX + blk * 512) for blk in range(L // 512)]
            tiles = []
            for bi, (src, s0, NT, ci, is_ctx, key0) in enumerate(blocks):
                for j in range(NT // 128):
                    tiles.append(dict(bi=bi, j=j, first=(j == 0), last=(j == NT // 128 - 1), is_ctx=is_ctx,
                                      t0=s0 + j * 128, key0=key0 + j * 128))
            hTs = {}
            for bi in (0, 1):
                src, s0, NT, ci, is_ctx, key0 = blocks[bi]
                xts = x_pre(src, s0, NT)
                hTs[bi] = hT_p.get()
                for j in range(NT // 128):
                    x_tr(xts[j], j, hTs[bi][0], hTs[bi][1], ci)
            nxt = None
            for gidx in range(len(tiles) + 2):
                if gidx < len(tiles):
                    tl = tiles[gidx]
                    bi = tl["bi"]
                    src, s0, NT, ci, is_ctx, key0 = blocks[bi]
                    if tl["first"] and bi >= 1 and bi + 1 < len(blocks):
                        nsrc, ns0, nNT, nci, _, _ = blocks[bi + 1]
                        nxt = (x_pre(nsrc, ns0, nNT), nci)
                        hTs[bi + 1] = hT_p.get()
                    tl["hT"], tl["BhT"] = hTs[bi]
                    if gidx >= 1:
                        s2_stage(tiles[gidx - 1])
                    s1_stage(tl)
                    if gidx >= 2:
                        s3_stage(tiles[gidx - 2])
                    if bi >= 1 and bi + 1 < len(blocks):
                        x_tr(nxt[0][tl["j"]], tl["j"], hTs[bi + 1][0], hTs[bi + 1][1], nxt[1])
                else:
                    if gidx == len(tiles):
                        s2_stage(tiles[gidx - 1])
                    s3_stage(tiles[gidx - 2])
                if gidx < len(tiles) and tiles[gidx]["last"] and not tiles[gidx]["is_ctx"]:
                    bi = tiles[gidx]["bi"]
                    z_stage(hTs[bi][0], hTs[bi][1], blocks[bi][1])
            fw.barrier()
            fw.flush()

        with contextlib.ExitStack() as ph:
            kT_p = Pool(nc, ph, "kTh", [128, NKEY], BF16, 2)
            V_p = Pool(nc, ph, "Vh", [128, NKC, 128], BF16, 2)
            q_p = Pool(nc, ph, "qblk", [128, 512], BF16, 3)
            szb_p = Pool(nc, ph, "szblk", [128, 512], BF16, 3)
            p_p = Pool(nc, ph, "pT", [128, 1024], BF16, 8)
            accA_p = Pool(nc, ph, "accA", [128, 1024], BF16, 2)
            accB_p = Pool(nc, ph, "accB", [128, 1024], BF16, 2)
            rden_p = Pool(nc, ph, "rden", [128, 512], F32, 2)
            o_p = Pool(nc, ph, "osb", [128, 512], F32, 2)
            w_p = Pool(nc, ph, "wsb", [128, 512], BF16, 3)
            scale = 128.0 ** -0.5
            NQB = L // 512
            NT2 = NKC // 2
            assert NKC % 2 == 0 and NT2 >= 3
            SW = [0, 1, 2]
            BSW = [Buf(), Buf(), Buf()]
            srr = [0]

            def s_get():
                i = srr[0] % 3
                srr[0] += 1
                return PSW[SW[i]], BSW[i]
            LAG = 2

            def make_unit(kTh, BkT, Vh, BV, h, qb, po):
                qblk, Bq = q_p.get(); szb, Bszb = szb_p.get()
                fw.dma("sync", qblk[:, :], qT_d[h, :, qb * 512:(qb + 1) * 512], writes=[Bq])
                fw.dma("sync", szb[:, :], szT_d[h, :, qb * 512:(qb + 1) * 512], writes=[Bszb])
                accA, BaA = accA_p.get()
                pts = {}

                def s_stage(t):
                    st_, Bst_ = s_get()
                    for i in range(2):
                        kc = 2 * t + i
                        fw.op("tensor", "matmul", dict(out=st_[:, i * 512:(i + 1) * 512], lhsT=kTh[:, kc * 128:(kc + 1) * 128], rhs=qblk[:, :],
                                                       start=True, stop=True),
                              reads=[BkT, Bq], writes=[Bst_], signal=(i == 1))
                    pT, BpT = p_p.get()
                    fw.op("scalar", "activation", dict(out=pT[:, :], in_=st_[:, :], func=AF.Exp, scale=scale),
                          reads=[Bst_], writes=[BpT])
                    pts[t] = (pT, BpT)
                    if t == 1:
                        p0, Bp0 = pts[0]
                        fw.op("vector", "tensor_tensor", dict(out=accA[:, :], in0=p0[:, :], in1=pT[:, :], op=ALU.add),
                              reads=[Bp0, BpT], writes=[BaA])
                    elif t > 1:
                        fw.op("vector", "tensor_tensor", dict(out=accA[:, :], in0=accA[:, :], in1=pT[:, :], op=ALU.add),
                              reads=[BpT, BaA], writes=[BaA])

                def pv_stage(t):
                    pT, BpT = pts.pop(t)
                    for i in range(2):
                        kc = 2 * t + i
                        fw.op("tensor", "matmul", dict(out=PS[po][:, :], lhsT=Vh[:, kc, :], rhs=pT[:, i * 512:(i + 1) * 512],
                                                       start=(kc == 0), stop=(kc == NKC - 1)),
                              reads=[BV, BpT], writes=[PB[po]], signal=(i == 1))

                def finalize():
                    pdt, Bpd = s_get()
                    for i in range(2):
                        fw.op("tensor", "matmul", dict(out=pdt[:, 0:512], lhsT=onesb[:, :], rhs=accA[:, i * 512:(i + 1) * 512],
                                                       start=(i == 0), stop=(i == 1)),
                              reads=[Bc, BaA], writes=[Bpd], signal=(i == 1))
                    rden, Brd = rden_p.get(); osb, Bo = o_p.get(); wsb, Bw = w_p.get()
                    fw.op("vector", "reciprocal", dict(out=rden[:, :], in_=pdt[:, 0:512]), reads=[Bpd], writes=[Brd])
                    fw.op("vector", "tensor_tensor", dict(out=osb[:, :], in0=PS[po][:, :], in1=rden[:, :], op=ALU.mult),
                          reads=[PB[po], Brd], writes=[Bo])
                    fw.op("gpsimd", "tensor_tensor", dict(out=wsb[:, :], in0=osb[:, :], in1=szb[:, :], op=ALU.mult),
                          reads=[Bo, Bszb], writes=[Bw])
                    fw.dma("gpsimd", wT_d[h, :, qb * 512:(qb + 1) * 512], wsb[:, :], reads=[Bw])
                return s_stage, pv_stage, finalize

            unit = 0
            prev = None
            for hk in range(NKV):
                kTh, BkT = kT_p.get(); Vh, BV = V_p.get()
                fw.dma("sync", kTh[:, :], kT_d[hk], writes=[BkT])
                fw.dma("sync", Vh[:, :, :], V_d[:, hk * 128:(hk + 1) * 128].rearrange("(c p) d -> p c d", p=128), writes=[BV])
                for g in range(4):
                    h = hk * 4 + g
                    for qb in range(NQB):
                        cur = make_unit(kTh, BkT, Vh, BV, h, qb, 6 + (unit % 2))
                        unit += 1
                        for t in range(NT2):
                            cur[0](t)
                            if t >= LAG:
                                cur[1](t - LAG)
                            elif prev is not None:
                                prev[1](NT2 - LAG + t)
                                if t == LAG - 1:
                                    prev[2]()
                        prev = cur
            for t in range(LAG):
                prev[1](NT2 - LAG + t)
            prev[2]()
            fw.barrier()
            fw.flush()

        with contextlib.ExitStack() as ph:
            w1out = T(ph, "w1out", [128, NCH, D], BF16); Bw = Buf()
            fw.dma("sync", w1out[:], w1out_bf, writes=[Bw])
            gate1 = T(ph, "gate1", [128, D], F32); fng = T(ph, "fng", [128, D], F32); Bg = Buf()
            fw.dma("sync", gate1[:], modrow[1, 0, 2 * D:3 * D].partition_broadcast(128), writes=[Bg])
            fw.dma("sync", fng[:], final_norm_g.partition_broadcast(128), writes=[Bg])
            wt_p = Pool(nc, ph, "wTt", [128, NCH, 512], BF16, 2)
            xr_p = Pool(nc, ph, "xr3", [128, D], F32, 3)
            xo_p = Pool(nc, ph, "xo3", [128, D], F32, 3)
            junk = T(ph, "junk3", [128, D], BF16); Bjunk = Buf()
            ss_p = Pool(nc, ph, "ss3", [128, 1], F32, 3)
            for blk in range(L // 512):
                wt, Bwt = wt_p.get()
                fw.dma("sync", wt[:, :, :], wT_d[:, :, blk * 512:(blk + 1) * 512].rearrange("h p t -> p h t"), writes=[Bwt])
                for j in range(4):
                    t0 = blk * 512 + j * 128
                    xr, Bxr = xr_p.get(); xo, Bxo = xo_p.get(); ss, Bss = ss_p.get()
                    fw.dma("sync", xr[:, :], x1_d[t0:t0 + 128, :], writes=[Bxr])
                    for half in range(2):
                        pb = (j % 2) * 2 + half
                        for c in range(NCH):
                            fw.op("tensor", "matmul", dict(out=PS[pb][:, :], lhsT=wt[:, c, j * 128:(j + 1) * 128], rhs=w1out[:, c, half * 512:(half + 1) * 512], start=(c == 0), stop=(c == NCH - 1)),
                                reads=[Bwt, Bw], writes=[PB[pb]], signal=(c == NCH - 1))
                        fw.op("vector", "tensor_tensor", dict(out=xo[:, half * 512:(half + 1) * 512], in0=PS[pb][:, :], in1=gate1[:, half * 512:(half + 1) * 512], op=ALU.mult),
                            reads=[PB[pb], Bg], writes=[Bxo])
                    fw.op("gpsimd", "tensor_tensor", dict(out=xo[:, :], in0=xo[:, :], in1=xr[:, :], op=ALU.add),
                          reads=[Bxo, Bxr], writes=[Bxo])
                    fw.op("scalar", "activation", dict(out=junk[:, :], in_=xo[:, :], func=AF.Square, accum_out=ss[:, 0:1]),
                          reads=[Bxo], writes=[Bjunk, Bss])
                    fw.op("vector", "tensor_scalar", dict(out=ss[:, :], in0=ss[:, :], scalar1=1.0 / D, scalar2=RMS_EPS, op0=ALU.mult, op1=ALU.add),
                          reads=[Bss], writes=[Bss])
                    fw.op("gpsimd", "tensor_tensor", dict(out=ss[:, :], in0=ss[:, :], in1=mhalf[:, 0:1], op=ALU.pow),
                          reads=[Bss, Bc], writes=[Bss])
                    fw.op("vector", "scalar_tensor_tensor", dict(out=xo[:, :], in0=xo[:, :], scalar=ss[:, 0:1], in1=fng[:, :], op0=ALU.mult, op1=ALU.mult),
                          reads=[Bxo, Bss, Bg], writes=[Bxo])
                    fw.dma("gpsimd", out_d[t0:t0 + 128, :], xo[:, :], reads=[Bxo])
            fw.barrier()
            fw.flush()
    return nc


def _rope_tables(L):
    rows = L // 64
    row = np.broadcast_to(np.arange(rows)[:, None], (rows, 64)).reshape(-1).astype(np.float32)
    col = np.broadcast_to(np.arange(64)[None, :], (rows, 64)).reshape(-1).astype(np.float32)
    inv_freq = (np.float32(10000.0) ** (-np.arange(0, 64, 2, dtype=np.float32) / np.float32(64))).astype(np.float32)
    ang = np.concatenate([row[:, None] * inv_freq[None, :], col[:, None] * inv_freq[None, :]], axis=1).astype(np.float32)
    return np.cos(ang).astype(np.float32), np.sin(ang).astype(np.float32)


_CACHE = {}


def make_in_maps(inputs, L, ncores):
    cos, sin = _rope_tables(L)
    ident = np.eye(128, dtype=np.float32)
    maps = []
    f = lambda a: np.ascontiguousarray(np.asarray(a, dtype=np.float32))
    shared = {k: f(v) for k, v in inputs.items() if k not in ("x", "c", "ctx")}
    for b in range(ncores):
        m = dict(shared)
        m["x"] = f(inputs["x"][b]); m["c"] = f(inputs["c"][b]); m["ctx"] = f(inputs["ctx"][b])
        m["ident"] = ident; m["rope_cos"] = cos; m["rope_sin"] = sin
        maps.append(m)
    return maps


def kernel(**inputs):
    x = np.asarray(inputs["x"])
    B, L, _ = x.shape
    key = (L,)
    if key not in _CACHE:
        _CACHE[key] = build_program(L)
    nc = _CACHE[key]
    maps = make_in_maps(inputs, L, B)
    res = run_bass_kernel_spmd(nc, maps, core_ids=list(range(B)))
    return np.stack([np.asarray(r["out"], dtype=np.float32) for r in res.results], axis=0)
```

```python
import contextlib
import numpy as np
import concourse.bass as bass
import concourse.mybir as mybir
from concourse.bass_utils import run_bass_kernel_spmd

F32 = mybir.dt.float32
BF16 = mybir.dt.bfloat16
ALU = mybir.AluOpType
AF = mybir.ActivationFunctionType
AX = mybir.AxisListType

D = 1024
E = 2048
NK = 8
NCH = 16
CW = 31
HALO = 15
CTX = 256
NQH = 16
NKV = 4
KVD = 512
W1C = 2 * E + 2 * KVD
RMS_EPS = 1e-6
LN_EPS = 1e-5
SEQ = 8192
NCORES = 8

ENGS = ["tensor", "vector", "scalar", "gpsimd", "sync"]
N_DMA_SEMS = {"sync": 14, "vector": 0, "scalar": 4, "gpsimd": 8, "tensor": 0}


class Buf:
    __slots__ = ("name", "last_write", "reads")

    def __init__(self, name=""):
        self.name = name
        self.last_write = None
        self.reads = []


class EngState:
    def __init__(self, name):
        self.name = name
        self.n = 0
        self.seen = {}
        self.queue = []
        self.dma_sems = []
        self.dma_rr = 0


class FW:
    def __init__(self, nc, stack):
        self.nc = nc
        self.sems = {}
        self.E = {}
        for e in ENGS:
            self.sems[f"tl_{e}"] = stack.enter_context(nc.semaphore(f"tl_{e}"))
            self.E[e] = EngState(e)
            for i in range(N_DMA_SEMS[e]):
                k = f"dq_{e}_{i}"
                self.sems[k] = stack.enter_context(nc.semaphore(k))
                self.E[e].dma_sems.append([k, 0])

    def _collect(self, eng, reads, writes):
        deps = {}

        def add(ev):
            if ev is None:
                return
            k, v = ev
            if deps.get(k, 0) < v:
                deps[k] = v
        for b in reads:
            add(b.last_write)
        for b in writes:
            add(b.last_write)
            for r in b.reads:
                add(r)
        st = self.E[eng]
        waits = []
        for k, v in deps.items():
            if eng == "tensor" and k == "tl_tensor":
                continue
            if st.seen.get(k, 0) >= v:
                continue
            st.seen[k] = v
            waits.append((k, v))
        return waits

    def _post(self, ev, reads, writes):
        for b in reads:
            b.reads.append(ev)
            if len(b.reads) > 64:
                b.reads = b.reads[-48:]
        for b in writes:
            b.last_write = ev
            b.reads = []

    def op(self, eng, name, kw, reads=(), writes=(), signal=True, args=()):
        fn = (lambda e, name=name, args=args, kw=kw: getattr(e, name)(*args, **kw))
        st = self.E[eng]
        waits = self._collect(eng, reads, writes)
        if signal:
            st.n += 1
            ev = (f"tl_{eng}", st.n)
        else:
            assert eng == "tensor"
            ev = (f"tl_{eng}", st.n + 1)
        st.queue.append((waits, fn, (f"tl_{eng}", 1) if signal else None))
        self._post(ev, reads, writes)
        return ev

    def dma(self, eng, out, in_, reads=(), writes=(), **kw):
        st = self.E[eng]
        slot = st.dma_sems[st.dma_rr % len(st.dma_sems)]
        st.dma_rr += 1
        k = slot[0]
        waits = self._collect(eng, reads, writes)
        if slot[1] > 0 and st.seen.get(k, 0) < slot[1]:
            st.seen[k] = slot[1]
            waits.append((k, slot[1]))
        slot[1] += 16
        ev = (k, slot[1])
        st.queue.append((waits, (lambda e, o=out, i=in_, kw=kw: e.dma_start(out=o, in_=i, **kw)), (k, 16)))
        self._post(ev, reads, writes)
        return ev

    def barrier(self):
        targets = {}
        for e in ENGS:
            st = self.E[e]
            if st.n > 0:
                targets[f"tl_{e}"] = st.n
            for k, c in st.dma_sems:
                if c > 0:
                    targets[k] = c
        for e in ENGS:
            st = self.E[e]
            waits = []
            for k, v in targets.items():
                if st.seen.get(k, 0) >= v:
                    continue
                st.seen[k] = v
                waits.append((k, v))
            if waits:
                st.queue.append((waits, None, None))

    def flush(self):
        nc = self.nc
        sems = self.sems
        with nc.Block() as block:
            def mk(e):
                st = self.E[e]

                def body(engine):
                    for waits, fn, inc in st.queue:
                        for k, v in waits:
                            engine.wait_ge(sems[k], v)
                        if fn is not None:
                            ins = fn(engine)
                            if inc is not None:
                                ins.then_inc(sems[inc[0]], inc[1])
                    st.queue = []
                return body
            block.tensor(mk("tensor"))
            block.vector(mk("vector"))
            block.scalar(mk("scalar"))
            block.gpsimd(mk("gpsimd"))
            block.sync(mk("sync"))


class Pool:
    def __init__(self, nc, stack, name, shape, dtype, n, psum=False):
        self.items = []
        for i in range(n):
            mk = nc.psum_tensor if psum else nc.sbuf_tensor
            t = stack.enter_context(mk(f"pl_{name}{i}", shape, dtype))
            self.items.append((t, Buf(f"{name}{i}")))
        self.i = 0

    def get(self):
        it = self.items[self.i % len(self.items)]
        self.i += 1
        return it


def build_program(L=SEQ, debug=False):
    assert L % 512 == 0
    NKEY = CTX + L
    NKC = NKEY // 128
    nc = bass.Bass("TRN2", target_bir_lowering=False)

    def din(name, shape):
        return nc.dram_tensor(name, list(shape), F32, kind="ExternalInput").ap()

    x_d = din("x", [L, D]); c_d = din("c", [D]); ctx_d = din("ctx", [CTX, D]); cctx_d = din("c_ctx", [D])
    l0_norm_g = din("l0_norm_g", [D]); l0_ada_w = din("l0_ada_w", [D, 3 * D]); l0_ada_b = din("l0_ada_b", [3 * D])
    l0_w_in = din("l0_w_in", [D, 3 * E]); l0_b_in = din("l0_b_in", [3 * E]); l0_dw_w = din("l0_dw_w", [CW, E])
    l0_dw_b = din("l0_dw_b", [E]); l0_ln_g = din("l0_ln_g", [E]); l0_ln_b = din("l0_ln_b", [E])
    l0_w_out = din("l0_w_out", [E, D]); l0_b_out = din("l0_b_out", [D])
    l1_norm_g = din("l1_norm_g", [D]); l1_ada_w = din("l1_ada_w", [D, 3 * D]); l1_ada_b = din("l1_ada_b", [3 * D])
    l1_w_in = din("l1_w_in", [D, W1C]); l1_q_norm_g = din("l1_q_norm_g", [128]); l1_k_norm_g = din("l1_k_norm_g", [128])
    l1_w_out = din("l1_w_out", [E, D]); final_norm_g = din("final_norm_g", [D])
    ident_d = din("ident", [128, 128]); cos_d = din("rope_cos", [L, 64]); sin_d = din("rope_sin", [L, 64])
    out_d = nc.dram_tensor("out", [L, D], F32, kind="ExternalOutput").ap()

    def dscr(name, shape, dt):
        return nc.dram_tensor(name, list(shape), dt).ap()

    w0in_bf = dscr("w0in_bf", [NCH, 128, NK, 3, 128], BF16)
    w0out_bf = dscr("w0out_bf", [128, NCH, D], BF16)
    diag_bf = dscr("diag_bf", [NCH, 128, CW * 128], BF16)
    w1in_bf = dscr("w1in_bf", [128, NK, W1C], BF16)
    w1out_bf = dscr("w1out_bf", [128, NCH, D], BF16)
    modrow = dscr("modrow", [2, 2, 3 * D], F32)
    if debug:
        x1_d = nc.dram_tensor("x1", [L, D], F32, kind="ExternalOutput").ap()
        ctx1_d = nc.dram_tensor("ctx1", [CTX, D], F32, kind="ExternalOutput").ap()
    else:
        x1_d = dscr("x1", [L, D], F32)
        ctx1_d = dscr("ctx1", [CTX, D], F32)
    qT_d = dscr("qT", [NQH, 128, L], BF16)
    kT_d = dscr("kT", [NKV, 128, NKEY], BF16)
    V_d = dscr("V", [NKEY, KVD], BF16)
    szT_d = dscr("szT", [NQH, 128, L], BF16)
    wT_d = dscr("wT", [NQH, 128, L], BF16)

    with contextlib.ExitStack() as top:
        fw = FW(nc, top)

        def T(stack, name, shape, dt):
            return stack.enter_context(nc.sbuf_tensor("sb_" + name, list(shape), dt))

        ident = T(top, "ident", [128, 128], F32); identb = T(top, "identb", [128, 128], BF16)
        onesb = T(top, "onesb", [128, 128], BF16)
        g0c = T(top, "g0c", [128, NK], F32); g1c = T(top, "g1c", [128, NK], F32)
        binc = T(top, "binc", [128, 3 * NCH], F32); dwT = T(top, "dwT", [128, NCH, CW], BF16)
        binh = T(top, "binh", [128, NCH], F32)
        dwbc = T(top, "dwbc", [128, NCH], F32); lngc = T(top, "lngc", [128, NCH], F32); lnbc = T(top, "lnbc", [128, NCH], F32)
        mod = [T(top, f"mod{l}", [128, 24, 2], F32) for l in range(2)]
        gmul = [T(top, f"gmul{l}", [128, NK, 2], F32) for l in range(2)]
        epsc = T(top, "epsc", [128, 2], F32)
        mhalf = T(top, "mhalf", [128, 512], F32)
        Bc = Buf("consts")
        PSALL = top.enter_context(nc.psum_tensor("psall", [128, 4096], F32))
        PSW = [PSALL[:, i * 1024:(i + 1) * 1024] for i in range(4)]
        PS = [PSALL[:, i * 512:(i + 1) * 512] for i in range(8)]
        PB = [Buf(f"ps{i}") for i in range(8)]

        with contextlib.ExitStack() as ph:
            stage = Pool(nc, ph, "stage", [48, 128], F32, 2)
            dwr = T(ph, "dwr", [CW, E], F32); Bdwr = Buf()
            adaw = T(ph, "adaw", [128, NK, 3 * D], F32); Badaw = Buf()
            craw = T(ph, "craw", [128, 2, NK], F32); condT = T(ph, "condT", [128, 2, NK], F32); Bcond = Buf()
            adabc = T(ph, "adabc", [128, 24], F32)
            adabr = T(ph, "adabr", [2, 3 * D], F32); rowt = T(ph, "rowt", [2, 3 * D], F32); Brow = Buf(); Badabr = Buf()

            fw.dma("sync", ident[:], ident_d, writes=[Bc])
            fw.op("vector", "tensor_copy", dict(out=identb[:], in_=ident[:]), reads=[Bc], writes=[Bc])
            fw.op("vector", "memset", dict(ap=onesb[:], constant=1.0), writes=[Bc])
            fw.op("vector", "memset", dict(ap=mhalf[:, :], constant=-0.5), writes=[Bc])
            fw.op("vector", "memset", dict(ap=epsc[:, 0:1], constant=RMS_EPS), writes=[Bc])
            fw.op("vector", "memset", dict(ap=epsc[:, 1:2], constant=LN_EPS), writes=[Bc])

            def to_cols(vec, n, dst):
                st_t, st_b = stage.get()
                fw.dma("sync", st_t[0:n, :], vec.rearrange("(n p) -> n p", p=128), writes=[st_b])
                fw.op("tensor", "transpose", dict(out=PS[0][:, 0:n], in_=st_t[0:n, :], identity=ident[0:n, 0:n]),
                      reads=[st_b, Bc], writes=[PB[0]])
                fw.op("vector", "tensor_copy", dict(out=dst, in_=PS[0][:, 0:n]), reads=[PB[0]], writes=[Bc])

            to_cols(l0_norm_g, NK, g0c[:]); to_cols(l1_norm_g, NK, g1c[:]); to_cols(l0_b_in, 3 * NCH, binc[:])
            to_cols(l0_dw_b, NCH, dwbc[:]); to_cols(l0_ln_g, NCH, lngc[:]); to_cols(l0_ln_b, NCH, lnbc[:])
            fw.op("vector", "tensor_scalar", dict(out=binh[:], in0=binc[:, NCH:2 * NCH], scalar1=0.5, scalar2=None, op0=ALU.mult), reads=[Bc], writes=[Bc])
            fw.dma("sync", dwr[:], l0_dw_w, writes=[Bdwr])
            for cch in range(NCH):
                fw.op("tensor", "transpose", dict(out=PS[1][:, cch * CW:(cch + 1) * CW], in_=dwr[:, cch * 128:(cch + 1) * 128], identity=ident[0:CW, 0:CW]),
                      reads=[Bdwr, Bc], writes=[PB[1]], signal=(cch == NCH - 1))
            fw.op("vector", "tensor_copy", dict(out=dwT[:].rearrange("p c k -> p (c k)"), in_=PS[1][:, 0:NCH * CW]),
                  reads=[PB[1]], writes=[Bc])
            fw.dma("sync", craw[:, 0, :], c_d.rearrange("(p k) -> p k", k=NK), writes=[Bcond])
            fw.dma("sync", craw[:, 1, :], cctx_d.rearrange("(p k) -> p k", k=NK), writes=[Bcond])
            fw.op("scalar", "activation", dict(out=condT[:], in_=craw[:], func=AF.Silu), reads=[Bcond], writes=[Bcond])
            for l, (aw, ab, gc) in enumerate([(l0_ada_w, l0_ada_b, g0c), (l1_ada_w, l1_ada_b, g1c)]):
                for h in range(2):
                    fw.dma("sync", adaw[:, h * 4:(h + 1) * 4, :], aw.rearrange("(p k) n -> p k n", k=NK)[:, h * 4:(h + 1) * 4, :],
                           writes=[Badaw])
                fw.dma("sync", adabr[:], ab.partition_broadcast(2), writes=[Badabr])
                to_cols(ab, 24, adabc[:])
                for j in range(24):
                    for k in range(NK):
                        fw.op("tensor", "matmul", dict(out=PS[2][:, 2 * j:2 * j + 2], lhsT=adaw[:, k, j * 128:(j + 1) * 128], rhs=condT[:, :, k], start=(k == 0), stop=(k == NK - 1)),
                              reads=[Badaw, Bcond], writes=[PB[2]], signal=(j == 23 and k == NK - 1))
                fw.op("vector", "tensor_tensor", dict(out=mod[l][:], in0=PS[2][:, 0:48].rearrange("p (j c) -> p j c", c=2), in1=adabc[:].unsqueeze(2).to_broadcast([128, 24, 2]), op=ALU.add),
                      reads=[PB[2], Bc], writes=[Bc])
                fw.op("vector", "scalar_tensor_tensor", dict(out=gmul[l][:], in0=mod[l][:, 8:16, :], scalar=1.0, in1=gc[:].unsqueeze(2).to_broadcast([128, NK, 2]), op0=ALU.add, op1=ALU.mult),
                      reads=[Bc], writes=[Bc])
                for n in range(6):
                    pb = 3 + (n % 2)
                    for k in range(NK):
                        fw.op("tensor", "matmul", dict(out=PS[pb][0:2, :], lhsT=condT[:, :, k], rhs=adaw[:, k, n * 512:(n + 1) * 512], start=(k == 0), stop=(k == NK - 1)),
                              reads=[Badaw, Bcond], writes=[PB[pb]], signal=(k == NK - 1))
                    fw.op("vector", "tensor_tensor", dict(out=rowt[:, n * 512:(n + 1) * 512], in0=PS[pb][0:2, :], in1=adabr[:, n * 512:(n + 1) * 512], op=ALU.add),
                          reads=[PB[pb], Badabr], writes=[Brow])
                fw.dma("sync", modrow[l], rowt[:], reads=[Brow])
            fw.barrier()
            fw.flush()

        with contextlib.ExitStack() as ph:
            slab = Pool(nc, ph, "slab", [128, 3 * E], F32, 2)
            slabb = Pool(nc, ph, "slabb", [128, 3 * E], BF16, 2)
            cast_i = [0]

            def cast(dst, src, rb, wb):
                eng = ["vector", "gpsimd", "scalar"][cast_i[0] % 3]
                cast_i[0] += 1
                if eng == "scalar":
                    fw.op(eng, "copy", dict(out=dst, in_=src), reads=[rb], writes=[wb])
                else:
                    fw.op(eng, "tensor_copy", dict(out=dst, in_=src), reads=[rb], writes=[wb])

            for c in range(NCH):
                t, tb = slabb.get()
                fw.op("vector", "tensor_tensor", dict(out=t[:, 0:CW * 128].rearrange("p (k j) -> p k j", j=128),
                                                      in0=identb[:].unsqueeze(1).to_broadcast([128, CW, 128]),
                                                      in1=dwT[:, c, :].unsqueeze(2).to_broadcast([128, CW, 128]), op=ALU.mult),
                      reads=[Bc], writes=[tb])
                fw.dma("gpsimd", diag_bf[c], t[:, 0:CW * 128], reads=[tb])
            for k in range(NK):
                s, sb = slab.get(); t, tb = slabb.get()
                fw.dma("sync", s[:, :], l0_w_in[k * 128:(k + 1) * 128, :], writes=[sb])
                for h in range(3):
                    cast(t[:, h * E:(h + 1) * E], s[:, h * E:(h + 1) * E], sb, tb)
                for tt in range(3):
                    fw.dma("gpsimd", w0in_bf[:, :, k, tt, :].rearrange("c p j -> p c j"),
                           t[:, tt * E:(tt + 1) * E].rearrange("p (c j) -> p c j", j=128), reads=[tb])
            for k in range(NK):
                s, sb = slab.get(); t, tb = slabb.get()
                fw.dma("sync", s[:, 0:W1C], l1_w_in[k * 128:(k + 1) * 128, :], writes=[sb])
                for h in range(2):
                    cast(t[:, h * 2560:(h + 1) * 2560], s[:, h * 2560:(h + 1) * 2560], sb, tb)
                fw.dma("gpsimd", w1in_bf[:, k, :], t[:, 0:W1C], reads=[tb])
            for wsrc, wdst in ((l0_w_out, w0out_bf), (l1_w_out, w1out_bf)):
                for i in range(4):
                    s, sb = slab.get(); t, tb = slabb.get()
                    fw.dma("sync", s[:, 0:4096].rearrange("p (c n) -> p c n", n=D),
                           wsrc.rearrange("(c p) n -> p c n", p=128)[:, 4 * i:4 * i + 4, :], writes=[sb])
                    for h in range(2):
                        cast(t[:, h * 2048:(h + 1) * 2048], s[:, h * 2048:(h + 1) * 2048], sb, tb)
                    fw.dma("gpsimd", wdst[:, 4 * i:4 * i + 4, :], t[:, 0:4096].rearrange("p (c n) -> p c n", n=D), reads=[tb])
            fw.barrier()
            fw.flush()

        with contextlib.ExitStack() as ph:
            w0out = T(ph, "w0out", [128, NCH, D], BF16); Bw0out = Buf()
            gate_t = T(ph, "gate_b", [128, D], F32)
            gbo_t = T(ph, "gbo_b", [128, D], F32)
            gate_b = [gate_t, gate_t]; gbo_b = [gbo_t, gbo_t]
            Bg = Buf()
            fw.dma("sync", w0out[:], w0out_bf, writes=[Bw0out])

            def set_gate(ci):
                fw.dma("sync", gate_t[:], modrow[0, ci, 2 * D:3 * D].partition_broadcast(128), writes=[Bg])
                fw.dma("sync", gbo_t[:], l0_b_out.partition_broadcast(128), writes=[Bg])
                fw.op("vector", "tensor_tensor", dict(out=gbo_t[:], in0=gbo_t[:], in1=gate_t[:], op=ALU.mult), reads=[Bg], writes=[Bg])
            xt_p = Pool(nc, ph, "xt", [128, D], F32, 5)
            ssq = Pool(nc, ph, "ssq", [128, 8], F32, 2)
            hT_p = Pool(nc, ph, "hT", [128, NK, 512 + 2 * HALO], BF16, 2)
            wch_p = Pool(nc, ph, "wch", [128, NK, 3, 128], BF16, 2)
            diag_p = Pool(nc, ph, "diag", [128, CW, 128], BF16, 2)
            sg_p = Pool(nc, ph, "sg", [128, 512 + 2 * HALO], F32, 2)
            v_p = Pool(nc, ph, "v", [128, 512 + 2 * HALO], BF16, 2)
            co = T(ph, "co", [128, NCH, 512], F32); Bco = [Buf() for _ in range(NCH)]
            cob_p = Pool(nc, ph, "cob", [128, 512], BF16, 3)
            sqb_p = Pool(nc, ph, "sqb", [128, 512], BF16, 3)
            sz = T(ph, "sz", [128, NCH, 512], BF16); Bsz = [Buf() for _ in range(NCH)]
            wg = T(ph, "wg", [128, NCH, 512], BF16); Bwg = [Buf() for _ in range(NCH)]
            mean = T(ph, "mean", [128, 512], F32); rstd = T(ph, "rstd", [128, 512], F32); Bst = Buf()
            tmp_p = Pool(nc, ph, "tmpn", [128, 512], F32, 2)
            sil_p = Pool(nc, ph, "sil", [128, 512], BF16, 5)
            xr_p = Pool(nc, ph, "xr", [128, D], F32, 1)
            xo_p = Pool(nc, ph, "xo", [128, D], F32, 1)

            cur_gate = [None]

            def l0_block(src, dst, s0, NT, ci, total):
                left_pad = (s0 == 0)
                right_pad = (s0 + NT == total)
                ntile = NT // 128
                nt1 = ntile + 1
                WT = NT + 2 * HALO
                S = {}
                hT, BhT = hT_p.get()

                def front_load():
                    xts = []
                    for j in range(nt1):
                        xt, Bxt = xt_p.get()
                        xts.append((xt, Bxt))
                        if j < ntile:
                            fw.dma("sync", xt[:, :], src[s0 + j * 128:s0 + (j + 1) * 128, :], writes=[Bxt])
                        else:
                            fw.op("gpsimd", "memset", dict(ap=xt[0:64, :], constant=0.0), writes=[Bxt])
                            if not left_pad:
                                fw.dma("sync", xt[0:HALO, :], src[s0 - HALO:s0, :], writes=[Bxt])
                            if not right_pad:
                                fw.dma("sync", xt[32:32 + HALO, :], src[s0 + NT:s0 + NT + HALO, :], writes=[Bxt])
                    S["xts"] = xts

                def front_pre():
                    ss, Bss = ssq.get()
                    S["ss"], S["Bss"] = ss, Bss
                    fw.op("vector", "memset", dict(ap=ss[:, :], constant=0.0), writes=[Bss])
                    xts = S["xts"]
                    for j in range(nt1):
                        xt, Bxt = xts[j]
                        nr = 128 if j < ntile else 64
                        jt, Bjt = tmp_p.get()
                        fw.op("scalar", "activation", dict(out=jt[0:nr, :].bitcast(BF16), in_=xt[0:nr, :], func=AF.Square, accum_out=ss[0:nr, j:j + 1]),
                              reads=[Bxt], writes=[Bjt, Bss])
                    fw.op("vector", "tensor_scalar", dict(out=ss[:, 0:nt1], in0=ss[:, 0:nt1], scalar1=1.0 / D, scalar2=RMS_EPS, op0=ALU.mult, op1=ALU.add),
                          reads=[Bss], writes=[Bss])
                    fw.op("gpsimd", "tensor_tensor", dict(out=ss[:, 0:nt1], in0=ss[:, 0:nt1], in1=mhalf[:, 0:nt1], op=ALU.pow),
                          reads=[Bss, Bc], writes=[Bss])
                    for j in range(nt1):
                        xt, Bxt = xts[j]
                        nr = 128 if j < ntile else 64
                        fw.op("vector", "tensor_scalar", dict(out=xt[0:nr, :], in0=xt[0:nr, :], scalar1=ss[0:nr, j:j + 1], scalar2=None, op0=ALU.mult),
                              reads=[Bxt, Bss], writes=[Bxt])
                    S["xts"] = xts

                def front_tr():
                    xts = S["xts"]
                    for j in range(nt1):
                        xt, Bxt = xts[j]
                        nr = 128 if j < ntile else 64
                        for half in range(2):
                            pb = (2 * j + half) % 4
                            for kk in range(4):
                                k = half * 4 + kk
                                fw.op("tensor", "transpose", dict(out=PS[pb][:, kk * 128:kk * 128 + nr], in_=xt[0:nr, k * 128:(k + 1) * 128], identity=ident[0:nr, 0:nr]),
                                      reads=[Bxt, Bc], writes=[PB[pb]], signal=(kk == 3))
                            for kk in range(4):
                                k = half * 4 + kk
                                eng = "scalar" if kk % 2 == 0 else "vector"
                                pieces = ([(hT[:, k, HALO + j * 128:HALO + (j + 1) * 128], PS[pb][:, kk * 128:(kk + 1) * 128])] if j < ntile else
                                          [(hT[:, k, 0:HALO], PS[pb][:, kk * 128:kk * 128 + HALO]),
                                           (hT[:, k, HALO + NT:HALO + NT + HALO], PS[pb][:, kk * 128 + 32:kk * 128 + 32 + HALO])])
                                for (o_, i_) in pieces:
                                    if eng == "scalar":
                                        fw.op("scalar", "activation", dict(out=o_, in_=i_, func=AF.Identity, scale=gmul[0][:, k, ci:ci + 1], bias=mod[0][:, k, ci:ci + 1]),
                                              reads=[PB[pb], Bc], writes=[BhT])
                                    else:
                                        fw.op("vector", "tensor_scalar", dict(out=o_, in0=i_, scalar1=gmul[0][:, k, ci:ci + 1], scalar2=mod[0][:, k, ci:ci + 1],
                                                                              op0=ALU.mult, op1=ALU.add),
                                              reads=[PB[pb], Bc], writes=[BhT])

                def p_stage(c):
                    wch, Bwch = wch_p.get()
                    fw.dma("sync", wch[:].rearrange("p k t j -> p (k t j)"), w0in_bf[c].rearrange("p k t j -> p (k t j)"), writes=[Bwch])
                    dg, Bdg = diag_p.get()
                    fw.dma("sync", dg[:].rearrange("p k j -> p (k j)"), diag_bf[c], writes=[Bdg])
                    sg, Bsg = sg_p.get(); v, Bv = v_p.get()
                    w1 = min(WT, 512)
                    rem = WT - w1
                    for k in range(NK):
                        fw.op("tensor", "matmul", dict(out=PS[2][:, 0:w1], lhsT=wch[:, k, 1, :], rhs=hT[:, k, 0:w1], start=(k == 0), stop=(k == NK - 1)),
                              reads=[Bwch, BhT], writes=[PB[2]], signal=(k == NK - 1))
                    if rem > 0:
                        for k in range(NK):
                            fw.op("tensor", "matmul", dict(out=PS[4][:, 0:rem], lhsT=wch[:, k, 1, :], rhs=hT[:, k, 512:WT], start=(k == 0), stop=(k == NK - 1)),
                                  reads=[Bwch, BhT], writes=[PB[4]], signal=(k == NK - 1))
                    fw.op("scalar", "activation", dict(out=sg[:, 0:w1], in_=PS[2][:, 0:w1], func=AF.Tanh, scale=0.5, bias=binh[:, c:c + 1]),
                          reads=[PB[2], Bc], writes=[Bsg])
                    if rem > 0:
                        fw.op("scalar", "activation", dict(out=sg[:, 512:WT], in_=PS[4][:, 0:rem], func=AF.Tanh, scale=0.5, bias=binh[:, c:c + 1]),
                              reads=[PB[4], Bc], writes=[Bsg])
                    fw.op("vector", "tensor_scalar", dict(out=sg[:, 0:WT], in0=sg[:, 0:WT], scalar1=0.5, scalar2=0.5, op0=ALU.mult, op1=ALU.add),
                          reads=[Bsg], writes=[Bsg])
                    for k in range(NK):
                        fw.op("tensor", "matmul", dict(out=PS[3][:, 0:w1], lhsT=wch[:, k, 0, :], rhs=hT[:, k, 0:w1], start=(k == 0), stop=(k == NK - 1)),
                              reads=[Bwch, BhT], writes=[PB[3]], signal=(k == NK - 1))
                    if rem > 0:
                        for k in range(NK):
                            fw.op("tensor", "matmul", dict(out=PS[4][:, 64:64 + rem], lhsT=wch[:, k, 0, :], rhs=hT[:, k, 512:WT], start=(k == 0), stop=(k == NK - 1)),
                                  reads=[Bwch, BhT], writes=[PB[4]], signal=(k == NK - 1))
                    fw.op("vector", "scalar_tensor_tensor", dict(out=v[:, 0:w1], in0=PS[3][:, 0:w1], scalar=binc[:, c:c + 1], in1=sg[:, 0:w1], op0=ALU.add, op1=ALU.mult),
                          reads=[PB[3], Bsg, Bc], writes=[Bv])
                    if rem > 0:
                        fw.op("vector", "scalar_tensor_tensor", dict(out=v[:, 512:WT], in0=PS[4][:, 64:64 + rem], scalar=binc[:, c:c + 1], in1=sg[:, 512:WT], op0=ALU.add, op1=ALU.mult),
                              reads=[PB[4], Bsg, Bc], writes=[Bv])
                    if left_pad:
                        fw.op("vector", "memset", dict(ap=v[:, 0:HALO], constant=0.0), writes=[Bv])
                    if right_pad:
                        fw.op("vector", "memset", dict(ap=v[:, HALO + NT:WT], constant=0.0), writes=[Bv])
                    for k in range(NK):
                        fw.op("tensor", "matmul", dict(out=PS[c % 2][:, 0:NT], lhsT=wch[:, k, 2, :], rhs=hT[:, k, HALO:HALO + NT], start=(k == 0), stop=(k == NK - 1)),
                              reads=[Bwch, BhT], writes=[PB[c % 2]], signal=(k == NK - 1))
                    fw.op("scalar", "activation", dict(out=sz[:, c, 0:NT], in_=PS[c % 2][:, 0:NT], func=AF.Silu, bias=binc[:, 2 * NCH + c:2 * NCH + c + 1]),
                          reads=[PB[c % 2], Bc], writes=[Bsz[c]])
                    return (v, Bv, dg, Bdg)

                def c_stage(c, st):
                    v, Bv, dg, Bdg = st
                    for k in range(CW):
                        fw.op("tensor", "matmul", dict(out=PS[5][:, 0:NT], lhsT=dg[:, k, :], rhs=v[:, k:k + NT], start=(k == 0), stop=(k == CW - 1)),
                              reads=[Bdg, Bv], writes=[PB[5]], signal=(k == CW - 1))
                    fw.op("scalar", "activation", dict(out=co[:, c, 0:NT], in_=PS[5][:, 0:NT], func=AF.Identity, bias=dwbc[:, c:c + 1]),
                          reads=[PB[5], Bc], writes=[Bco[c]])
                    cob, Bcob = cob_p.get(); sqb, Bsqb = sqb_p.get()
                    fw.op("vector", "tensor_copy", dict(out=cob[:, 0:NT], in_=co[:, c, 0:NT]), reads=[Bco[c]], writes=[Bcob])
                    fw.op("gpsimd", "tensor_tensor", dict(out=sqb[:, 0:NT], in0=co[:, c, 0:NT], in1=co[:, c, 0:NT], op=ALU.mult),
                          reads=[Bco[c]], writes=[Bsqb])
                    S["pend_stats"] = (c, cob, Bcob, sqb, Bsqb)

                def stats_mm():
                    if S.get("pend_stats") is None:
                        return
                    c, cob, Bcob, sqb, Bsqb = S["pend_stats"]
                    S["pend_stats"] = None
                    fw.op("tensor", "matmul", dict(out=PS[6][:, 0:NT], lhsT=onesb[:], rhs=cob[:, 0:NT], start=(c == 0), stop=(c == NCH - 1)),
                          reads=[Bcob, Bc], writes=[PB[6]])
                    fw.op("tensor", "matmul", dict(out=PS[7][:, 0:NT], lhsT=onesb[:], rhs=sqb[:, 0:NT], start=(c == 0), stop=(c == NCH - 1)),
                          reads=[Bsqb, Bc], writes=[PB[7]])

                S["prev"] = None

                def mid(c0, c1):
                    for c in range(c0, c1):
                        cur = p_stage(c)
                        stats_mm()
                        if S["prev"] is not None:
                            c_stage(c - 1, S["prev"])
                        S["prev"] = cur
                    if c1 == NCH:
                        stats_mm()
                        c_stage(NCH - 1, S["prev"])
                        stats_mm()

                def stats_n():
                    fw.op("vector", "tensor_scalar", dict(out=mean[:, 0:NT], in0=PS[6][:, 0:NT], scalar1=1.0 / E, scalar2=None, op0=ALU.mult),
                          reads=[PB[6]], writes=[Bst])
                    fw.op("vector", "tensor_tensor", dict(out=rstd[:, 0:NT], in0=mean[:, 0:NT], in1=mean[:, 0:NT], op=ALU.mult),
                          reads=[Bst], writes=[Bst])
                    fw.op("vector", "scalar_tensor_tensor", dict(out=rstd[:, 0:NT], in0=PS[7][:, 0:NT], scalar=1.0 / E, in1=rstd[:, 0:NT], op0=ALU.mult, op1=ALU.subtract),
                          reads=[PB[7], Bst], writes=[Bst])
                    fw.op("scalar", "activation", dict(out=rstd[:, 0:NT], in_=rstd[:, 0:NT], func=AF.Sqrt, bias=epsc[:, 1:2]),
                          reads=[Bst, Bc], writes=[Bst])
                    fw.op("vector", "reciprocal", dict(out=rstd[:, 0:NT], in_=rstd[:, 0:NT]), reads=[Bst], writes=[Bst])

                def n_pre(c0, c1):
                    for c in range(c0, c1):
                        fw.op("vector", "tensor_tensor", dict(out=co[:, c, 0:NT], in0=co[:, c, 0:NT], in1=mean[:, 0:NT], op=ALU.subtract),
                              reads=[Bst], writes=[Bco[c]])
                        fw.op("vector", "tensor_tensor", dict(out=co[:, c, 0:NT], in0=co[:, c, 0:NT], in1=rstd[:, 0:NT], op=ALU.mult),
                              reads=[Bst], writes=[Bco[c]])

                def n_post(c0, c1):
                    for c in range(c0, c1):
                        sl, Bsl = sil_p.get()
                        fw.op("scalar", "activation", dict(out=sl[:, 0:NT], in_=co[:, c, 0:NT], func=AF.Silu, scale=lngc[:, c:c + 1], bias=lnbc[:, c:c + 1]),
                              reads=[Bco[c], Bc], writes=[Bsl])
                        fw.op("gpsimd", "tensor_tensor", dict(out=wg[:, c, 0:NT], in0=sl[:, 0:NT], in1=sz[:, c, 0:NT], op=ALU.mult),
                              reads=[Bsl, Bsz[c]], writes=[Bwg[c]])

                def o_stage():
                    if cur_gate[0] != ci:
                        set_gate(ci)
                        cur_gate[0] = ci
                    for j in range(ntile):
                        xr, Bxr = xr_p.get(); xo, Bxo = xo_p.get()
                        fw.dma("gpsimd", xr[:, :], src[s0 + j * 128:s0 + (j + 1) * 128, :], writes=[Bxr])
                        fw.op("gpsimd", "tensor_tensor", dict(out=xr[:, :], in0=xr[:, :], in1=gbo_b[ci][:], op=ALU.add),
                              reads=[Bxr, Bg], writes=[Bxr])
                        for half in range(2):
                            pb = half
                            for c in range(NCH):
                                fw.op("tensor", "matmul", dict(out=PS[pb][:, :], lhsT=wg[:, c, j * 128:(j + 1) * 128], rhs=w0out[:, c, half * 512:(half + 1) * 512], start=(c == 0), stop=(c == NCH - 1)),
                                    reads=[Bwg[c], Bw0out], writes=[PB[pb]], signal=(c == NCH - 1))
                            fw.op("vector", "tensor_tensor", dict(out=xo[:, half * 512:(half + 1) * 512], in0=PS[pb][:, :], in1=gate_b[ci][:, half * 512:(half + 1) * 512], op=ALU.mult),
                                reads=[PB[pb], Bg], writes=[Bxo])
                        fw.op("vector", "tensor_tensor", dict(out=xo[:, :], in0=xo[:, :], in1=xr[:, :], op=ALU.add),
                              reads=[Bxo, Bxr], writes=[Bxo])
                        fw.dma("gpsimd", dst[s0 + j * 128:s0 + (j + 1) * 128, :], xo[:, :], reads=[Bxo])

                return dict(front_load=front_load, front_pre=front_pre, front_tr=front_tr, mid=mid, stats_n=stats_n, n_pre=n_pre, n_post=n_post, o_stage=o_stage)

            specs = [(ctx_d, ctx1_d, 0, CTX, 1, CTX)] + [(x_d, x1_d, blk * 512, 512, 0, L) for blk in range(L // 512)]
            blk_objs = {}

            def getb(bi):
                if bi not in blk_objs:
                    blk_objs[bi] = l0_block(*specs[bi])
                return blk_objs[bi]
            nb = len(specs)
            A = getb(0)
            A["front_load"](); A["front_pre"](); A["front_tr"](); A["mid"](0, 4)
            if nb > 1:
                getb(1)["front_load"]()
            A["mid"](4, 12)
            for bi in range(1, nb):
                A = getb(bi - 1); Bk = getb(bi)
                Bk["front_pre"]()
                A["mid"](12, NCH)
                Bk["front_tr"]()
                if bi + 1 < nb:
                    getb(bi + 1)["front_load"]()
                A["stats_n"]()
                A["n_pre"](0, 3)
                A["n_post"](0, 1)
                npost, npre = 1, 3
                for i in range(8):
                    Bk["mid"](i, i + 1)
                    e = min(npost + 2, NCH)
                    A["n_post"](npost, e)
                    npost = e
                    e2 = min(npre + 2, NCH)
                    if e2 > npre:
                        A["n_pre"](npre, e2)
                        npre = e2
                assert npost == NCH and npre == NCH
                A["o_stage"]()
                Bk["mid"](8, 12)
            A = getb(nb - 1)
            A["mid"](12, NCH); A["stats_n"](); A["n_pre"](0, NCH); A["n_post"](0, NCH); A["o_stage"]()
            fw.barrier()
            fw.flush()

        if debug == "l0":
            with contextlib.ExitStack() as ph:
                t = T(ph, "dbg", [128, D], F32); Bt = Buf()
                for j in range(L // 128):
                    fw.dma("sync", t[:], x1_d[j * 128:(j + 1) * 128, :], reads=[], writes=[Bt])
                    fw.dma("sync", out_d[j * 128:(j + 1) * 128, :], t[:], reads=[Bt])
                fw.barrier()
                fw.flush()
            return nc

        with contextlib.ExitStack() as ph:
            w1in = T(ph, "w1in", [128, NK, 3072], BF16); Bw1 = Buf()
            fw.dma("sync", w1in[:, 0:4, :], w1in_bf[:, 0:4, 0:3072], writes=[Bw1])
            fw.dma("sync", w1in[:, 4:8, :], w1in_bf[:, 4:8, 0:3072], writes=[Bw1])
            wz_p = Pool(nc, ph, "wz", [128, NK, 128], BF16, 3)
            gqk = T(ph, "gqk", [128, 20, 128], F32); Bgqk = Buf()
            fw.dma("sync", gqk[:, 0, :], l1_q_norm_g.partition_broadcast(128), writes=[Bgqk])
            fw.dma("sync", gqk[:, 16, :], l1_k_norm_g.partition_broadcast(128), writes=[Bgqk])
            fw.op("vector", "tensor_copy", dict(out=gqk[:, 1:16, :], in_=gqk[:, 0:1, :].to_broadcast([128, 15, 128])),
                  reads=[Bgqk], writes=[Bgqk])
            fw.op("vector", "tensor_copy", dict(out=gqk[:, 17:20, :], in_=gqk[:, 16:17, :].to_broadcast([128, 3, 128])),
                  reads=[Bgqk], writes=[Bgqk])
            xt_p = Pool(nc, ph, "xt2", [128, D], F32, 8)
            junk = T(ph, "junk2", [128, D], BF16); Bjunk = Buf()
            ssq = Pool(nc, ph, "ssq2", [128, 4], F32, 3)
            hT_p = Pool(nc, ph, "hT2", [128, NK, 512], BF16, 2)
            qk_p = Pool(nc, ph, "qk", [128, 20, 128], F32, 2)
            scr_p = Pool(nc, ph, "scr", [128, 20, 128], F32, 2)
            hs_p = Pool(nc, ph, "hs", [128, 20], F32, 3)
            cs_p = Pool(nc, ph, "cs", [128, 2, 64], F32, 3)
            qr_p = Pool(nc, ph, "qr", [128, 20, 128], BF16, 2)
            qst_p = Pool(nc, ph, "qst", [128, 20, 128], BF16, 3)
            vst_p = Pool(nc, ph, "vst", [128, KVD], BF16, 3)
            szst_p = Pool(nc, ph, "szst", [128, 4, 512], BF16, 3)

            def x_pre(src, s0, NT):
                ntile = NT // 128
                ss, Bss = ssq.get()
                fw.op("vector", "memset", dict(ap=ss[:, :], constant=0.0), writes=[Bss])
                xts = []
                for j in range(ntile):
                    xt, Bxt = xt_p.get(); xts.append((xt, Bxt))
                    fw.dma("sync", xt[:, :], src[s0 + j * 128:s0 + (j + 1) * 128, :], writes=[Bxt])
                    fw.op("scalar", "activation", dict(out=junk[:, :], in_=xt[:, :], func=AF.Square, accum_out=ss[:, j:j + 1]),
                          reads=[Bxt], writes=[Bjunk, Bss])
                fw.op("vector", "tensor_scalar", dict(out=ss[:, 0:ntile], in0=ss[:, 0:ntile], scalar1=1.0 / D, scalar2=RMS_EPS, op0=ALU.mult, op1=ALU.add),
                      reads=[Bss], writes=[Bss])
                fw.op("gpsimd", "tensor_tensor", dict(out=ss[:, 0:ntile], in0=ss[:, 0:ntile], in1=mhalf[:, 0:ntile], op=ALU.pow),
                      reads=[Bss, Bc], writes=[Bss])
                for j in range(ntile):
                    xt, Bxt = xts[j]
                    fw.op("vector", "tensor_scalar", dict(out=xt[:, :], in0=xt[:, :], scalar1=ss[:, j:j + 1], scalar2=None, op0=ALU.mult),
                          reads=[Bxt, Bss], writes=[Bxt])
                return xts

            def x_tr(xtb, j, hT, BhT, ci):
                xt, Bxt = xtb
                for half in range(2):
                    pb = half
                    for kk in range(4):
                        k = half * 4 + kk
                        fw.op("tensor", "transpose", dict(out=PS[pb][:, kk * 128:(kk + 1) * 128], in_=xt[:, k * 128:(k + 1) * 128], identity=ident[:, :]),
                              reads=[Bxt, Bc], writes=[PB[pb]], signal=(kk == 3))
                    for kk in range(4):
                        k = half * 4 + kk
                        if True:
                            fw.op("scalar", "activation", dict(out=hT[:, k, j * 128:(j + 1) * 128], in_=PS[pb][:, kk * 128:(kk + 1) * 128],
                                                               func=AF.Identity, scale=gmul[1][:, k, ci:ci + 1], bias=mod[1][:, k, ci:ci + 1]),
                                  reads=[PB[pb], Bc], writes=[BhT])
                        else:
                            fw.op("vector", "tensor_scalar", dict(out=hT[:, k, j * 128:(j + 1) * 128], in0=PS[pb][:, kk * 128:(kk + 1) * 128],
                                                                  scalar1=gmul[1][:, k, ci:ci + 1], scalar2=mod[1][:, k, ci:ci + 1],
                                                                  op0=ALU.mult, op1=ALU.add),
                                  reads=[PB[pb], Bc], writes=[BhT])

            def s1_stage(tl):
                hT, BhT, j, is_ctx = tl["hT"], tl["BhT"], tl["j"], tl["is_ctx"]
                nh0 = 16 if is_ctx else 0
                qk, Bqk = qk_p.get(); scr, Bscr = scr_p.get(); vst, Bvst = vst_p.get()
                tl.update(qk=qk, Bqk=Bqk, scr=scr, Bscr=Bscr, vst=vst, Bvst=Bvst, nh0=nh0)
                groups = ([] if is_ctx else [0, 1, 2, 3]) + [4, 5]
                for gi, g in enumerate(groups):
                    pb = 2 + (gi % 2)
                    for k in range(NK):
                        fw.op("tensor", "matmul", dict(out=PS[pb][:, :], lhsT=hT[:, k, j * 128:(j + 1) * 128], rhs=w1in[:, k, g * 512:(g + 1) * 512],
                                                       start=(k == 0), stop=(k == NK - 1)),
                              reads=[BhT, Bw1], writes=[PB[pb]], signal=(k == NK - 1))
                    if g < 5:
                        fw.op("scalar", "copy", dict(out=qk[:, g * 4:(g + 1) * 4, :].rearrange("p h d -> p (h d)"), in_=PS[pb][:, :]),
                              reads=[PB[pb]], writes=[Bqk])
                        fw.op("scalar", "activation", dict(out=scr[:, g * 4:(g + 1) * 4, :].rearrange("p h d -> p (h d)"), in_=PS[pb][:, :], func=AF.Square),
                              reads=[PB[pb]], writes=[Bscr])
                    else:
                        fw.op("vector", "tensor_copy", dict(out=vst[:, :], in_=PS[pb][:, :]), reads=[PB[pb]], writes=[Bvst])

            def s2_stage(tl):
                qk, Bqk, scr, Bscr, nh0, is_ctx = tl["qk"], tl["Bqk"], tl["scr"], tl["Bscr"], tl["nh0"], tl["is_ctx"]
                nh = 20 - nh0
                hs, Bhs = hs_p.get()
                fw.op("vector", "tensor_reduce", dict(out=hs[:, nh0:20], in_=scr[:, nh0:20, :], axis=AX.X, op=ALU.add),
                      reads=[Bscr], writes=[Bhs])
                fw.op("vector", "tensor_scalar", dict(out=hs[:, nh0:20], in0=hs[:, nh0:20], scalar1=1.0 / 128, scalar2=RMS_EPS, op0=ALU.mult, op1=ALU.add),
                      reads=[Bhs], writes=[Bhs])
                fw.op("scalar", "activation", dict(out=hs[:, nh0:20], in_=hs[:, nh0:20], func=AF.Sqrt), reads=[Bhs], writes=[Bhs])
                if not is_ctx:
                    fw.op("vector", "tensor_tensor", dict(out=qk[:, :, :], in0=qk[:, :, :], in1=gqk[:, :, :], op=ALU.mult),
                          reads=[Bqk, Bgqk], writes=[Bqk])
                fw.op("vector", "reciprocal", dict(out=hs[:, nh0:20], in_=hs[:, nh0:20]), reads=[Bhs], writes=[Bhs])
                fw.op("vector", "tensor_tensor", dict(out=qk[:, nh0:20, :], in0=qk[:, nh0:20, :],
                                                      in1=hs[:, nh0:20].unsqueeze(2).to_broadcast([128, nh, 128]), op=ALU.mult),
                      reads=[Bqk, Bhs], writes=[Bqk])
                qr, Bqr = qr_p.get()
                tl.update(qr=qr, Bqr=Bqr)
                if is_ctx:
                    fw.op("vector", "tensor_tensor", dict(out=qr[:, nh0:20, :], in0=qk[:, nh0:20, :], in1=gqk[:, nh0:20, :], op=ALU.mult),
                          reads=[Bqk, Bgqk], writes=[Bqr])
                    return
                cs, Bcs = cs_p.get()
                t0 = tl["t0"]
                fw.dma("sync", cs[:, 0, :], cos_d[t0:t0 + 128, :], writes=[Bcs])
                fw.dma("sync", cs[:, 1, :], sin_d[t0:t0 + 128, :], writes=[Bcs])
                qv = qk[:, :, :].rearrange("p h (a b i) -> p h a b i", a=2, b=2)
                qo = qr[:, :, :].rearrange("p h (a b i) -> p h a b i", a=2, b=2)
                x1v = qv[:, :, :, 0, :]; x2v = qv[:, :, :, 1, :]
                Cb = cs[:, 0, :].rearrange("p (a i) -> p a i", a=2).unsqueeze(1).to_broadcast([128, 20, 2, 32])
                Sb = cs[:, 1, :].rearrange("p (a i) -> p a i", a=2).unsqueeze(1).to_broadcast([128, 20, 2, 32])
                t1 = scr[:, 0:10, :].rearrange("p h (a i) -> p (h a) i", a=4).rearrange("p (h a) i -> p h a i", a=2)
                t2 = scr[:, 10:20, :].rearrange("p h (a i) -> p (h a) i", a=4).rearrange("p (h a) i -> p h a i", a=2)
                fw.op("vector", "tensor_tensor", dict(out=t1, in0=x1v, in1=Cb, op=ALU.mult), reads=[Bqk, Bcs], writes=[Bscr])
                fw.op("vector", "tensor_tensor", dict(out=t2, in0=x2v, in1=Sb, op=ALU.mult), reads=[Bqk, Bcs], writes=[Bscr])
                fw.op("vector", "tensor_tensor", dict(out=qo[:, :, :, 0, :], in0=t1, in1=t2, op=ALU.subtract), reads=[Bscr], writes=[Bqr])
                fw.op("vector", "tensor_tensor", dict(out=t1, in0=x1v, in1=Sb, op=ALU.mult), reads=[Bqk, Bcs], writes=[Bscr])
                fw.op("vector", "tensor_tensor", dict(out=t2, in0=x2v, in1=Cb, op=ALU.mult), reads=[Bqk, Bcs], writes=[Bscr])
                fw.op("vector", "tensor_tensor", dict(out=qo[:, :, :, 1, :], in0=t1, in1=t2, op=ALU.add), reads=[Bscr], writes=[Bqr])

            def s3_stage(tl):
                qr, Bqr, nh0, is_ctx = tl["qr"], tl["Bqr"], tl["nh0"], tl["is_ctx"]
                qst, Bqst = qst_p.get()
                hgroups = [(16, 20, 4)] if is_ctx else [(0, 8, 4), (8, 16, 5), (16, 20, 4)]
                for gi, (h0, h1, pb) in enumerate(hgroups):
                    ptv = PS[pb].bitcast(BF16)
                    for h in range(h0, h1):
                        fw.op("tensor", "transpose", dict(out=ptv[:, (h - h0) * 128:(h - h0 + 1) * 128], in_=qr[:, h, :], identity=identb[:, :]),
                              reads=[Bqr, Bc], writes=[PB[pb]], signal=(h == h1 - 1))
                    nw = (h1 - h0) * 128
                    fw.op("scalar", "copy", dict(out=qst[:, h0:h1, :].rearrange("p h t -> p (h t)"), in_=ptv[:, 0:nw]),
                          reads=[PB[pb]], writes=[Bqst])
                t0, key0 = tl["t0"], tl["key0"]
                if not is_ctx:
                    fw.dma("gpsimd", qT_d[:, :, t0:t0 + 128].rearrange("h p t -> p h t"), qst[:, 0:16, :], reads=[Bqst])
                fw.dma("gpsimd", kT_d[:, :, key0:key0 + 128].rearrange("h p t -> p h t"), qst[:, 16:20, :], reads=[Bqst])
                fw.dma("gpsimd", V_d[key0:key0 + 128, :], tl["vst"][:, :], reads=[tl["Bvst"]])

            def z_stage(hT, BhT, s0):
                for c4 in range(4):
                    szst, Bszst = szst_p.get()
                    for cc in range(4):
                        c = c4 * 4 + cc
                        pb = 6 + (c % 2)
                        wz, Bwz = wz_p.get()
                        fw.dma("sync", wz[:, :, :], w1in_bf[:, :, 3072 + c * 128:3072 + (c + 1) * 128], writes=[Bwz])
                        for k in range(NK):
                            fw.op("tensor", "matmul", dict(out=PS[pb][:, :], lhsT=wz[:, k, :], rhs=hT[:, k, :], start=(k == 0), stop=(k == NK - 1)),
                                  reads=[BhT, Bwz], writes=[PB[pb]], signal=(k == NK - 1))
                        fw.op("scalar", "activation", dict(out=szst[:, cc, :], in_=PS[pb][:, :], func=AF.Silu),
                              reads=[PB[pb]], writes=[Bszst])
                    fw.dma("gpsimd", szT_d[c4 * 4:(c4 + 1) * 4, :, s0:s0 + 512].rearrange("c p t -> p c t"), szst[:, :, :], reads=[Bszst])

            blocks = [(ctx1_d, 0, CTX, 1, True, 0)] + [(x1_d, blk * 512, 512, 0, False, CTX + blk * 512) for blk in range(L // 512)]
            tiles = []
            for bi, (src, s0, NT, ci, is_ctx, key0) in enumerate(blocks):
                for j in range(NT // 128):
                    tiles.append(dict(bi=bi, j=j, first=(j == 0), last=(j == NT // 128 - 1), is_ctx=is_ctx,
                                      t0=s0 + j * 128, key0=key0 + j * 128))
            hTs = {}
            for bi in (0, 1):
                src, s0, NT, ci, is_ctx, key0 = blocks[bi]
                xts = x_pre(src, s0, NT)
                hTs[bi] = hT_p.get()
                for j in range(NT // 128):
                    x_tr(xts[j], j, hTs[bi][0], hTs[bi][1], ci)
            nxt = None
            for gidx in range(len(tiles) + 2):
                if gidx < len(tiles):
                    tl = tiles[gidx]
                    bi = tl["bi"]
                    src, s0, NT, ci, is_ctx, key0 = blocks[bi]
                    if tl["first"] and bi >= 1 and bi + 1 < len(blocks):
                        nsrc, ns0, nNT, nci, _, _ = blocks[bi + 1]
                        nxt = (x_pre(nsrc, ns0, nNT), nci)
                        hTs[bi + 1] = hT_p.get()
                    tl["hT"], tl["BhT"] = hTs[bi]
                    if gidx >= 1:
                        s2_stage(tiles[gidx - 1])
                    s1_stage(tl)
                    if gidx >= 2:
                        s3_stage(tiles[gidx - 2])
                    if bi >= 1 and bi + 1 < len(blocks):
                        x_tr(nxt[0][tl["j"]], tl["j"], hTs[bi + 1][0], hTs[bi + 1][1], nxt[1])
                else:
                    if gidx == len(tiles):
                        s2_stage(tiles[gidx - 1])
                    s3_stage(tiles[gidx - 2])
                if gidx < len(tiles) and tiles[gidx]["last"] and not tiles[gidx]["is_ctx"]:
                    bi = tiles[gidx]["bi"]
                    z_stage(hTs[bi][0], hTs[bi][1], blocks[bi][1])
            fw.barrier()
            fw.flush()

        with contextlib.ExitStack() as ph:
            kT_p = Pool(nc, ph, "kTh", [128, NKEY], BF16, 2)
            V_p = Pool(nc, ph, "Vh", [128, NKC, 128], BF16, 2)
            q_p = Pool(nc, ph, "qblk", [128, 512], BF16, 3)
            szb_p = Pool(nc, ph, "szblk", [128, 512], BF16, 3)
            p_p = Pool(nc, ph, "pT", [128, 1024], BF16, 8)
            accA_p = Pool(nc, ph, "accA", [128, 1024], BF16, 2)
            accB_p = Pool(nc, ph, "accB", [128, 1024], BF16, 2)
            rden_p = Pool(nc, ph, "rden", [128, 512], F32, 2)
            o_p = Pool(nc, ph, "osb", [128, 512], F32, 2)
            w_p = Pool(nc, ph, "wsb", [128, 512], BF16, 3)
            scale = 128.0 ** -0.5
            NQB = L // 512
            NT2 = NKC // 2
            assert NKC % 2 == 0 and NT2 >= 3
            SW = [0, 1, 2]
            BSW = [Buf(), Buf(), Buf()]
            srr = [0]

            def s_get():
                i = srr[0] % 3
                srr[0] += 1
                return PSW[SW[i]], BSW[i]
            LAG = 2

            def make_unit(kTh, BkT, Vh, BV, h, qb, po):
                qblk, Bq = q_p.get(); szb, Bszb = szb_p.get()
                fw.dma("sync", qblk[:, :], qT_d[h, :, qb * 512:(qb + 1) * 512], writes=[Bq])
                fw.dma("sync", szb[:, :], szT_d[h, :, qb * 512:(qb + 1) * 512], writes=[Bszb])
                accA, BaA = accA_p.get()
                pts = {}

                def s_stage(t):
                    st_, Bst_ = s_get()
                    for i in range(2):
                        kc = 2 * t + i
                        fw.op("tensor", "matmul", dict(out=st_[:, i * 512:(i + 1) * 512], lhsT=kTh[:, kc * 128:(kc + 1) * 128], rhs=qblk[:, :],
                                                       start=True, stop=True),
                              reads=[BkT, Bq], writes=[Bst_], signal=(i == 1))
                    pT, BpT = p_p.get()
                    fw.op("scalar", "activation", dict(out=pT[:, :], in_=st_[:, :], func=AF.Exp, scale=scale),
                          reads=[Bst_], writes=[BpT])
                    pts[t] = (pT, BpT)
                    if t == 1:
                        p0, Bp0 = pts[0]
                        fw.op("vector", "tensor_tensor", dict(out=accA[:, :], in0=p0[:, :], in1=pT[:, :], op=ALU.add),
                              reads=[Bp0, BpT], writes=[BaA])
                    elif t > 1:
                        fw.op("vector", "tensor_tensor", dict(out=accA[:, :], in0=accA[:, :], in1=pT[:, :], op=ALU.add),
                              reads=[BpT, BaA], writes=[BaA])

                def pv_stage(t):
                    pT, BpT = pts.pop(t)
                    for i in range(2):
                        kc = 2 * t + i
                        fw.op("tensor", "matmul", dict(out=PS[po][:, :], lhsT=Vh[:, kc, :], rhs=pT[:, i * 512:(i + 1) * 512],
                                                       start=(kc == 0), stop=(kc == NKC - 1)),
                              reads=[BV, BpT], writes=[PB[po]], signal=(i == 1))

                def finalize():
                    pdt, Bpd = s_get()
                    for i in range(2):
                        fw.op("tensor", "matmul", dict(out=pdt[:, 0:512], lhsT=onesb[:, :], rhs=accA[:, i * 512:(i + 1) * 512],
                                                       start=(i == 0), stop=(i == 1)),
                              reads=[Bc, BaA], writes=[Bpd], signal=(i == 1))
                    rden, Brd = rden_p.get(); osb, Bo = o_p.get(); wsb, Bw = w_p.get()
                    fw.op("vector", "reciprocal", dict(out=rden[:, :], in_=pdt[:, 0:512]), reads=[Bpd], writes=[Brd])
                    fw.op("vector", "tensor_tensor", dict(out=osb[:, :], in0=PS[po][:, :], in1=rden[:, :], op=ALU.mult),
                          reads=[PB[po], Brd], writes=[Bo])
                    fw.op("gpsimd", "tensor_tensor", dict(out=wsb[:, :], in0=osb[:, :], in1=szb[:, :], op=ALU.mult),
                          reads=[Bo, Bszb], writes=[Bw])
                    fw.dma("gpsimd", wT_d[h, :, qb * 512:(qb + 1) * 512], wsb[:, :], reads=[Bw])
                return s_stage, pv_stage, finalize

            unit = 0
            prev = None
            for hk in range(NKV):
                kTh, BkT = kT_p.get(); Vh, BV = V_p.get()
                fw.dma("sync", kTh[:, :], kT_d[hk], writes=[BkT])
                fw.dma("sync", Vh[:, :, :], V_d[:, hk * 128:(hk + 1) * 128].rearrange("(c p) d -> p c d", p=128), writes=[BV])
                for g in range(4):
                    h = hk * 4 + g
                    for qb in range(NQB):
                        cur = make_unit(kTh, BkT, Vh, BV, h, qb, 6 + (unit % 2))
                        unit += 1
                        for t in range(NT2):
                            cur[0](t)
                            if t >= LAG:
                                cur[1](t - LAG)
                            elif prev is not None:
                                prev[1](NT2 - LAG + t)
                                if t == LAG - 1:
                                    prev[2]()
                        prev = cur
            for t in range(LAG):
                prev[1](NT2 - LAG + t)
            prev[2]()
            fw.barrier()
            fw.flush()

        with contextlib.ExitStack() as ph:
            w1out = T(ph, "w1out", [128, NCH, D], BF16); Bw = Buf()
            fw.dma("sync", w1out[:], w1out_bf, writes=[Bw])
            gate1 = T(ph, "gate1", [128, D], F32); fng = T(ph, "fng", [128, D], F32); Bg = Buf()
            fw.dma("sync", gate1[:], modrow[1, 0, 2 * D:3 * D].partition_broadcast(128), writes=[Bg])
            fw.dma("sync", fng[:], final_norm_g.partition_broadcast(128), writes=[Bg])
            wt_p = Pool(nc, ph, "wTt", [128, NCH, 512], BF16, 2)
            xr_p = Pool(nc, ph, "xr3", [128, D], F32, 3)
            xo_p = Pool(nc, ph, "xo3", [128, D], F32, 3)
            junk = T(ph, "junk3", [128, D], BF16); Bjunk = Buf()
            ss_p = Pool(nc, ph, "ss3", [128, 1], F32, 3)
            for blk in range(L // 512):
                wt, Bwt = wt_p.get()
                fw.dma("sync", wt[:, :, :], wT_d[:, :, blk * 512:(blk + 1) * 512].rearrange("h p t -> p h t"), writes=[Bwt])
                for j in range(4):
                    t0 = blk * 512 + j * 128
                    xr, Bxr = xr_p.get(); xo, Bxo = xo_p.get(); ss, Bss = ss_p.get()
                    fw.dma("sync", xr[:, :], x1_d[t0:t0 + 128, :], writes=[Bxr])
                    for half in range(2):
                        pb = (j % 2) * 2 + half
                        for c in range(NCH):
                            fw.op("tensor", "matmul", dict(out=PS[pb][:, :], lhsT=wt[:, c, j * 128:(j + 1) * 128], rhs=w1out[:, c, half * 512:(half + 1) * 512], start=(c == 0), stop=(c == NCH - 1)),
                                reads=[Bwt, Bw], writes=[PB[pb]], signal=(c == NCH - 1))
                        fw.op("vector", "tensor_tensor", dict(out=xo[:, half * 512:(half + 1) * 512], in0=PS[pb][:, :], in1=gate1[:, half * 512:(half + 1) * 512], op=ALU.mult),
                            reads=[PB[pb], Bg], writes=[Bxo])
                    fw.op("gpsimd", "tensor_tensor", dict(out=xo[:, :], in0=xo[:, :], in1=xr[:, :], op=ALU.add),
                          reads=[Bxo, Bxr], writes=[Bxo])
                    fw.op("scalar", "activation", dict(out=junk[:, :], in_=xo[:, :], func=AF.Square, accum_out=ss[:, 0:1]),
                          reads=[Bxo], writes=[Bjunk, Bss])
                    fw.op("vector", "tensor_scalar", dict(out=ss[:, :], in0=ss[:, :], scalar1=1.0 / D, scalar2=RMS_EPS, op0=ALU.mult, op1=ALU.add),
                          reads=[Bss], writes=[Bss])
                    fw.op("gpsimd", "tensor_tensor", dict(out=ss[:, :], in0=ss[:, :], in1=mhalf[:, 0:1], op=ALU.pow),
                          reads=[Bss, Bc], writes=[Bss])
                    fw.op("vector", "scalar_tensor_tensor", dict(out=xo[:, :], in0=xo[:, :], scalar=ss[:, 0:1], in1=fng[:, :], op0=ALU.mult, op1=ALU.mult),
                          reads=[Bxo, Bss, Bg], writes=[Bxo])
                    fw.dma("gpsimd", out_d[t0:t0 + 128, :], xo[:, :], reads=[Bxo])
            fw.barrier()
            fw.flush()
    return nc


def _rope_tables(L):
    rows = L // 64
    row = np.broadcast_to(np.arange(rows)[:, None], (rows, 64)).reshape(-1).astype(np.float32)
    col = np.broadcast_to(np.arange(64)[None, :], (rows, 64)).reshape(-1).astype(np.float32)
    inv_freq = (np.float32(10000.0) ** (-np.arange(0, 64, 2, dtype=np.float32) / np.float32(64))).astype(np.float32)
    ang = np.concatenate([row[:, None] * inv_freq[None, :], col[:, None] * inv_freq[None, :]], axis=1).astype(np.float32)
    return np.cos(ang).astype(np.float32), np.sin(ang).astype(np.float32)


_CACHE = {}


def make_in_maps(inputs, L, ncores):
    cos, sin = _rope_tables(L)
    ident = np.eye(128, dtype=np.float32)
    maps = []
    f = lambda a: np.ascontiguousarray(np.asarray(a, dtype=np.float32))
    shared = {k: f(v) for k, v in inputs.items() if k not in ("x", "c", "ctx")}
    for b in range(ncores):
        m = dict(shared)
        m["x"] = f(inputs["x"][b]); m["c"] = f(inputs["c"][b]); m["ctx"] = f(inputs["ctx"][b])
        m["ident"] = ident; m["rope_cos"] = cos; m["rope_sin"] = sin
        maps.append(m)
    return maps


def kernel(**inputs):
    x = np.asarray(inputs["x"])
    B, L, _ = x.shape
    key = (L,)
    if key not in _CACHE:
        _CACHE[key] = build_program(L)
    nc = _CACHE[key]
    maps = make_in_maps(inputs, L, B)
    res = run_bass_kernel_spmd(nc, maps, core_ids=list(range(B)))
    return np.stack([np.asarray(r["out"], dtype=np.float32) for r in res.results], axis=0)
```
